# Optimizing a Trainium2 kernel written in Bass

```python
import jax
import jax.numpy as jnp
from jax import lax
import numpy as np

D_MODEL = 1024
BATCH = 2
SEQ = 8192
DEPTH = 2

GRID_W = 64
CTX_LEN = 256
HEAD_DIM = 64
A_HEADS = 8
A_KV_HEADS = 2
A_WINDOW = 128
A_BLOCK = 128
B_HEADS = 8
NA_ROWS = 8
NA_COLS = 16
QA_W = A_HEADS * HEAD_DIM
KVA_W = A_KV_HEADS * HEAD_DIM
QKVB_W = B_HEADS * HEAD_DIM
EVEN_SPLITS = (QA_W, QA_W + KVA_W, QA_W + 2 * KVA_W, QA_W + 2 * KVA_W + QKVB_W, QA_W + 2 * KVA_W + 2 * QKVB_W)
EVEN_IN = QA_W + 2 * KVA_W + 3 * QKVB_W
EVEN_OUT = (A_HEADS + B_HEADS) * HEAD_DIM
C_HEADS = 16
C_Q_RANK = 384
C_KV_RANK = 256
C_NOPE = 64
C_ROPE = 32
C_V = 64
C_IN = C_Q_RANK + C_KV_RANK + C_ROPE
C_OUT = C_HEADS * C_V
Q_BLOCK = 128
D_FF = 4 * D_MODEL
N_EVEN = (DEPTH + 1) // 2
N_ODD = DEPTH // 2
ROPE_THETA = 10000.0
NORM_EPS = 1e-6
NEG_INF = -1e30

kernel_name = 'hybrid_prefix_dit_block'


def rms_norm(x, g):
    xf = x.astype(jnp.float32)
    y = xf * lax.rsqrt(jnp.mean(xf * xf, axis=-1, keepdims=True) + NORM_EPS)
    return (y * g.astype(jnp.float32)).astype(x.dtype)


def modulate(h, shift, scale):
    return h * (1 + scale) + shift


def adaln(cond, w, b):
    m = jax.nn.silu(cond) @ w + b
    return jnp.split(m[:, None, :], 6, axis=-1)


def rope_1d(x, pos):
    half = x.shape[-1] // 2
    inv = ROPE_THETA ** (-jnp.arange(half, dtype=jnp.float32) / half)
    ang = pos.astype(jnp.float32)[:, None] * inv[None, :]
    cos, sin = jnp.cos(ang)[:, None, :], jnp.sin(ang)[:, None, :]
    x1, x2 = x[..., :half], x[..., half:]
    return jnp.concatenate([x1 * cos - x2 * sin, x1 * sin + x2 * cos], axis=-1).astype(x.dtype)


def rope_2d(x, row, col):
    half = x.shape[-1] // 2
    return jnp.concatenate([rope_1d(x[..., :half], row), rope_1d(x[..., half:], col)], axis=-1)


def squared_relu_mlp(h, w1, w2):
    return jnp.square(jax.nn.relu(h @ w1)) @ w2


def ctx_self_attention(q, k, v, scale, sink=None):
    bsz, n, hq, dq = q.shape
    g = k.shape[2]
    r = hq // g
    qg = q.reshape(bsz, n, g, r, dq)
    s = jnp.einsum('bqgrd,bkgd->bgrqk', qg, k, preferred_element_type=jnp.float32) * scale
    if sink is not None:
        sk = jnp.broadcast_to(sink.astype(jnp.float32).reshape(g, r)[None, :, :, None, None], (bsz, g, r, n, 1))
        s = jnp.concatenate([s, sk], axis=-1)
    p = jax.nn.softmax(s, axis=-1)[..., :n].astype(v.dtype)
    o = jnp.einsum('bgrqk,bkgd->bqgrd', p, v)
    return o.reshape(bsz, n, hq * v.shape[-1])


def window_attention(q, k, v, kc, vc, sink):
    bsz, seq, hq, d = q.shape
    g = k.shape[2]
    r = hq // g
    nb = seq // A_BLOCK
    qb = q.reshape(bsz, nb, A_BLOCK, g, r, d)

    def band(t):
        tb = jnp.pad(t.reshape(bsz, nb, A_BLOCK, g, d), ((0, 0), (1, 1), (0, 0), (0, 0), (0, 0)))
        return jnp.concatenate([tb[:, :-2], tb[:, 1:-1], tb[:, 2:]], axis=2)

    kb, vb = band(k), band(v)
    qpos = jnp.arange(seq).reshape(nb, A_BLOCK)
    kpos = (jnp.arange(nb)[:, None] - 1) * A_BLOCK + jnp.arange(3 * A_BLOCK)[None, :]
    valid = ((kpos[:, None, :] >= 0) & (kpos[:, None, :] < seq)
             & (jnp.abs(qpos[:, :, None] - kpos[:, None, :]) <= A_WINDOW))
    scale = d ** -0.5
    s_loc = jnp.einsum('bnqgrd,bnkgd->bngrqk', qb, kb, preferred_element_type=jnp.float32) * scale
    s_loc = jnp.where(valid[None, :, None, None], s_loc, NEG_INF)
    s_ctx = jnp.einsum('bnqgrd,bcgd->bngrqc', qb, kc, preferred_element_type=jnp.float32) * scale
    s_sink = jnp.broadcast_to(sink.astype(jnp.float32).reshape(g, r)[None, None, :, :, None, None],
                              (bsz, nb, g, r, A_BLOCK, 1))
    p = jax.nn.softmax(jnp.concatenate([s_loc, s_ctx, s_sink], axis=-1), axis=-1).astype(v.dtype)
    nloc = 3 * A_BLOCK
    nctx = kc.shape[1]
    o = (jnp.einsum('bngrqk,bnkgd->bnqgrd', p[..., :nloc], vb)
         + jnp.einsum('bngrqc,bcgd->bnqgrd', p[..., nloc:nloc + nctx], vc))
    return o.reshape(bsz, seq, hq * d)


def neighbourhood_attention(q, k, v, kc, vc, rpb, rows_n):
    bsz, _, h, d = q.shape
    kh = min(NA_ROWS, rows_n)
    r = jnp.arange(rows_n)
    row_idx = jnp.clip(r - kh // 2, 0, rows_n - kh)[:, None] + jnp.arange(kh)[None, :]
    cq = jnp.arange(GRID_W)
    c0 = jnp.clip(cq - NA_COLS // 2, 0, GRID_W - NA_COLS)
    col_ok = (cq[None, :] >= c0[:, None]) & (cq[None, :] < c0[:, None] + NA_COLS)
    dri = row_idx - r[:, None] + NA_ROWS - 1
    dci = jnp.clip(cq[None, :] - cq[:, None], 1 - NA_COLS, NA_COLS - 1) + NA_COLS - 1
    bias = rpb.astype(jnp.float32)[:, dri[:, None, :, None], dci[None, :, None, :]]
    bias = jnp.moveaxis(bias, 0, 1)
    qg = q.reshape(bsz, rows_n, GRID_W, h, d)
    kg = k.reshape(bsz, rows_n, GRID_W, h, d)[:, row_idx]
    vg = v.reshape(bsz, rows_n, GRID_W, h, d)[:, row_idx]
    scale = d ** -0.5
    s_loc = jnp.einsum('brqhd,brikhd->brhqik', qg, kg, preferred_element_type=jnp.float32) * scale + bias[None]
    s_loc = jnp.where(col_ok[:, None, :], s_loc, NEG_INF).reshape(bsz, rows_n, h, GRID_W, kh * GRID_W)
    s_ctx = jnp.einsum('brqhd,bchd->brhqc', qg, kc, preferred_element_type=jnp.float32) * scale
    p = jax.nn.softmax(jnp.concatenate([s_loc, s_ctx], axis=-1), axis=-1).astype(v.dtype)
    nloc = kh * GRID_W
    p_loc = p[..., :nloc].reshape(bsz, rows_n, h, GRID_W, kh, GRID_W)
    o = jnp.einsum('brhqik,brikhd->brqhd', p_loc, vg) + jnp.einsum('brhqc,bchd->brqhd', p[..., nloc:], vc)
    return o.reshape(bsz, rows_n * GRID_W, h * d)


def blockwise_dense_attention(q, k, v, kc, vc, scale):
    bsz, seq, h, dq = q.shape
    nb = seq // Q_BLOCK
    kall = jnp.concatenate([kc, k], axis=1)
    vall = jnp.concatenate([vc, v], axis=1)
    qb = jnp.moveaxis(q.reshape(bsz, nb, Q_BLOCK, h, dq), 1, 0)

    def one_block(qblk):
        s = jnp.einsum('bqhd,bkhd->bhqk', qblk, kall, preferred_element_type=jnp.float32) * scale
        p = jax.nn.softmax(s, axis=-1).astype(vall.dtype)
        return jnp.einsum('bhqk,bkhd->bqhd', p, vall)

    o = lax.map(one_block, qb)
    return jnp.moveaxis(o, 0, 1).reshape(bsz, seq, h * v.shape[-1])


def even_mixer(h_lat, h_ctx, row, col, rows_n, w_in, w_out, a_qn, a_kn, a_sink, b_qn, b_kn, b_rpb, need_ctx_out):
    def heads(hh):
        bsz, n, _ = hh.shape
        qa, ka, va, qb, kb, vb = jnp.split(hh @ w_in, EVEN_SPLITS, axis=-1)
        qa = rms_norm(qa.reshape(bsz, n, A_HEADS, HEAD_DIM), a_qn)
        ka = rms_norm(ka.reshape(bsz, n, A_KV_HEADS, HEAD_DIM), a_kn)
        va = va.reshape(bsz, n, A_KV_HEADS, HEAD_DIM)
        qb = rms_norm(qb.reshape(bsz, n, B_HEADS, HEAD_DIM), b_qn)
        kb = rms_norm(kb.reshape(bsz, n, B_HEADS, HEAD_DIM), b_kn)
        vb = vb.reshape(bsz, n, B_HEADS, HEAD_DIM)
        return qa, ka, va, qb, kb, vb

    qa, ka, va, qb, kb, vb = heads(h_lat)
    qa_c, ka_c, va_c, qb_c, kb_c, vb_c = heads(h_ctx)
    qa = rope_2d(qa, row, col)
    ka = rope_2d(ka, row, col)
    o_a = window_attention(qa, ka, va, ka_c, va_c, a_sink)
    o_b = neighbourhood_attention(qb, kb, vb, kb_c, vb_c, b_rpb, rows_n)
    y_lat = jnp.concatenate([o_a, o_b], axis=-1) @ w_out
    y_ctx = None
    if need_ctx_out:
        scale = HEAD_DIM ** -0.5
        oa_c = ctx_self_attention(qa_c, ka_c, va_c, scale, a_sink)
        ob_c = ctx_self_attention(qb_c, kb_c, vb_c, scale)
        y_ctx = jnp.concatenate([oa_c, ob_c], axis=-1) @ w_out
    return y_lat, y_ctx


def mla_project(hh, w_in, qa_norm, kva_norm, w_uq, w_ukv, qn_nope, qn_rope, kn_nope, kn_rope):
    bsz, n, _ = hh.shape
    cq, ckv, kr = jnp.split(hh @ w_in, (C_Q_RANK, C_Q_RANK + C_KV_RANK), axis=-1)
    q = (rms_norm(cq, qa_norm) @ w_uq).reshape(bsz, n, C_HEADS, C_NOPE + C_ROPE)
    kv = (rms_norm(ckv, kva_norm) @ w_ukv).reshape(bsz, n, C_HEADS, C_NOPE + C_V)
    q_nope = rms_norm(q[..., :C_NOPE], qn_nope)
    q_rope = rms_norm(q[..., C_NOPE:], qn_rope)
    k_nope = rms_norm(kv[..., :C_NOPE], kn_nope)
    v = kv[..., C_NOPE:]
    k_rope = rms_norm(kr[:, :, None, :], kn_rope)
    return q_nope, q_rope, k_nope, k_rope, v


def odd_mixer(h_lat, h_ctx, row, col, w_in, qa_norm, kva_norm, w_uq, w_ukv,
              qn_nope, qn_rope, kn_nope, kn_rope, w_out, need_ctx_out):
    q_nope, q_rope, k_nope, k_rope, v = mla_project(h_lat, w_in, qa_norm, kva_norm, w_uq, w_ukv,
                                                    qn_nope, qn_rope, kn_nope, kn_rope)
    q = jnp.concatenate([q_nope, rope_2d(q_rope, row, col)], axis=-1)
    k_rope = rope_2d(k_rope, row, col)
    k = jnp.concatenate([k_nope, jnp.broadcast_to(k_rope, k_nope.shape[:-1] + (C_ROPE,))], axis=-1)
    qc_nope, qc_rope, kc_nope, kc_rope, vc = mla_project(h_ctx, w_in, qa_norm, kva_norm, w_uq, w_ukv,
                                                         qn_nope, qn_rope, kn_nope, kn_rope)
    kc = jnp.concatenate([kc_nope, jnp.broadcast_to(kc_rope, kc_nope.shape[:-1] + (C_ROPE,))], axis=-1)
    scale = (C_NOPE + C_ROPE) ** -0.5
    y_lat = blockwise_dense_attention(q, k, v, kc, vc, scale) @ w_out
    y_ctx = None
    if need_ctx_out:
        qc = jnp.concatenate([qc_nope, qc_rope], axis=-1)
        y_ctx = ctx_self_attention(qc, kc, vc, scale) @ w_out
    return y_lat, y_ctx


def setup_inputs(seed: int = 0) -> dict:
    key = jax.random.key(seed)
    ks = jax.random.split(key, 32)
    f32 = jnp.float32

    def nrm(k, shape, fan_in):
        return jax.random.normal(k, shape, f32) * fan_in ** -0.5

    def gain(k, shape):
        return 1.0 + 0.05 * jax.random.normal(k, shape, f32)

    return {
        'x': jax.random.normal(ks[0], (BATCH, SEQ, D_MODEL), f32),
        'c': jax.random.normal(ks[1], (BATCH, D_MODEL), f32),
        'ctx': jax.random.normal(ks[2], (BATCH, CTX_LEN, D_MODEL), f32),
        'c_ctx': jax.random.normal(ks[3], (D_MODEL,), f32),
        'ada_w': nrm(ks[4], (DEPTH, D_MODEL, 6 * D_MODEL), D_MODEL),
        'ada_b': 0.02 * jax.random.normal(ks[5], (DEPTH, 6 * D_MODEL), f32),
        'norm_mix': gain(ks[6], (DEPTH, D_MODEL)),
        'norm_mlp': gain(ks[7], (DEPTH, D_MODEL)),
        'mlp_w1': nrm(ks[8], (DEPTH, D_MODEL, D_FF), D_MODEL),
        'mlp_w2': nrm(ks[9], (DEPTH, D_FF, D_MODEL), D_FF),
        'e_w_in': nrm(ks[10], (N_EVEN, D_MODEL, EVEN_IN), D_MODEL),
        'e_w_out': nrm(ks[11], (N_EVEN, EVEN_OUT, D_MODEL), EVEN_OUT),
        'a_q_norm': gain(ks[12], (N_EVEN, HEAD_DIM)),
        'a_k_norm': gain(ks[13], (N_EVEN, HEAD_DIM)),
        'a_sink': jax.random.normal(ks[14], (N_EVEN, A_HEADS), f32),
        'b_q_norm': gain(ks[15], (N_EVEN, HEAD_DIM)),
        'b_k_norm': gain(ks[16], (N_EVEN, HEAD_DIM)),
        'b_rpb': 0.1 * jax.random.normal(ks[17], (N_EVEN, B_HEADS, 2 * NA_ROWS - 1, 2 * NA_COLS - 1), f32),
        'o_w_in': nrm(ks[18], (N_ODD, D_MODEL, C_IN), D_MODEL),
        'o_qa_norm': gain(ks[19], (N_ODD, C_Q_RANK)),
        'o_kva_norm': gain(ks[20], (N_ODD, C_KV_RANK)),
        'o_w_uq': nrm(ks[21], (N_ODD, C_Q_RANK, C_HEADS * (C_NOPE + C_ROPE)), C_Q_RANK),
        'o_w_ukv': nrm(ks[22], (N_ODD, C_KV_RANK, C_HEADS * (C_NOPE + C_V)), C_KV_RANK),
        'o_qn_nope': gain(ks[23], (N_ODD, C_NOPE)),
        'o_qn_rope': gain(ks[24], (N_ODD, C_ROPE)),
        'o_kn_nope': gain(ks[25], (N_ODD, C_NOPE)),
        'o_kn_rope': gain(ks[26], (N_ODD, C_ROPE)),
        'o_w_out': nrm(ks[27], (N_ODD, C_OUT, D_MODEL), C_OUT),
    }


def reference(x, c, ctx, c_ctx, ada_w, ada_b, norm_mix, norm_mlp, mlp_w1, mlp_w2,
              e_w_in, e_w_out, a_q_norm, a_k_norm, a_sink, b_q_norm, b_k_norm, b_rpb,
              o_w_in, o_qa_norm, o_kva_norm, o_w_uq, o_w_ukv,
              o_qn_nope, o_qn_rope, o_kn_nope, o_kn_rope, o_w_out):
    seq = x.shape[1]
    rows_n = seq // GRID_W
    t = jnp.arange(seq, dtype=jnp.int32)
    row, col = t // GRID_W, t % GRID_W
    for i in range(DEPTH):
        last = i == DEPTH - 1
        j = i // 2
        sh1, sc1, g1, sh2, sc2, g2 = adaln(c, ada_w[i], ada_b[i])
        csh1, csc1, cg1, csh2, csc2, cg2 = adaln(c_ctx[None, :], ada_w[i], ada_b[i])
        h_lat = modulate(rms_norm(x, norm_mix[i]), sh1, sc1)
        h_ctx = modulate(rms_norm(ctx, norm_mix[i]), csh1, csc1)
        if i % 2 == 0:
            y_lat, y_ctx = even_mixer(h_lat, h_ctx, row, col, rows_n, e_w_in[j], e_w_out[j],
                                      a_q_norm[j], a_k_norm[j], a_sink[j],
                                      b_q_norm[j], b_k_norm[j], b_rpb[j], not last)
        else:
            y_lat, y_ctx = odd_mixer(h_lat, h_ctx, row, col, o_w_in[j], o_qa_norm[j], o_kva_norm[j],
                                     o_w_uq[j], o_w_ukv[j], o_qn_nope[j], o_qn_rope[j],
                                     o_kn_nope[j], o_kn_rope[j], o_w_out[j], not last)
        x = x + g1 * y_lat
        x = x + g2 * squared_relu_mlp(modulate(rms_norm(x, norm_mlp[i]), sh2, sc2), mlp_w1[i], mlp_w2[i])
        if not last:
            ctx = ctx + cg1 * y_ctx
            ctx = ctx + cg2 * squared_relu_mlp(modulate(rms_norm(ctx, norm_mlp[i]), csh2, csc2),
                                               mlp_w1[i], mlp_w2[i])
    return x
```

```python
import contextlib
import numpy as np
import ml_dtypes
import concourse.bass as bass
import concourse.mybir as mybir
from concourse.bass_utils import run_bass_kernel_spmd

F32 = mybir.dt.float32
BF16 = mybir.dt.bfloat16
AF = mybir.ActivationFunctionType
ALU = mybir.AluOpType

NCORES = 8
T = 2048
HAL = 256
CT = 256
E = HAL + T + HAL + CT
NQ = T + CT
NKEY = 8192 + CT
EPS = 1e-6
NEG = -30000.0


def _region(ap):
    name = ap.name
    space = str(ap.space)
    dims = ap.ap
    off = int(ap.offset)
    if space == "DRAM":
        lo = off
        hi = off + sum(int(s) * (int(c) - 1) for s, c in dims if int(s) > 0) + 1
        return (name, "DRAM", 0, 1, lo, hi)
    if space == "PSUM":
        return (name, "PSUM", 0, 128, 0, 1 << 30)
    pstep, pcnt = int(dims[0][0]), int(dims[0][1])
    fsz = 1
    for d in ap.tensor.shape[1:]:
        fsz *= int(d)
    p0 = off // fsz
    f0 = off % fsz
    p1 = p0 + 1 if pstep == 0 else p0 + (pstep // fsz) * (pcnt - 1) + 1
    f1 = f0 + sum(int(s) * (int(c) - 1) for s, c in dims[1:] if int(s) > 0) + 1
    return (name, "SB", p0, p1, f0, f1)


def _overlap(a, b):
    return a[2] < b[3] and b[2] < a[3] and a[4] < b[5] and b[4] < a[5]


def _covers(a, b):
    return a[2] <= b[2] and a[3] >= b[3] and a[4] <= b[4] and a[5] >= b[5]


class Sched:
    ENGS = ("pe", "act", "dve", "pool", "sp")

    def __init__(self, nc, n_dma_sems=12):
        self.nc = nc
        self.ops = []
        self.track = {}
        self.n_dma_sems = n_dma_sems

    def add(self, eng, fn, reads=(), writes=(), dma=False):
        idx = len(self.ops)
        rr = list(dict.fromkeys(_region(a) for a in reads))
        ww = list(dict.fromkeys(_region(a) for a in writes))
        deps = set()
        for r in rr:
            lst = self.track.setdefault(r[0], [])
            psum = r[1] == "PSUM"
            for (box, oi, kind) in lst:
                if (kind == "w" or psum) and _overlap(box, r):
                    deps.add(oi)
        for w in ww:
            lst = self.track.setdefault(w[0], [])
            for (box, oi, kind) in lst:
                if _overlap(box, w):
                    deps.add(oi)
        for r in rr:
            lst = self.track[r[0]]
            if r[1] == "PSUM":
                lst[:] = [t for t in lst if not _covers(r, t[0])]
                lst.append((r, idx, "w"))
            else:
                lst[:] = [t for t in lst if not (t[2] == "r" and t[1] < idx and self.ops[t[1]]["eng"] == eng
                                                 and not self.ops[t[1]]["dma"] and not dma and _covers(r, t[0]))]
                lst.append((r, idx, "r"))
        for w in ww:
            lst = self.track[w[0]]
            lst[:] = [t for t in lst if not _covers(w, t[0])]
            lst.append((w, idx, "w"))
        deps.discard(idx)
        self.ops.append(dict(eng=eng, fn=fn, deps=deps, dma=dma, sig=False, rr=rr, ww=ww))
        return idx

    def dma(self, out, in_, eng="sp"):
        return self.add(eng, lambda e: e.dma_start(out=out, in_=in_), [in_], [out], dma=True)

    def emit(self, final_wait_ops=()):
        nc = self.nc
        ops = self.ops

        def needs_wait(x, y):
            X, Y = ops[x], ops[y]
            if Y["dma"] or X["dma"]:
                return True
            if X["eng"] == Y["eng"]:
                if X["eng"] == "pe":
                    return False
                for w in Y["ww"]:
                    for r in X["rr"]:
                        if w[0] == r[0] and _overlap(w, r):
                            return True
                return False
            return True

        for i, X in enumerate(ops):
            X["wdeps"] = [y for y in X["deps"] if needs_wait(i, y)]
            for y in X["wdeps"]:
                ops[y]["sig"] = True
        for i in final_wait_ops:
            ops[i]["sig"] = True
        cnt = {e: 0 for e in self.ENGS}
        dma_k = {e: 0 for e in self.ENGS}
        dma_semcnt = {}
        for X in ops:
            if X["dma"]:
                q = X["eng"]
                k = dma_k[q]
                dma_k[q] += 1
                s = (q, k % self.n_dma_sems)
                dma_semcnt[s] = dma_semcnt.get(s, 0) + 1
                X["dsem"] = s
                X["dval"] = 16 * dma_semcnt[s]
            elif X["sig"]:
                cnt[X["eng"]] += 1
                X["cnt"] = cnt[X["eng"]]
        with contextlib.ExitStack() as st:
            sems = {e: st.enter_context(nc.semaphore("s_" + e)) for e in ("pe", "act", "dve", "pool")}
            dsems = {}
            for q in self.ENGS:
                for j in range(min(self.n_dma_sems, dma_k[q])):
                    dsems[(q, j)] = st.enter_context(nc.semaphore("d_%s_%d" % (q, j)))
            block = st.enter_context(nc.Block())
            per_eng = {e: [i for i, X in enumerate(ops) if X["eng"] == e] for e in self.ENGS}

            def run_stream(ename, e):
                known = {}

                def wait(key, semh, val):
                    if known.get(key, 0) >= val:
                        return
                    e.wait_ge(semh, val)
                    known[key] = val

                def wait_op(Y):
                    if Y["dma"]:
                        wait(Y["dsem"], dsems[Y["dsem"]], Y["dval"])
                    else:
                        wait(Y["eng"], sems[Y["eng"]], Y["cnt"])

                for i in per_eng[ename]:
                    X = ops[i]
                    for y in sorted(X["wdeps"]):
                        wait_op(ops[y])
                    if X["dma"]:
                        if X["dval"] > 16:
                            wait(X["dsem"], dsems[X["dsem"]], X["dval"] - 16)
                        X["fn"](e).then_inc(dsems[X["dsem"]], 16)
                    else:
                        ins = X["fn"](e)
                        if X["sig"]:
                            ins.then_inc(sems[ename], 1)
                if ename == "sp":
                    for i in final_wait_ops:
                        wait_op(ops[i])

            @block.tensor
            def _(e):
                run_stream("pe", e)

            @block.scalar
            def _(e):
                run_stream("act", e)

            @block.vector
            def _(e):
                run_stream("dve", e)

            @block.gpsimd
            def _(e):
                run_stream("pool", e)

            @block.sync
            def _(e):
                run_stream("sp", e)
        self.stats = dict(n_ops=len(ops), cnt=cnt, dma=dma_k)


class KB:
    def __init__(self, mode):
        self.mode = mode
        self.nc = bass.Bass("TRN2", target_bir_lowering=False)
        self.S = Sched(self.nc)
        self.st = contextlib.ExitStack()
        self.final = []
        self._rot = {}

    def din(self, name, shape, dt=F32):
        return self.nc.dram_tensor(name, list(shape), dt, kind="ExternalInput").ap()

    def dout(self, name, shape, dt=F32):
        return self.nc.dram_tensor(name, list(shape), dt, kind="ExternalOutput").ap()

    def dint(self, name, shape, dt=F32):
        return self.nc.dram_tensor(name, list(shape), dt, kind="Internal").ap()

    def sb(self, name, shape, dt):
        return self.st.enter_context(self.nc.sbuf_tensor(name, list(shape), dt))

    def rot(self, key, lst):
        i = self._rot.get(key, 0)
        self._rot[key] = i + 1
        return lst[i % len(lst)]

    def mm(self, out, lhsT, rhs, start=True, stop=True, skip=False):
        kw = dict(skip_group_check=True) if skip else {}
        return self.S.add("pe", lambda e: e.matmul(out, lhsT=lhsT, rhs=rhs, start=start, stop=stop, **kw),
                          [lhsT, rhs], [out])

    def actv(self, out, in_, func, scale=1.0, bias=None):
        reads = [in_]
        kw = {}
        if isinstance(scale, float) or isinstance(scale, int):
            kw["scale"] = float(scale)
        else:
            kw["scale"] = scale
            reads.append(scale)
        if bias is not None:
            kw["bias"] = bias
            if not isinstance(bias, float):
                reads.append(bias)
        return self.S.add("act", lambda e: e.activation(out=out, in_=in_, func=func, **kw), reads, [out])

    def tt(self, eng, out, in0, in1, op):
        return self.S.add(eng, lambda e: e.tensor_tensor(out=out, in0=in0, in1=in1, op=op), [in0, in1], [out])

    def stt(self, eng, out, in0, scalar, in1, op0, op1):
        reads = [in0, in1]
        if not isinstance(scalar, float):
            reads.append(scalar)
        return self.S.add(eng, lambda e: e.scalar_tensor_tensor(out=out, in0=in0, scalar=scalar, in1=in1, op0=op0, op1=op1),
                          reads, [out])

    def ts(self, eng, out, in0, s1, op0, s2=None, op1=None):
        reads = [in0]
        if not isinstance(s1, float):
            reads.append(s1)
        if s2 is not None and not isinstance(s2, float):
            reads.append(s2)
        if op1 is None:
            return self.S.add(eng, lambda e: e.tensor_scalar(out=out, in0=in0, scalar1=s1, scalar2=None, op0=op0), reads, [out])
        return self.S.add(eng, lambda e: e.tensor_scalar(out=out, in0=in0, scalar1=s1, scalar2=s2, op0=op0, op1=op1), reads, [out])

    def copy(self, eng, out, in_):
        if eng == "act":
            return self.S.add("act", lambda e: e.activation(out=out, in_=in_, func=AF.Copy), [in_], [out])
        return self.S.add(eng, lambda e: e.tensor_copy(out=out, in_=in_), [in_], [out])

    def recip(self, out, in_):
        return self.S.add("dve", lambda e: e.reciprocal(out=out, in_=in_), [in_], [out])

    def memset(self, eng, out, val):
        return self.S.add(eng, lambda e: e.memset(out, val), [], [out])

    def dma(self, out, in_, eng="sp"):
        return self.S.dma(out, in_, eng)


def v3(ap2, a):
    return ap2.rearrange("p (a b) -> p a b", a=a)


MULT, ADD = ALU.mult, ALU.add
NA = 36352


def build(mode, stop=0):
    K = KB(mode)
    nc, S = K.nc, K.S
    A_ = mode in ("A", "F")
    B_ = mode in ("B", "F")

    vecs_d = K.din("vecs", [128, 48])
    cmat_d = K.din("cmat", [4, 128, 128], BF16)
    w1_d = K.din("mlp_w1", [2, 1024, 4096])
    w2_d = K.din("mlp_w2", [2, 4096, 1024])
    if A_:
        xT_d = K.din("xT", [1024, T])
        xh_d = K.din("xhT", [1024, 2 * HAL])
        ctx_d = K.din("ctxT", [1024, CT])
        cond_d = K.din("condT", [128, 16])
        adaw_d = K.din("ada_w", [2, 1024, 6144])
        adab_d = K.din("adab", [128, 96])
        ewin_d = K.din("e_w_in", [1024, 2304])
        ewout_d = K.din("e_w_out", [1024, 1024])
        owin_d = K.din("o_w_in", [1024, 672])
        ropeA_d = K.din("ropeA", [2, 128, 2 * HAL + T])
        ropeK_d = K.din("ropeK", [2, 32, T])
        amask_d = K.din("amask", [128, 4, 512], BF16)
        bbias_d = K.din("bbias", [5, 128, 6, 1024], BF16)
        sink_d = K.din("sink", [128, 8])
    if B_:
        wuq_d = K.din("o_w_uq", [384, 1536])
        wukv_d = K.din("o_w_ukv", [256, 2048])
        owout_d = K.din("o_w_out", [1024, 1024])
        ropeQ_d = K.din("ropeQ", [2, 32, T], BF16)
        out_o = K.dout("outT", [1024, T])
    if mode == "A":
        x1_o = K.dout("x1T", [1024, T])
        cqn_o = K.dout("cqn", [384, T], BF16)
        xchg_o = K.dout("xchg", [288, T + CT], BF16)
        mod1_o = K.dout("mod1", [128, 96])
    if mode == "B":
        x1_d = K.din("x1T", [1024, T])
        cqn_d = K.din("cqn", [384, T], BF16)
        kvall_d = K.din("kvall", [288, NKEY], BF16)
        mod1_d = K.din("mod1", [128, 96])
    if mode == "F":
        xchg_o = K.dint("xchg_i", [288, T + CT], BF16)
        kvg_i = K.dint("kvg_i", [4 * 288, T + CT], BF16)

    if stop:
        dbg_o = K.dout("dbg", [128, NA], BF16)
        dbg36_o = K.dout("dbg36", [128, 8 * NQ], BF16)
        dbgx_o = K.dout("dbgx", [128, 8 * NQ])

    def finish():
        if stop:
            K.final.append(K.dma(dbg_o, AR[:, :]))
            K.final.append(K.dma(dbg36_o, A36[:, :]))
            K.final.append(K.dma(dbgx_o, xT[:, :, :].rearrange("p a b -> p (a b)")))
        S.emit(final_wait_ops=K.final)
        return K

    xT = K.sb("xTs", [128, 8, NQ], F32)
    A36 = K.sb("A36", [128, 8 * NQ], BF16)
    A36v = v3(A36[:, :], 8)
    mod = K.sb("mod", [128, 2, 48, 2], F32)
    Amat = K.sb("Amat", [128, 2, 2, 2, 8], F32)
    vecs = K.sb("vecs_s", [128, 48], F32)
    gsc = K.sb("gsc", [128, 4], F32)
    ones128 = K.sb("ones128", [128, 128], BF16)
    blk = K.sb("blk", [128, 128], BF16)
    blk96 = K.sb("blk96", [128, 128], BF16)
    cmat = K.sb("cmat_s", [128, 4, 128], BF16)
    ident = cmat[:, 0, :]
    sqb = [K.sb("sqb%d" % i, [128, 512], BF16) for i in range(2)]
    ftmp = [K.sb("ftmp%d" % i, [128, 512], F32) for i in range(3)]
    rsb = [K.sb("rsb%d" % i, [128, 512], F32) for i in range(2)]
    AR = K.sb("AR", [128, NA], BF16)
    ARF = K.sb("ARF", [128, 3072], F32)
    PS = [K.st.enter_context(nc.psum_tensor("PS%d" % i, [128, 512], F32)) for i in range(8)]

    def arv(off, dims, p0=0, p1=128, t=AR):
        n = 1
        for d in dims:
            n *= d
        ap = t[p0:p1, off:off + n]
        if len(dims) == 2:
            ap = ap.rearrange("p (a b) -> p a b", a=dims[0])
        elif len(dims) == 3:
            ap = ap.rearrange("p (a b c) -> p a b c", a=dims[0], b=dims[1])
        return ap

    K.dma(vecs[:], vecs_d)
    K.dma(cmat[:], cmat_d.rearrange("c p q -> p c q"))
    epsv = K.sb("epsv", [128, 1], F32)
    K.memset("dve", epsv[:], EPS)
    K.memset("dve", ones128[:], 1.0)
    K.memset("dve", blk[:], 0.0)
    K.memset("dve", blk[0:64, 0:64], 1.0)
    K.memset("dve", blk[64:128, 64:128], 1.0)
    K.memset("dve", blk96[:], 0.0)
    K.memset("dve", blk96[0:64, 0:64], 1.0)
    K.memset("dve", blk96[64:96, 64:96], 1.0)
    K.ts("dve", gsc[:, 0:1], vecs[:, 32:33], 0.125, MULT)
    K.ts("dve", gsc[:, 1:2], vecs[:, 34:35], 0.125, MULT)
    K.ts("dve", gsc[:, 2:3], vecs[:, 41:42], float(96 ** -0.5), MULT)

    def mod_ap(l, kind, m, j):
        return mod[:, l, kind * 8 + m, j:j + 1]

    def norm_block(src, nt, l, which, j, dst):
        ss = K.rot("ssb", [PS[6], PS[7]])
        for k in range(8):
            sq = K.rot("sqb", sqb)
            K.tt("pool", sq[:, :nt], src[:, k, :], src[:, k, :], MULT)
            K.mm(ss[:, :nt], ones128[:], sq[:, :nt], start=(k == 0), stop=(k == 7))
        rs = K.rot("rsb", rsb)
        K.actv(rs[:, :nt], ss[:, :nt], AF.Sqrt, scale=1.0 / 1024.0, bias=epsv[:, 0:1])
        K.recip(rs[:, :nt], rs[:, :nt])
        shift_kind = 0 if which == 0 else 3
        for k in range(8):
            t = K.rot("ftmp", ftmp)
            K.stt("dve", t[:, :nt], src[:, k, :], Amat[:, l, which, j, k:k + 1], rs[:, :nt], MULT, MULT)
            K.actv(dst[:, k, :], t[:, :nt], AF.Identity, scale=1.0, bias=mod_ap(l, shift_kind, k, j))

    def mlp(l, nblocks, h2T):
        W1v = w1_d[l].rearrange("(k p) f -> p k f", p=128)
        W2v = w2_d[l].rearrange("(k p) f -> p k f", p=128)
        w1b = [arv(0, [8, 512]), arv(4096, [8, 512])]
        w2b = [arv(8192, [4, 1024]), arv(12288, [4, 1024])]
        ub = [arv(16384, [4, 512]), arv(18432, [4, 512])]
        rb = [arv(20480 + i * 512, [512]) for i in range(2)]
        def load_e8(e8):
            K.dma(w1b[e8 % 2], W1v[:, :, e8 * 512:(e8 + 1) * 512], eng="pool")
            K.dma(w2b[e8 % 2], W2v[:, e8 * 4:(e8 + 1) * 4, :], eng="pool")
        load_e8(0)
        for e8 in range(8):
            w1 = w1b[e8 % 2]
            w2 = w2b[e8 % 2]
            if e8 + 1 < 8:
                load_e8(e8 + 1)
            for (t0, nt, j) in nblocks:
                u = K.rot("ub", ub)
                for fc in range(4):
                    acc = K.rot("mlpacc", [PS[0], PS[1], PS[2]])
                    for k in range(8):
                        K.mm(acc[:, :nt], w1[:, k, fc * 128:(fc + 1) * 128], h2T[:, k, t0:t0 + nt], start=(k == 0), stop=(k == 7))
                    r = K.rot("rb", rb)
                    K.actv(r[:, :nt], acc[:, :nt], AF.Relu)
                    K.tt("pool", u[:, fc, :nt], r[:, :nt], r[:, :nt], MULT)
                for m in range(8):
                    acc = K.rot("mlpacc2", [PS[3], PS[4], PS[5]])
                    for fc in range(4):
                        K.mm(acc[:, :nt], w2[:, fc, m * 128:(m + 1) * 128], u[:, fc, :nt], start=(fc == 0), stop=(fc == 3))
                    K.stt("dve", xT[:, m, t0:t0 + nt], acc[:, :nt], mod_ap(l, 5, m, j), xT[:, m, t0:t0 + nt], MULT, ADD)

    OWN_BLOCKS = [(b * 512, 512, 0) for b in range(4)]
    CTX_BLOCK = (T, CT, 1)

    if A_:
        xTv = xT_d.rearrange("(k p) t -> p k t", p=128)
        for k in range(8):
            K.dma(xT[:, k, 0:T], xTv[:, k, :])
        K.dma(xT[:, :, T:NQ], ctx_d.rearrange("(k p) t -> p k t", p=128))
        cond = K.sb("cond", [128, 16], F32)
        silu = K.sb("silu", [128, 16], BF16)
        adab = K.sb("adab_s", [128, 96], F32)
        sinkx = K.sb("sinkx", [128, 8], F32)
        K.dma(cond[:], cond_d)
        K.dma(adab[:], adab_d)
        K.dma(sinkx[:], sink_d)
        K.actv(silu[:], cond[:], AF.Silu)
        K.actv(sinkx[:], sinkx[:], AF.Exp)
        siluv = v3(silu[:, :], 8)
        for l in range(2):
            Wv = adaw_d[l].rearrange("(k p) f -> p k f", p=128)
            acc = PS[0] if l == 0 else PS[1]
            adawb = [arv(0, [8, 1024]), arv(8192, [8, 1024])]
            if l == 0:
                K.dma(adawb[0], Wv[:, :, 0:1024], eng="pool")
            for piece in range(6):
                wb = adawb[(l * 6 + piece) % 2]
                nxt = l * 6 + piece + 1
                if nxt < 12:
                    Wn = adaw_d[nxt // 6].rearrange("(k p) f -> p k f", p=128)
                    K.dma(adawb[nxt % 2], Wn[:, :, (nxt % 6) * 1024:(nxt % 6 + 1) * 1024], eng="pool")
                for m in range(8):
                    f = piece * 8 + m
                    for k in range(8):
                        K.mm(acc[:, 2 * f:2 * f + 2], wb[:, k, m * 128:(m + 1) * 128], siluv[:, k, :], start=(k == 0), stop=(k == 7), skip=True)
            K.tt("dve", mod[:, l, :, :], v3(acc[:, 0:96], 48), adab[:, l * 48:(l + 1) * 48].unsqueeze(2).broadcast_to([128, 48, 2]), ADD)
            for which in range(2):
                sck = 1 if which == 0 else 4
                nv = vecs[:, (l * 2 + which) * 8:(l * 2 + which) * 8 + 8]
                for j in range(2):
                    K.stt("dve", Amat[:, l, which, j, :], mod[:, l, sck * 8:sck * 8 + 8, j], 1.0, nv, ADD, MULT)
        if mode == "A":
            mo = K.sb("mo", [128, 96], F32)
            K.copy("dve", v3(mo[:, :], 48), mod[:, 1, :, :])
            K.final.append(K.dma(mod1_o, mo[:]))
        if stop == 1:
            return finish()

        QT = arv(0, [4, NQ])
        KT = arv(9216, [2, E])
        VV = arv(14848, [22, 4, 128])
        wp = arv(26112, [8, 768])
        hblk = arv(32256, [8, 512])
        PTb = [arv(26112 + i * 512, [512]) for i in range(3)]
        biasb = [arv(27648 + i * 3072, [6, 512]) for i in range(2)]
        amask = arv(33792, [4, 512])
        ropetab = arv(0, [2, 512], t=ARF)
        xhs = arv(1024, [8, 256], t=ARF)
        ewv = ewin_d.rearrange("(k p) c -> p k c", p=128)
        xhv = xh_d.rearrange("(k p) t -> p k t", p=128)
        ropeAv = ropeA_d.rearrange("c p t -> p c t")

        def post_chunk(acc, nt, dst, gain, rope, tabc0):
            sq = K.rot("sqb", sqb)
            raw = K.rot("ftmp", ftmp)
            K.actv(sq[:, :nt], acc[:, :nt], AF.Square)
            K.copy("dve", raw[:, :nt], acc[:, :nt])
            ss = PS[3]
            K.mm(ss[:, :nt], blk[:], sq[:, :nt])
            rs = K.rot("rsb", rsb)
            K.actv(rs[:, :nt], ss[:, :nt], AF.Sqrt, scale=1.0 / 64.0, bias=epsv[:, 0:1])
            K.recip(rs[:, :nt], rs[:, :nt])
            if not rope:
                K.stt("dve", dst, raw[:, :nt], gain, rs[:, :nt], MULT, MULT)
                return
            qn = K.rot("sqb", sqb)
            K.stt("dve", qn[:, :nt], raw[:, :nt], gain, rs[:, :nt], MULT, MULT)
            sw = PS[4]
            K.mm(sw[:, :nt], cmat[:, 1, :], qn[:, :nt])
            t1 = K.rot("ftmp", ftmp)
            t2 = K.rot("ftmp", ftmp)
            K.tt("pool", t1[:, :nt], qn[:, :nt], ropetab[:, 0, :nt], MULT)
            K.tt("dve", t2[:, :nt], sw[:, :nt], ropetab[:, 1, :nt], MULT)
            K.tt("pool", dst, t1[:, :nt], t2[:, :nt], ADD)

        def l0_pass(pid, do_attn=True):
            isA = pid == 0
            half = pid - 1
            nQc = 4 if isA else 2
            nKc = 1 if isA else 2
            nV = 2 if isA else 4
            if isA:
                for s in range(2):
                    for jj in range(4):
                        K.dma(wp[:, :, jj * 128 + s * 64: jj * 128 + s * 64 + 64], ewv[:, :, s * 256 + jj * 64: s * 256 + jj * 64 + 64], eng="pool")
                K.dma(wp[:, :, 512:640], ewv[:, :, 512:640], eng="pool")
                K.dma(wp[:, :, 640:768], ewv[:, :, 640:768], eng="pool")
                qcol0, kcol0, vcol0 = 0, 512, 640
                gq, gk = gsc[:, 0:1], vecs[:, 33:34]
            else:
                K.dma(wp[:, :, 0:256], ewv[:, :, 768 + 256 * half: 768 + 256 * half + 256], eng="pool")
                K.dma(wp[:, :, 256:512], ewv[:, :, 1280 + 256 * half: 1280 + 256 * half + 256], eng="pool")
                K.dma(wp[:, :, 512:768], ewv[:, :, 1792 + 256 * half: 1792 + 256 * half + 256], eng="pool")
                qcol0, kcol0, vcol0 = 0, 256, 512
                gq, gk = gsc[:, 1:2], vecs[:, 35:36]
            K.memset("pool", VV[:, :, 0:nV, 64:128], 1.0)
            blocks = []
            blocks.append(("hb", 256, 0, None, 0, 0))
            for b in range(4):
                blocks.append((b, 512, HAL + b * 512, b * 512, 0, HAL + b * 512))
            blocks.append(("ha", 256, HAL + T, None, 0, HAL + T))
            blocks.append(("ctx", 256, 2 * HAL + T, T, 1, None))
            for (bid, nt, e0, q0, j, rc0) in blocks:
                if bid == "hb":
                    K.dma(xhs, xhv[:, :, 0:256])
                    src = xhs
                elif bid == "ha":
                    K.dma(xhs, xhv[:, :, 256:512])
                    src = xhs
                elif bid == "ctx":
                    src = xT[:, :, T:NQ]
                else:
                    src = xT[:, :, bid * 512:(bid + 1) * 512]
                rope = isA and (rc0 is not None)
                if rope:
                    K.dma(ropetab[:, :, :nt], ropeAv[:, :, rc0:rc0 + nt])
                norm_block(src, nt, 0, 0, j, hblk[:, :, :nt])
                if q0 is not None:
                    for qc in range(nQc):
                        acc = K.rot("pacc", [PS[0], PS[1], PS[2]])
                        for k in range(8):
                            K.mm(acc[:, :nt], wp[:, k, qcol0 + qc * 128: qcol0 + (qc + 1) * 128], hblk[:, k, :nt], start=(k == 0), stop=(k == 7))
                        post_chunk(acc, nt, QT[:, qc, q0:q0 + nt], gq, rope, rc0)
                for kc in range(nKc):
                    acc = K.rot("pacc", [PS[0], PS[1], PS[2]])
                    for k in range(8):
                        K.mm(acc[:, :nt], wp[:, k, kcol0 + kc * 128: kcol0 + (kc + 1) * 128], hblk[:, k, :nt], start=(k == 0), stop=(k == 7))
                    post_chunk(acc, nt, KT[:, kc, e0:e0 + nt], gk, rope, rc0)
                for tt_ in range(nt // 128):
                    acc = PS[5]
                    for k in range(8):
                        K.mm(acc[:, 0:nV * 64], hblk[:, k, tt_ * 128:(tt_ + 1) * 128], wp[:, k, vcol0:vcol0 + nV * 64], start=(k == 0), stop=(k == 7))
                    ec = e0 // 128 + tt_
                    K.copy("act", VV[:, ec, 0:nV, 0:64], v3(acc[:, 0:nV * 64], nV))

            if not do_attn:
                return
            if isA:
                K.dma(amask, amask_d)

            def finalize(O, heads_hc, sink_cols):
                rec = K.rot("rsb", rsb)
                if sink_cols is not None:
                    for hh in range(4):
                        K.ts("dve", rec[64:128, hh * 128:(hh + 1) * 128], O[64:128, hh * 128:(hh + 1) * 128], sinkx[64:128, sink_cols[hh]:sink_cols[hh] + 1], ADD)
                    K.recip(rec[64:128, :], rec[64:128, :])
                else:
                    K.recip(rec[64:128, :], O[64:128, :])
                return rec

            def attn_tile_A(q0, chunks):
                for g in range(2):
                    O = K.rot("Ob", [PS[3], PS[4]])
                    for ci, (ec, mv) in enumerate(chunks):
                        Sb = K.rot("Sb", [PS[0], PS[1], PS[2]])
                        if mv is not None:
                            K.mm(Sb[:, :], ident, amask[:, mv, :], start=True, stop=False, skip=True)
                        K.mm(Sb[:, :], KT[64 * g:64 * g + 64, 0, ec * 128:(ec + 1) * 128], QT[64 * g:64 * g + 64, 0:4, q0:q0 + 128],
                             start=(mv is None), stop=True, skip=True)
                        pt = K.rot("PT", PTb)
                        K.actv(pt, Sb[:, :], AF.Exp)
                        K.mm(O[:, :], VV[:, ec, g, :], pt, start=(ci == 0), stop=(ci == len(chunks) - 1))
                    rec = finalize(O, None, [4 * g + hh for hh in range(4)])
                    for hh in range(4):
                        hc = 4 * g + hh
                        dst = A36v[64 * (hc % 2):64 * (hc % 2) + 64, hc // 2, q0:q0 + 128]
                        K.tt("dve", dst, O[0:64, hh * 128:(hh + 1) * 128], rec[64:128, hh * 128:(hh + 1) * 128], MULT)

            def attn_tile_B(q0, chunks, bias):
                O = K.rot("Ob", [PS[3], PS[4]])
                K.memset("dve", O[:, :], 0.0)
                SB6 = [PS[0], PS[1], PS[2], PS[5], PS[6], PS[7]]
                for ci, (ec, bi) in enumerate(chunks):
                    Sb2 = [K.rot("Sb6", SB6), K.rot("Sb6", SB6)]
                    pt = K.rot("PT", PTb)
                    for s in range(2):
                        Sb = Sb2[s]
                        if bi is not None:
                            K.mm(Sb[:, 0:256], ident, bias[:, bi, s * 256:(s + 1) * 256], start=True, stop=False, skip=True)
                        for cc in range(2):
                            K.mm(Sb[:, cc * 128:(cc + 1) * 128], KT[64 * s:64 * s + 64, cc, ec * 128:(ec + 1) * 128],
                                 QT[64 * s:64 * s + 64, cc, q0:q0 + 128], start=(bi is None), stop=True, skip=True)
                    for s in range(2):
                        K.actv(pt[:, s * 256:(s + 1) * 256], Sb2[s][:, 0:256], AF.Exp)
                    for hh in range(4):
                        pos = (hh % 2) * 2 + hh // 2
                        K.mm(O[:, hh * 128:(hh + 1) * 128], VV[:, ec, hh, :], pt[:, pos * 128:(pos + 1) * 128], start=False, stop=False, skip=True)
                rec = finalize(O, None, None)
                for hh in range(4):
                    hc = 8 + 4 * half + hh
                    dst = A36v[64 * (hc % 2):64 * (hc % 2) + 64, hc // 2, q0:q0 + 128]
                    K.tt("dve", dst, O[0:64, hh * 128:(hh + 1) * 128], rec[64:128, hh * 128:(hh + 1) * 128], MULT)

            CTXC = [(20, None), (21, None)]
            for jt in range(16):
                q0 = jt * 128
                if isA:
                    chunks = [(2 + jt - 1, 2 if jt == 0 else 0), (2 + jt, None), (2 + jt + 1, 3 if jt == 15 else 1)] + CTXC
                    attn_tile_A(q0, chunks)
                else:
                    if jt == 0:
                        ms, var = list(range(-2, 4)), 0
                    elif jt == 15:
                        ms, var = list(range(12, 18)), 4
                    else:
                        ms = list(range(jt - 2, jt + 3))
                        var = 1 if jt == 1 else (3 if jt == 14 else 2)
                    bias = K.rot("biasb", biasb)
                    K.dma(bias, bbias_d[var][:, :, 512 * half:512 * half + 512])
                    chunks = [(m + 2, ci) for ci, m in enumerate(ms)] + CTXC
                    attn_tile_B(q0, chunks, bias)
            for ct in range(2):
                q0 = T + ct * 128
                if isA:
                    attn_tile_A(q0, CTXC)
                else:
                    attn_tile_B(q0, CTXC, None)

        if stop == 2:
            l0_pass(0, False)
            return finish()
        if stop == 3:
            l0_pass(0)
            return finish()
        if stop == 4:
            l0_pass(1, False)
            return finish()
        if stop == 5:
            l0_pass(1)
            return finish()
        for pid in range(3):
            l0_pass(pid)
        if stop == 6:
            return finish()

        wout = arv(0, [8, 1024])
        K.dma(wout, ewout_d.rearrange("(k p) f -> p k f", p=128), eng="pool")
        for (t0, nt, j) in OWN_BLOCKS + [CTX_BLOCK]:
            for m in range(8):
                acc = K.rot("oacc", [PS[5], PS[6], PS[7]])
                for k in range(8):
                    K.mm(acc[:, :nt], wout[:, k, m * 128:(m + 1) * 128], A36v[:, k, t0:t0 + nt], start=(k == 0), stop=(k == 7))
                K.stt("dve", xT[:, m, t0:t0 + nt], acc[:, :nt], mod_ap(0, 2, m, j), xT[:, m, t0:t0 + nt], MULT, ADD)
        if stop == 7:
            return finish()
        for (t0, nt, j) in OWN_BLOCKS + [CTX_BLOCK]:
            norm_block(xT[:, :, t0:t0 + nt], nt, 0, 1, j, A36v[:, :, t0:t0 + nt])
        mlp(0, OWN_BLOCKS + [CTX_BLOCK], A36v)

        if stop == 8:
            return finish()
        win1 = arv(0, [8, 672])
        hb1 = arv(5376, [8, 512])
        CQN = arv(9472, [3, T])
        CKVN = arv(15616, [2, NQ])
        KR = arv(20224, [NQ], p0=0, p1=32)
        K.dma(win1, owin_d.rearrange("(k p) c -> p k c", p=128), eng="pool")
        rawv = arv(1024, [3, 512], t=ARF)
        rk = arv(0, [2, 512], p0=0, p1=32, t=ARF)
        ropeKv = ropeK_d.rearrange("c p t -> p c t")
        for (t0, nt, j) in OWN_BLOCKS + [CTX_BLOCK]:
            norm_block(xT[:, :, t0:t0 + nt], nt, 1, 0, j, hb1[:, :, :nt])
            groups = [(384, 2, 256.0, 39, CKVN)]
            if j == 0:
                groups = [(0, 3, 384.0, 36, CQN)] + groups
            for (c0, ncn, dn, gcol, dstT) in groups:
                ss = PS[3]
                for c in range(ncn):
                    acc = K.rot("pacc", [PS[0], PS[1], PS[2]])
                    for k in range(8):
                        K.mm(acc[:, :nt], win1[:, k, c0 + c * 128:c0 + (c + 1) * 128], hb1[:, k, :nt], start=(k == 0), stop=(k == 7))
                    sq = K.rot("sqb", sqb)
                    K.actv(sq[:, :nt], acc[:, :nt], AF.Square)
                    K.copy("dve", rawv[:, c, :nt], acc[:, :nt])
                    K.mm(ss[:, :nt], ones128[:], sq[:, :nt], start=(c == 0), stop=(c == ncn - 1))
                rs = K.rot("rsb", rsb)
                K.actv(rs[:, :nt], ss[:, :nt], AF.Sqrt, scale=1.0 / dn, bias=epsv[:, 0:1])
                K.recip(rs[:, :nt], rs[:, :nt])
                for c in range(ncn):
                    K.stt("dve", dstT[:, c, t0:t0 + nt], rawv[:, c, :nt], vecs[:, gcol + c:gcol + c + 1], rs[:, :nt], MULT, MULT)
            acc = K.rot("pacc", [PS[0], PS[1], PS[2]])
            for k in range(8):
                K.mm(acc[0:32, :nt], win1[:, k, 640:672], hb1[:, k, :nt], start=(k == 0), stop=(k == 7))
            sq = K.rot("sqb", sqb)
            raw = K.rot("ftmp", ftmp)
            K.actv(sq[0:32, :nt], acc[0:32, :nt], AF.Square)
            K.copy("dve", raw[0:32, :nt], acc[0:32, :nt])
            ss = PS[3]
            K.mm(ss[0:32, :nt], ones128[0:32, 0:32], sq[0:32, :nt])
            rs = K.rot("rsb", rsb)
            K.actv(rs[0:32, :nt], ss[0:32, :nt], AF.Sqrt, scale=1.0 / 32.0, bias=epsv[0:32, 0:1])
            K.recip(rs[0:32, :nt], rs[0:32, :nt])
            if j == 1:
                K.stt("dve", KR[:, t0:t0 + nt], raw[0:32, :nt], vecs[0:32, 43:44], rs[0:32, :nt], MULT, MULT)
            else:
                K.dma(rk[:, :, :nt], ropeKv[:, :, t0:t0 + nt])
                qn = K.rot("sqb", sqb)
                K.stt("dve", qn[0:32, :nt], raw[0:32, :nt], vecs[0:32, 43:44], rs[0:32, :nt], MULT, MULT)
                sw = PS[4]
                K.mm(sw[0:32, :nt], cmat[0:32, 2, 0:32], qn[0:32, :nt])
                t1 = K.rot("ftmp", ftmp)
                t2 = K.rot("ftmp", ftmp)
                K.tt("pool", t1[0:32, :nt], qn[0:32, :nt], rk[:, 0, :nt], MULT)
                K.tt("dve", t2[0:32, :nt], sw[0:32, :nt], rk[:, 1, :nt], MULT)
                K.tt("pool", KR[:, t0:t0 + nt], t1[0:32, :nt], t2[0:32, :nt], ADD)
        d1 = K.dma(xchg_o[0:256, :].rearrange("(c p) t -> p c t", p=128), CKVN)
        d2 = K.dma(xchg_o[256:288, :], KR)
        if mode == "A":
            K.final += [d1, d2]
            K.final.append(K.dma(cqn_o.rearrange("(c p) t -> p c t", p=128), CQN))
            x1v = x1_o.rearrange("(k p) t -> p k t", p=128)
            for k in range(8):
                K.final.append(K.dma(x1v[:, k, :], xT[:, k, 0:T]))

    if B_:
        CQN = arv(9472, [3, T])
        CKVALL = v3(A36[:, 0:2 * NKEY], 2)
        VVh = arv(0, [66, 128])
        KTh = arv(15616, [NKEY], p0=0, p1=96)
        QTh = arv(24064, [T], p0=0, p1=96)
        ropeQ = arv(26112, [2, T], p0=64, p1=96)
        PTb = [arv(30208 + i * 512, [512]) for i in range(3)]
        wuqh = [arv(31744 + i * 288, [3, 96]) for i in range(2)]
        wukvh = [arv(32320 + i * 256, [2, 128]) for i in range(2)]
        woh = [arv(32832 + i * 1024, [1024], p0=0, p1=64) for i in range(2)]
        OTh = [arv(34880 + i * 512, [512], p0=0, p1=64) for i in range(2)]
        if mode == "B":
            x1v = x1_d.rearrange("(k p) t -> p k t", p=128)
            for k in range(8):
                K.dma(xT[:, k, 0:T], x1v[:, k, :])
            K.dma(CQN, cqn_d.rearrange("(c p) t -> p c t", p=128))
            mo = K.sb("mo", [128, 96], F32)
            K.dma(mo[:], mod1_d)
            K.copy("dve", mod[:, 1, :, :], v3(mo[:, :], 48))
            for which in range(2):
                sck = 1 if which == 0 else 4
                nv = vecs[:, (2 + which) * 8:(2 + which) * 8 + 8]
                K.stt("dve", Amat[:, 1, which, 0, :], mod[:, 1, sck * 8:sck * 8 + 8, 0], 1.0, nv, ADD, MULT)
            for c in range(2):
                for kq in range(4):
                    K.dma(CKVALL[:, c, kq * 2112:(kq + 1) * 2112], kvall_d[c * 128:(c + 1) * 128, kq * 2112:(kq + 1) * 2112])
            K.dma(KTh[64:96, :], kvall_d[256:288, :])
        else:
            gat = K.S.add("pool", lambda e: e.collective_compute("AllGather", ALU.bypass, replica_groups=[[0, 1, 2, 3], [4, 5, 6, 7]],
                                                                 ins=[xchg_o[:, :]], outs=[kvg_i[:, :]]),
                          [xchg_o[:, :]], [kvg_i[:, :]], dma=True)
            for rr in range(4):
                for c in range(2):
                    K.dma(CKVALL[:, c, rr * T:(rr + 1) * T], kvg_i[rr * 288 + c * 128: rr * 288 + (c + 1) * 128, 0:T])
                K.dma(KTh[64:96, rr * T:(rr + 1) * T], kvg_i[rr * 288 + 256: rr * 288 + 288, 0:T])
            for c in range(2):
                K.dma(CKVALL[:, c, 4 * T:NKEY], xchg_o[c * 128:(c + 1) * 128, T:T + CT])
            K.dma(KTh[64:96, 4 * T:NKEY], xchg_o[256:288, T:T + CT])
        K.dma(ropeQ, ropeQ_d.rearrange("c p t -> p c t"))
        K.memset("pool", VVh[:, :, 64:128], 1.0)
        wuqv = wuq_d.rearrange("(c p) f -> p c f", p=128)
        wukvv = wukv_d.rearrange("(c p) f -> p c f", p=128)
        MISC = [PS[5], PS[6], PS[7]]
        KBLK = [(kb * 512, 512) for kb in range(16)] + [(8192, 256)]
        def load_head(h):
            K.dma(wuqh[h % 2], wuqv[:, :, h * 96:(h + 1) * 96], eng="pool")
            K.dma(wukvh[h % 2], wukvv[:, :, h * 128:(h + 1) * 128], eng="pool")
            K.dma(woh[h % 2], owout_d[h * 64:(h + 1) * 64, :], eng="pool")
        load_head(0)
        for h in range(16):
            wq = wuqh[h % 2]
            wkv = wukvh[h % 2]
            wo = woh[h % 2]
            if h + 1 < 16:
                load_head(h + 1)
            for (k0, nk) in KBLK:
                acc = K.rot("misc", MISC)
                for c in range(2):
                    K.mm(acc[0:64, :nk], wkv[:, c, 0:64], CKVALL[:, c, k0:k0 + nk], start=(c == 0), stop=(c == 1))
                raw = K.rot("ftmp", ftmp)
                K.copy("dve", raw[0:64, :nk], acc[0:64, :nk])
                sq = K.rot("sqb", sqb)
                K.tt("pool", sq[0:64, :nk], raw[0:64, :nk], raw[0:64, :nk], MULT)
                ss = K.rot("misc", MISC)
                K.mm(ss[0:64, :nk], ones128[0:64, 0:64], sq[0:64, :nk])
                rs = K.rot("rsb", rsb)
                K.actv(rs[0:64, :nk], ss[0:64, :nk], AF.Sqrt, scale=1.0 / 64.0, bias=epsv[0:64, 0:1])
                K.recip(rs[0:64, :nk], rs[0:64, :nk])
                K.stt("dve", KTh[0:64, k0:k0 + nk], raw[0:64, :nk], vecs[0:64, 42:43], rs[0:64, :nk], MULT, MULT)
            for c4 in range(0, 66, 4):
                n4 = min(4, 66 - c4)
                acc = K.rot("misc", MISC)
                for i in range(n4):
                    for c in range(2):
                        K.mm(acc[:, i * 64:(i + 1) * 64], CKVALL[:, c, (c4 + i) * 128:(c4 + i + 1) * 128], wkv[:, c, 64:128],
                             start=(c == 0), stop=(c == 1), skip=True)
                K.copy("dve", VVh[:, c4:c4 + n4, 0:64], v3(acc[:, 0:n4 * 64], n4))
            for qb in range(4):
                acc = K.rot("misc", MISC)
                for c in range(3):
                    K.mm(acc[0:96, :], wq[:, c, :], CQN[:, c, qb * 512:(qb + 1) * 512], start=(c == 0), stop=(c == 2))
                raw = K.rot("ftmp", ftmp)
                K.copy("dve", raw[0:96, :], acc[0:96, :])
                sq = K.rot("sqb", sqb)
                K.tt("pool", sq[0:96, :], raw[0:96, :], raw[0:96, :], MULT)
                ss = K.rot("misc", MISC)
                K.mm(ss[0:96, :], blk96[0:96, 0:96], sq[0:96, :])
                rs = K.rot("rsb", rsb)
                K.actv(rs[0:96, :], ss[0:96, :], AF.Sqrt, scale=vecs[0:96, 44:45], bias=epsv[0:96, 0:1])
                K.recip(rs[0:96, :], rs[0:96, :])
                qd = QTh[0:96, qb * 512:(qb + 1) * 512]
                K.stt("dve", qd, raw[0:96, :], gsc[0:96, 2:3], rs[0:96, :], MULT, MULT)
                sw = K.rot("misc", MISC)
                qr = QTh[64:96, qb * 512:(qb + 1) * 512]
                K.mm(sw[0:32, :], cmat[64:96, 3, 0:32], qr)
                t1 = K.rot("ftmp", ftmp)
                t2 = K.rot("ftmp", ftmp)
                K.tt("pool", t1[64:96, :], qr, ropeQ[:, 0, qb * 512:(qb + 1) * 512], MULT)
                K.tt("dve", t2[64:96, :], sw[0:32, :], ropeQ[:, 1, qb * 512:(qb + 1) * 512], MULT)
                K.tt("pool", qr, t1[64:96, :], t2[64:96, :], ADD)
            for qb in range(4):
                O = K.rot("Ob", [PS[3], PS[4]])
                for c in range(66):
                    Sb = K.rot("Sb", [PS[0], PS[1], PS[2]])
                    K.mm(Sb[:, :], KTh[0:96, c * 128:(c + 1) * 128], QTh[0:96, qb * 512:(qb + 1) * 512])
                    pt = K.rot("PT", PTb)
                    K.actv(pt, Sb[:, :], AF.Exp)
                    K.mm(O[:, :], VVh[:, c, :], pt, start=(c == 0), stop=(c == 65))
                rec = K.rot("rsb", rsb)
                K.recip(rec[64:128, :], O[64:128, :])
                ot = K.rot("OTh", OTh)
                K.tt("dve", ot, O[0:64, :], rec[64:128, :], MULT)
                for m in range(8):
                    Y = K.rot("misc", MISC)
                    K.mm(Y[:, :], wo[:, m * 128:(m + 1) * 128], ot)
                    K.stt("dve", xT[:, m, qb * 512:(qb + 1) * 512], Y[:, :], mod_ap(1, 2, m, 0), xT[:, m, qb * 512:(qb + 1) * 512], MULT, ADD)
        for (t0, nt, j) in OWN_BLOCKS:
            norm_block(xT[:, :, t0:t0 + nt], nt, 1, 1, 0, A36v[:, :, t0:t0 + nt])
        mlp(1, OWN_BLOCKS, A36v)
        ov = out_o.rearrange("(k p) t -> p k t", p=128)
        for k in range(8):
            K.final.append(K.dma(ov[:, k, :], xT[:, k, 0:T]))

    return finish()


_BF = ml_dtypes.bfloat16


def _fm(v):
    return np.ascontiguousarray(np.asarray(v, np.float32).reshape(-1, 128).T)


def _rope_tab(pos, hw):
    inv = (np.float32(10000.0) ** (-np.arange(hw, dtype=np.float32) / np.float32(hw))).astype(np.float32)
    ang = pos.astype(np.float32)[None, :] * inv[:, None]
    return np.cos(ang).astype(np.float32), np.sin(ang).astype(np.float32)


def _perm_signed(blocks, n):
    P = np.zeros((n, n), np.float32)
    for (b, hw) in blocks:
        for i in range(hw):
            P[b + hw + i, b + i] = -1.0
            P[b + i, b + hw + i] = 1.0
    return P


def _host_common(inp):
    f32 = np.float32
    vecs = np.zeros((128, 48), f32)
    vecs[:, 0:8] = _fm(inp["norm_mix"][0])
    vecs[:, 8:16] = _fm(inp["norm_mlp"][0])
    vecs[:, 16:24] = _fm(inp["norm_mix"][1])
    vecs[:, 24:32] = _fm(inp["norm_mlp"][1])
    rep64 = lambda v: np.tile(np.asarray(v, f32).reshape(64), 2)
    vecs[:, 32] = rep64(inp["a_q_norm"][0])
    vecs[:, 33] = rep64(inp["a_k_norm"][0])
    vecs[:, 34] = rep64(inp["b_q_norm"][0])
    vecs[:, 35] = rep64(inp["b_k_norm"][0])
    vecs[:, 36:39] = _fm(inp["o_qa_norm"][0])
    vecs[:, 39:41] = _fm(inp["o_kva_norm"][0])
    vecs[0:64, 41] = inp["o_qn_nope"][0]
    vecs[64:96, 41] = inp["o_qn_rope"][0]
    vecs[:, 42] = rep64(inp["o_kn_nope"][0])
    vecs[:, 43] = np.tile(np.asarray(inp["o_kn_rope"][0], f32), 4)
    vecs[0:64, 44] = 1.0 / 64.0
    vecs[64:128, 44] = 1.0 / 32.0
    cmat = np.zeros((4, 128, 128), f32)
    cmat[0] = np.eye(128, dtype=f32)
    cmat[1] = _perm_signed([(0, 16), (32, 16), (64, 16), (96, 16)], 128)
    p32 = _perm_signed([(0, 8), (16, 8)], 32)
    cmat[2, 0:32, 0:32] = p32
    cmat[3, 64:96, 0:32] = p32
    return dict(vecs=vecs, cmat=cmat.astype(_BF),
                mlp_w1=np.ascontiguousarray(inp["mlp_w1"], f32), mlp_w2=np.ascontiguousarray(inp["mlp_w2"], f32))


def _rope32_tables(tok):
    row, col = tok // 64, tok % 64
    cr, sr = _rope_tab(row, 8)
    cc, sc = _rope_tab(col, 8)
    cos = np.concatenate([cr, cr, cc, cc], 0)
    sin = np.concatenate([sr, sr, sc, sc], 0)
    return np.stack([cos, sin], 0).astype(np.float32)


def _host_A(inp, core):
    f32 = np.float32
    b, r = core // 4, core % 4
    x = inp["x"][b]
    d = {}
    d["xT"] = np.ascontiguousarray(x[r * T:(r + 1) * T].T, f32)
    xh = np.zeros((1024, 2 * HAL), f32)
    if r > 0:
        xh[:, 0:HAL] = x[r * T - HAL:r * T].T
    if r < 3:
        xh[:, HAL:] = x[(r + 1) * T:(r + 1) * T + HAL].T
    d["xhT"] = xh
    d["ctxT"] = np.ascontiguousarray(inp["ctx"][b].T, f32)
    cond = np.zeros((128, 8, 2), f32)
    cond[:, :, 0] = _fm(inp["c"][b])
    cond[:, :, 1] = _fm(inp["c_ctx"])
    d["condT"] = cond.reshape(128, 16)
    d["ada_w"] = np.ascontiguousarray(inp["ada_w"], f32)
    d["adab"] = np.concatenate([_fm(inp["ada_b"][0]), _fm(inp["ada_b"][1])], 1)
    d["e_w_in"] = np.ascontiguousarray(inp["e_w_in"][0], f32)
    d["e_w_out"] = np.ascontiguousarray(inp["e_w_out"][0], f32)
    d["o_w_in"] = np.ascontiguousarray(inp["o_w_in"][0], f32)
    tok = np.arange(r * T - HAL, (r + 1) * T + HAL)
    tokc = np.clip(tok, 0, 8191)
    cr, sr = _rope_tab(tokc // 64, 16)
    cc, sc = _rope_tab(tokc % 64, 16)
    cos64 = np.concatenate([cr, cr, cc, cc], 0)
    sin64 = np.concatenate([sr, sr, sc, sc], 0)
    d["ropeA"] = np.stack([np.tile(cos64, (2, 1)), np.tile(sin64, (2, 1))], 0).astype(f32)
    d["ropeK"] = _rope32_tables(np.arange(r * T, (r + 1) * T))
    kk = np.arange(128)[:, None]
    qq = np.arange(128)[None, :]
    prev = np.where(kk >= qq, 0.0, NEG).astype(f32)
    nxt = np.where(kk <= qq, 0.0, NEG).astype(f32)
    allneg = np.full((128, 128), NEG, f32)
    var = [prev, nxt, allneg if r == 0 else prev, allneg if r == 3 else nxt]
    d["amask"] = np.stack([np.tile(v, (1, 4)) for v in var], 1).astype(_BF)
    rpb = np.asarray(inp["b_rpb"][0], f32)
    bb = np.full((5, 128, 6, 8, 128), NEG, f32)
    k_i = np.arange(128)
    q_i = np.arange(128)
    for vi, jt in enumerate([0, 1, 5, 14, 15]):
        if jt == 0:
            ms = list(range(-2, 4))
        elif jt == 15:
            ms = list(range(12, 18))
        else:
            ms = list(range(jt - 2, jt + 3))
        rr = r if vi != 2 else 1
        gq = 32 * rr + 2 * jt + q_i // 64
        cq = q_i % 64
        start = np.clip(gq - 4, 0, 120)
        c0 = np.clip(cq - 8, 0, 48)
        for ci, m in enumerate(ms):
            gk = 32 * rr + 2 * m + k_i // 64
            ck = k_i % 64
            valid = ((gk[:, None] >= 0) & (gk[:, None] < 128) & (gk[:, None] >= start[None, :]) & (gk[:, None] < start[None, :] + 8)
                     & (ck[:, None] >= c0[None, :]) & (ck[:, None] < c0[None, :] + 16))
            dri = np.clip(gk[:, None] - gq[None, :] + 7, 0, 14)
            dci = np.clip(ck[:, None] - cq[None, :], -15, 15) + 15
            g = rpb[:, dri, dci]
            bb[vi, :, ci, :, :] = np.where(valid[None], g, NEG).transpose(1, 0, 2)
    bb = bb[:, :, :, [0, 2, 1, 3, 4, 6, 5, 7], :]
    d["bbias"] = np.ascontiguousarray(bb).reshape(5, 128, 6, 1024).astype(_BF)
    d["sink"] = np.tile(np.asarray(inp["a_sink"][0], f32)[None, :], (128, 1))
    return d


def _host_B(inp, core):
    f32 = np.float32
    r = core % 4
    d = {}
    d["o_w_uq"] = np.ascontiguousarray(inp["o_w_uq"][0], f32)
    d["o_w_ukv"] = np.ascontiguousarray(inp["o_w_ukv"][0], f32)
    d["o_w_out"] = np.ascontiguousarray(inp["o_w_out"][0], f32)
    d["ropeQ"] = _rope32_tables(np.arange(r * T, (r + 1) * T)).astype(_BF)
    return d


_NC_CACHE = {}


def _get_nc(mode):
    if mode not in _NC_CACHE:
        _NC_CACHE[mode] = build(mode).nc
    return _NC_CACHE[mode]


FUSED = False


def kernel(**inputs):
    inp = {k: np.asarray(v) for k, v in inputs.items()}
    common = _host_common(inp)
    out = np.empty((2, 8192, 1024), np.float32)
    if FUSED:
        maps = []
        for c in range(NCORES):
            m = dict(common)
            m.update(_host_A(inp, c))
            m.update(_host_B(inp, c))
            maps.append(m)
        res = run_bass_kernel_spmd(_get_nc("F"), maps, core_ids=list(range(NCORES)))
        for c in range(NCORES):
            out[c // 4, (c % 4) * T:(c % 4 + 1) * T, :] = np.asarray(res.results[c]["outT"]).T
        return out
    mapsA = []
    for c in range(NCORES):
        m = dict(common)
        m.update(_host_A(inp, c))
        mapsA.append(m)
    resA = run_bass_kernel_spmd(_get_nc("A"), mapsA, core_ids=list(range(NCORES)))
    ra = resA.results
    mapsB = []
    for c in range(NCORES):
        b = c // 4
        m = dict(common)
        m.update(_host_B(inp, c))
        m["x1T"] = np.asarray(ra[c]["x1T"])
        m["cqn"] = np.asarray(ra[c]["cqn"])
        m["mod1"] = np.asarray(ra[c]["mod1"])
        kv = np.concatenate([np.asarray(ra[4 * b + rr]["xchg"])[:, 0:T] for rr in range(4)] + [np.asarray(ra[c]["xchg"])[:, T:T + CT]], axis=1)
        m["kvall"] = np.ascontiguousarray(kv)
        mapsB.append(m)
    resB = run_bass_kernel_spmd(_get_nc("B"), mapsB, core_ids=list(range(NCORES)))
    for c in range(NCORES):
        out[c // 4, (c % 4) * T:(c % 4 + 1) * T, :] = np.asarray(resB.results[c]["outT"]).T
    return out
```

```python
import contextlib
import numpy as np
import ml_dtypes
import concourse.bass as bass
import concourse.mybir as mybir
from concourse.bass_utils import run_bass_kernel_spmd

F32 = mybir.dt.float32
BF16 = mybir.dt.bfloat16
AF = mybir.ActivationFunctionType
ALU = mybir.AluOpType

NCORES = 8
T = 2048
HAL = 256
CT = 256
E = HAL + T + HAL + CT
NQ = T + CT
NKEY = 8192 + CT
EPS = 1e-6
NEG = -30000.0


def _region(ap):
    name = ap.name
    space = str(ap.space)
    dims = ap.ap
    off = int(ap.offset)
    if space == "DRAM":
        lo = off
        hi = off + sum(int(s) * (int(c) - 1) for s, c in dims if int(s) > 0) + 1
        return (name, "DRAM", 0, 1, lo, hi)
    if space == "PSUM":
        fszp = 1
        for d in ap.tensor.shape[1:]:
            fszp *= int(d)
        g0 = off % fszp
        g1 = g0 + sum(int(s) * (int(c) - 1) for s, c in dims[1:] if int(s) > 0) + 1
        return (name, "PSUM", 0, 128, g0 // 512, (g1 - 1) // 512 + 1)
    pstep, pcnt = int(dims[0][0]), int(dims[0][1])
    fsz = 1
    for d in ap.tensor.shape[1:]:
        fsz *= int(d)
    p0 = off // fsz
    f0 = off % fsz
    p1 = p0 + 1 if pstep == 0 else p0 + (pstep // fsz) * (pcnt - 1) + 1
    f1 = f0 + sum(int(s) * (int(c) - 1) for s, c in dims[1:] if int(s) > 0) + 1
    return (name, "SB", p0, p1, f0, f1)


def _overlap(a, b):
    return a[2] < b[3] and b[2] < a[3] and a[4] < b[5] and b[4] < a[5]


def _covers(a, b):
    return a[2] <= b[2] and a[3] >= b[3] and a[4] <= b[4] and a[5] >= b[5]


class Sched:
    ENGS = ("pe", "act", "dve", "pool", "sp")

    def __init__(self, nc, n_dma_sems=12):
        self.nc = nc
        self.ops = []
        self.track = {}
        self.n_dma_sems = n_dma_sems

    def add(self, eng, fn, reads=(), writes=(), dma=False):
        idx = len(self.ops)
        rr = list(dict.fromkeys(_region(a) for a in reads))
        ww = list(dict.fromkeys(_region(a) for a in writes))
        deps = set()
        for r in rr:
            lst = self.track.setdefault(r[0], [])
            psum = r[1] == "PSUM"
            for (box, oi, kind) in lst:
                if (kind == "w" or psum) and _overlap(box, r):
                    deps.add(oi)
        for w in ww:
            lst = self.track.setdefault(w[0], [])
            for (box, oi, kind) in lst:
                if _overlap(box, w):
                    deps.add(oi)
        for r in rr:
            lst = self.track[r[0]]
            if r[1] == "PSUM":
                lst[:] = [t for t in lst if not _covers(r, t[0])]
                lst.append((r, idx, "w"))
            else:
                lst[:] = [t for t in lst if not (t[2] == "r" and t[1] < idx and self.ops[t[1]]["eng"] == eng
                                                 and not self.ops[t[1]]["dma"] and not dma and _covers(r, t[0]))]
                lst.append((r, idx, "r"))
        for w in ww:
            lst = self.track[w[0]]
            lst[:] = [t for t in lst if not _covers(w, t[0])]
            lst.append((w, idx, "w"))
        deps.discard(idx)
        self.ops.append(dict(eng=eng, fn=fn, deps=deps, dma=dma, sig=False, rr=rr, ww=ww))
        return idx

    def dma(self, out, in_, eng="sp"):
        return self.add(eng, lambda e: e.dma_start(out=out, in_=in_), [in_], [out], dma=True)

    def emit(self, final_wait_ops=()):
        nc = self.nc
        ops = self.ops

        def needs_wait(x, y):
            X, Y = ops[x], ops[y]
            if Y["dma"] or X["dma"]:
                return True
            if X["eng"] == Y["eng"]:
                if X["eng"] == "pe":
                    return False
                for w in Y["ww"]:
                    for r in X["rr"]:
                        if w[0] == r[0] and _overlap(w, r):
                            return True
                return False
            return True

        for i, X in enumerate(ops):
            X["wdeps"] = [y for y in X["deps"] if needs_wait(i, y)]
            for y in X["wdeps"]:
                ops[y]["sig"] = True
        for i in final_wait_ops:
            ops[i]["sig"] = True
        cnt = {e: 0 for e in self.ENGS}
        dma_k = {e: 0 for e in self.ENGS}
        dma_semcnt = {}
        for X in ops:
            if X["dma"]:
                q = X["eng"]
                k = dma_k[q]
                dma_k[q] += 1
                s = (q, k % self.n_dma_sems)
                dma_semcnt[s] = dma_semcnt.get(s, 0) + 1
                X["dsem"] = s
                X["dval"] = 16 * dma_semcnt[s]
            elif X["sig"]:
                cnt[X["eng"]] += 1
                X["cnt"] = cnt[X["eng"]]
        with contextlib.ExitStack() as st:
            sems = {e: st.enter_context(nc.semaphore("s_" + e)) for e in ("pe", "act", "dve", "pool")}
            dsems = {}
            for q in self.ENGS:
                for j in range(min(self.n_dma_sems, dma_k[q])):
                    dsems[(q, j)] = st.enter_context(nc.semaphore("d_%s_%d" % (q, j)))
            block = st.enter_context(nc.Block())
            per_eng = {e: [i for i, X in enumerate(ops) if X["eng"] == e] for e in self.ENGS}

            def run_stream(ename, e):
                known = {}

                def wait(key, semh, val):
                    if known.get(key, 0) >= val:
                        return
                    e.wait_ge(semh, val)
                    known[key] = val

                def wait_op(Y):
                    if Y["dma"]:
                        wait(Y["dsem"], dsems[Y["dsem"]], Y["dval"])
                    else:
                        wait(Y["eng"], sems[Y["eng"]], Y["cnt"])

                for i in per_eng[ename]:
                    X = ops[i]
                    for y in sorted(X["wdeps"]):
                        wait_op(ops[y])
                    if X["dma"]:
                        if X["dval"] > 16:
                            wait(X["dsem"], dsems[X["dsem"]], X["dval"] - 16)
                        X["fn"](e).then_inc(dsems[X["dsem"]], 16)
                    else:
                        ins = X["fn"](e)
                        if X["sig"]:
                            ins.then_inc(sems[ename], 1)
                if ename == "sp":
                    for i in final_wait_ops:
                        wait_op(ops[i])

            @block.tensor
            def _(e):
                run_stream("pe", e)

            @block.scalar
            def _(e):
                run_stream("act", e)

            @block.vector
            def _(e):
                run_stream("dve", e)

            @block.gpsimd
            def _(e):
                run_stream("pool", e)

            @block.sync
            def _(e):
                run_stream("sp", e)
        self.stats = dict(n_ops=len(ops), cnt=cnt, dma=dma_k)


class KB:
    def __init__(self, mode):
        self.mode = mode
        self.nc = bass.Bass("TRN2", target_bir_lowering=False)
        self.S = Sched(self.nc)
        self.st = contextlib.ExitStack()
        self.final = []
        self._rot = {}

    def din(self, name, shape, dt=F32):
        return self.nc.dram_tensor(name, list(shape), dt, kind="ExternalInput").ap()

    def dout(self, name, shape, dt=F32):
        return self.nc.dram_tensor(name, list(shape), dt, kind="ExternalOutput").ap()

    def dint(self, name, shape, dt=F32):
        return self.nc.dram_tensor(name, list(shape), dt, kind="Internal").ap()

    def sb(self, name, shape, dt):
        return self.st.enter_context(self.nc.sbuf_tensor(name, list(shape), dt))

    def rot(self, key, lst):
        i = self._rot.get(key, 0)
        self._rot[key] = i + 1
        return lst[i % len(lst)]

    def mm(self, out, lhsT, rhs, start=True, stop=True, skip=False):
        kw = dict(skip_group_check=True) if skip else {}
        return self.S.add("pe", lambda e: e.matmul(out, lhsT=lhsT, rhs=rhs, start=start, stop=stop, **kw),
                          [lhsT, rhs], [out])

    def actv(self, out, in_, func, scale=1.0, bias=None):
        reads = [in_]
        kw = {}
        if isinstance(scale, float) or isinstance(scale, int):
            kw["scale"] = float(scale)
        else:
            kw["scale"] = scale
            reads.append(scale)
        if bias is not None:
            kw["bias"] = bias
            if not isinstance(bias, float):
                reads.append(bias)
        return self.S.add("act", lambda e: e.activation(out=out, in_=in_, func=func, **kw), reads, [out])

    def tt(self, eng, out, in0, in1, op):
        return self.S.add(eng, lambda e: e.tensor_tensor(out=out, in0=in0, in1=in1, op=op), [in0, in1], [out])

    def stt(self, eng, out, in0, scalar, in1, op0, op1):
        reads = [in0, in1]
        if not isinstance(scalar, float):
            reads.append(scalar)
        return self.S.add(eng, lambda e: e.scalar_tensor_tensor(out=out, in0=in0, scalar=scalar, in1=in1, op0=op0, op1=op1),
                          reads, [out])

    def ts(self, eng, out, in0, s1, op0, s2=None, op1=None):
        reads = [in0]
        if not isinstance(s1, float):
            reads.append(s1)
        if s2 is not None and not isinstance(s2, float):
            reads.append(s2)
        if op1 is None:
            return self.S.add(eng, lambda e: e.tensor_scalar(out=out, in0=in0, scalar1=s1, scalar2=None, op0=op0), reads, [out])
        return self.S.add(eng, lambda e: e.tensor_scalar(out=out, in0=in0, scalar1=s1, scalar2=s2, op0=op0, op1=op1), reads, [out])

    def copy(self, eng, out, in_):
        if eng == "act":
            return self.S.add("act", lambda e: e.activation(out=out, in_=in_, func=AF.Copy), [in_], [out])
        return self.S.add(eng, lambda e: e.tensor_copy(out=out, in_=in_), [in_], [out])

    def recip(self, out, in_):
        return self.S.add("dve", lambda e: e.reciprocal(out=out, in_=in_), [in_], [out])

    def memset(self, eng, out, val):
        return self.S.add(eng, lambda e: e.memset(out, val), [], [out])

    def dma(self, out, in_, eng="sp"):
        return self.S.dma(out, in_, eng)


def v3(ap2, a):
    return ap2.rearrange("p (a b) -> p a b", a=a)


MULT, ADD = ALU.mult, ALU.add
NA = 36480


def build(mode, stop=0):
    K = KB(mode)
    nc, S = K.nc, K.S
    A_ = mode in ("A", "F")
    B_ = mode in ("B", "F")

    vecs_d = K.din("vecs", [128, 48])
    cmat_d = K.din("cmat", [4, 128, 128], BF16)
    w1_d = K.din("mlp_w1", [2, 1024, 4096])
    w2_d = K.din("mlp_w2", [2, 4096, 1024])
    if A_:
        xT_d = K.din("xT", [1024, T])
        xh_d = K.din("xhT", [1024, 2 * HAL])
        ctx_d = K.din("ctxT", [1024, CT])
        cond_d = K.din("condT", [128, 16])
        adaw_d = K.din("ada_w", [2, 1024, 6144])
        adab_d = K.din("adab", [128, 96])
        ewin_d = K.din("e_w_in", [1024, 2304])
        ewout_d = K.din("e_w_out", [1024, 1024])
        owin_d = K.din("o_w_in", [1024, 672])
        ropeA_d = K.din("ropeA", [2, 128, 2 * HAL + T])
        ropeK_d = K.din("ropeK", [2, 32, T])
        amask_d = K.din("amask", [128, 4, 512], BF16)
        bbias_d = K.din("bbias", [5, 128, 6, 1024], BF16)
        sink_d = K.din("sink", [128, 8])
    if B_:
        wuq_d = K.din("o_w_uq", [384, 1536])
        wukv_d = K.din("o_w_ukv", [256, 2048])
        owout_d = K.din("o_w_out", [1024, 1024])
        ropeQ_d = K.din("ropeQ", [2, 32, T], BF16)
        out_o = K.dout("outT", [1024, T])
    if mode == "A":
        x1_o = K.dout("x1T", [1024, T])
        cqn_o = K.dout("cqn", [384, T], BF16)
        xchg_o = K.dout("xchg", [288, T + CT], BF16)
        mod1_o = K.dout("mod1", [128, 96])
    if mode == "B":
        x1_d = K.din("x1T", [1024, T])
        cqn_d = K.din("cqn", [384, T], BF16)
        kvall_d = K.din("kvall", [288, NKEY], BF16)
        mod1_d = K.din("mod1", [128, 96])
    if mode == "F":
        xchg_o = K.dint("xchg_i", [288, T + CT], BF16)
        kvg_i = K.dint("kvg_i", [4 * 288, T + CT], BF16)

    if stop:
        dbg_o = K.dout("dbg", [128, NA], BF16)
        dbg36_o = K.dout("dbg36", [128, 8 * NQ], BF16)
        dbgx_o = K.dout("dbgx", [128, 8 * NQ])

    def finish():
        if stop:
            K.final.append(K.dma(dbg_o, AR[:, :]))
            K.final.append(K.dma(dbg36_o, A36[:, :]))
            K.final.append(K.dma(dbgx_o, xT[:, :, :].rearrange("p a b -> p (a b)")))
        S.emit(final_wait_ops=K.final)
        return K

    xT = K.sb("xTs", [128, 8, NQ], F32)
    A36 = K.sb("A36", [128, 8 * NQ], BF16)
    A36v = v3(A36[:, :], 8)
    mod = K.sb("mod", [128, 2, 48, 2], F32)
    Amat = K.sb("Amat", [128, 2, 2, 2, 8], F32)
    vecs = K.sb("vecs_s", [128, 48], F32)
    gsc = K.sb("gsc", [128, 4], F32)
    ones128 = K.sb("ones128", [128, 128], BF16)
    blk = K.sb("blk", [128, 128], BF16)
    blk96 = K.sb("blk96", [128, 128], BF16)
    cmat = K.sb("cmat_s", [128, 4, 128], BF16)
    ident = cmat[:, 0, :]
    sqb = [K.sb("sqb%d" % i, [128, 512], BF16) for i in range(2)]
    ftmp = [K.sb("ftmp%d" % i, [128, 512], F32) for i in range(3)]
    rsb = [K.sb("rsb%d" % i, [128, 512], F32) for i in range(2)]
    AR = K.sb("AR", [128, NA], BF16)
    ARF = K.sb("ARF", [128, 3072], F32)
    PD = [K.st.enter_context(nc.psum_tensor("PD%d" % i, [128, 2, 512], F32)) for i in range(4)]
    PS = [PD[i // 2][:, i % 2, :] for i in range(8)]

    def pipeline(n, issue, consume, look=1):
        for i in range(min(look, n)):
            issue(i)
        for i in range(n):
            if i + look < n:
                issue(i + look)
            consume(i)

    def arv(off, dims, p0=0, p1=128, t=AR):
        n = 1
        for d in dims:
            n *= d
        ap = t[p0:p1, off:off + n]
        if len(dims) == 2:
            ap = ap.rearrange("p (a b) -> p a b", a=dims[0])
        elif len(dims) == 3:
            ap = ap.rearrange("p (a b c) -> p a b c", a=dims[0], b=dims[1])
        return ap

    K.dma(vecs[:], vecs_d)
    K.dma(cmat[:], cmat_d.rearrange("c p q -> p c q"))
    epsv = K.sb("epsv", [128, 1], F32)
    K.memset("dve", epsv[:], EPS)
    K.memset("dve", ones128[:], 1.0)
    K.memset("dve", blk[:], 0.0)
    K.memset("dve", blk[0:64, 0:64], 1.0)
    K.memset("dve", blk[64:128, 64:128], 1.0)
    K.memset("dve", blk96[:], 0.0)
    K.memset("dve", blk96[0:64, 0:64], 1.0)
    K.memset("dve", blk96[64:96, 64:96], 1.0)
    K.ts("dve", gsc[:, 0:1], vecs[:, 32:33], 0.125, MULT)
    K.ts("dve", gsc[:, 1:2], vecs[:, 34:35], 0.125, MULT)
    K.ts("dve", gsc[:, 2:3], vecs[:, 41:42], float(96 ** -0.5), MULT)

    def mod_ap(l, kind, m, j):
        return mod[:, l, kind * 8 + m, j:j + 1]

    def norm_block(src, nt, l, which, j, dst):
        ss = K.rot("ssb", [PS[6], PS[7]])
        for k in range(8):
            sq = K.rot("sqb", sqb)
            K.tt("pool", sq[:, :nt], src[:, k, :], src[:, k, :], MULT)
            K.mm(ss[:, :nt], ones128[:], sq[:, :nt], start=(k == 0), stop=(k == 7))
        rs = K.rot("rsb", rsb)
        K.actv(rs[:, :nt], ss[:, :nt], AF.Sqrt, scale=1.0 / 1024.0, bias=epsv[:, 0:1])
        K.recip(rs[:, :nt], rs[:, :nt])
        shift_kind = 0 if which == 0 else 3
        for k in range(8):
            t = K.rot("ftmp", ftmp)
            K.stt("dve", t[:, :nt], src[:, k, :], Amat[:, l, which, j, k:k + 1], rs[:, :nt], MULT, MULT)
            K.actv(dst[:, k, :], t[:, :nt], AF.Identity, scale=1.0, bias=mod_ap(l, shift_kind, k, j))

    def mlp(l, nblocks, h2T):
        W1v = w1_d[l].rearrange("(k p) f -> p k f", p=128)
        W2v = w2_d[l].rearrange("(k p) f -> p k f", p=128)
        w1b = [arv(0, [8, 512]), arv(4096, [8, 512])]
        w2b = [arv(8192, [4, 1024]), arv(12288, [4, 1024])]
        ub = [arv(16384, [4, 512]), arv(18432, [4, 512])]
        rb = [arv(20480 + i * 512, [512]) for i in range(2)]
        def load_e8(e8):
            K.dma(w1b[e8 % 2], W1v[:, :, e8 * 512:(e8 + 1) * 512], eng="pool")
            K.dma(w2b[e8 % 2], W2v[:, e8 * 4:(e8 + 1) * 4, :], eng="pool")
        load_e8(0)
        for e8 in range(8):
            w1 = w1b[e8 % 2]
            w2 = w2b[e8 % 2]
            if e8 + 1 < 8:
                load_e8(e8 + 1)
            for (t0, nt, j) in nblocks:
                u = K.rot("ub", ub)
                for fc in range(4):
                    acc = K.rot("mlpacc", [PS[0], PS[1], PS[2]])
                    for k in range(8):
                        K.mm(acc[:, :nt], w1[:, k, fc * 128:(fc + 1) * 128], h2T[:, k, t0:t0 + nt], start=(k == 0), stop=(k == 7))
                    r = K.rot("rb", rb)
                    K.actv(r[:, :nt], acc[:, :nt], AF.Relu)
                    K.tt("pool", u[:, fc, :nt], r[:, :nt], r[:, :nt], MULT)
                for m in range(8):
                    acc = K.rot("mlpacc2", [PS[3], PS[4], PS[5]])
                    for fc in range(4):
                        K.mm(acc[:, :nt], w2[:, fc, m * 128:(m + 1) * 128], u[:, fc, :nt], start=(fc == 0), stop=(fc == 3))
                    K.stt("dve", xT[:, m, t0:t0 + nt], acc[:, :nt], mod_ap(l, 5, m, j), xT[:, m, t0:t0 + nt], MULT, ADD)

    OWN_BLOCKS = [(b * 512, 512, 0) for b in range(4)]
    CTX_BLOCK = (T, CT, 1)

    if A_:
        xTv = xT_d.rearrange("(k p) t -> p k t", p=128)
        for k in range(8):
            K.dma(xT[:, k, 0:T], xTv[:, k, :])
        K.dma(xT[:, :, T:NQ], ctx_d.rearrange("(k p) t -> p k t", p=128))
        cond = K.sb("cond", [128, 16], F32)
        silu = K.sb("silu", [128, 16], BF16)
        adab = K.sb("adab_s", [128, 96], F32)
        sinkx = K.sb("sinkx", [128, 8], F32)
        K.dma(cond[:], cond_d)
        K.dma(adab[:], adab_d)
        K.dma(sinkx[:], sink_d)
        K.actv(silu[:], cond[:], AF.Silu)
        K.actv(sinkx[:], sinkx[:], AF.Exp)
        siluv = v3(silu[:, :], 8)
        for l in range(2):
            Wv = adaw_d[l].rearrange("(k p) f -> p k f", p=128)
            acc = PS[0] if l == 0 else PS[1]
            adawb = [arv(0, [8, 1024]), arv(8192, [8, 1024])]
            if l == 0:
                K.dma(adawb[0], Wv[:, :, 0:1024], eng="pool")
            for piece in range(6):
                wb = adawb[(l * 6 + piece) % 2]
                nxt = l * 6 + piece + 1
                if nxt < 12:
                    Wn = adaw_d[nxt // 6].rearrange("(k p) f -> p k f", p=128)
                    K.dma(adawb[nxt % 2], Wn[:, :, (nxt % 6) * 1024:(nxt % 6 + 1) * 1024], eng="pool")
                for m in range(8):
                    f = piece * 8 + m
                    for k in range(8):
                        K.mm(acc[:, 2 * f:2 * f + 2], wb[:, k, m * 128:(m + 1) * 128], siluv[:, k, :], start=(k == 0), stop=(k == 7), skip=True)
            K.tt("dve", mod[:, l, :, :], v3(acc[:, 0:96], 48), adab[:, l * 48:(l + 1) * 48].unsqueeze(2).broadcast_to([128, 48, 2]), ADD)
            for which in range(2):
                sck = 1 if which == 0 else 4
                nv = vecs[:, (l * 2 + which) * 8:(l * 2 + which) * 8 + 8]
                for j in range(2):
                    K.stt("dve", Amat[:, l, which, j, :], mod[:, l, sck * 8:sck * 8 + 8, j], 1.0, nv, ADD, MULT)
        if mode == "A":
            mo = K.sb("mo", [128, 96], F32)
            K.copy("dve", v3(mo[:, :], 48), mod[:, 1, :, :])
            K.final.append(K.dma(mod1_o, mo[:]))
        if stop == 1:
            return finish()

        QT = arv(0, [4, NQ])
        KT = arv(9216, [2, E])
        VV = arv(14848, [22, 4, 128])
        wp = arv(26112, [8, 768])
        hblk = arv(32256, [8, 512])
        PT2 = [arv(26112 + i * 1024, [1024]) for i in range(2)]
        biasb = [arv(28160 + i * 3072, [6, 512]) for i in range(2)]
        amask = arv(34304, [4, 512])
        ropetab = arv(0, [2, 512], t=ARF)
        xhs = arv(1024, [8, 256], t=ARF)
        ewv = ewin_d.rearrange("(k p) c -> p k c", p=128)
        xhv = xh_d.rearrange("(k p) t -> p k t", p=128)
        ropeAv = ropeA_d.rearrange("c p t -> p c t")

        def post_chunk(acc, nt, dst, gain, rope, tabc0):
            sq = K.rot("sqb", sqb)
            raw = K.rot("ftmp", ftmp)
            K.actv(sq[:, :nt], acc[:, :nt], AF.Square)
            K.copy("dve", raw[:, :nt], acc[:, :nt])
            ss = PS[3]
            K.mm(ss[:, :nt], blk[:], sq[:, :nt])
            rs = K.rot("rsb", rsb)
            K.actv(rs[:, :nt], ss[:, :nt], AF.Sqrt, scale=1.0 / 64.0, bias=epsv[:, 0:1])
            K.recip(rs[:, :nt], rs[:, :nt])
            if not rope:
                K.stt("dve", dst, raw[:, :nt], gain, rs[:, :nt], MULT, MULT)
                return
            qn = K.rot("sqb", sqb)
            K.stt("dve", qn[:, :nt], raw[:, :nt], gain, rs[:, :nt], MULT, MULT)
            sw = PS[4]
            K.mm(sw[:, :nt], cmat[:, 1, :], qn[:, :nt])
            t1 = K.rot("ftmp", ftmp)
            t2 = K.rot("ftmp", ftmp)
            K.tt("pool", t1[:, :nt], qn[:, :nt], ropetab[:, 0, :nt], MULT)
            K.tt("dve", t2[:, :nt], sw[:, :nt], ropetab[:, 1, :nt], MULT)
            K.tt("pool", dst, t1[:, :nt], t2[:, :nt], ADD)

        def l0_pass(pid, do_attn=True):
            isA = pid == 0
            half = pid - 1
            nQc = 4 if isA else 2
            nKc = 1 if isA else 2
            nV = 2 if isA else 4
            if isA:
                for s in range(2):
                    for jj in range(4):
                        K.dma(wp[:, :, jj * 128 + s * 64: jj * 128 + s * 64 + 64], ewv[:, :, s * 256 + jj * 64: s * 256 + jj * 64 + 64], eng="pool")
                K.dma(wp[:, :, 512:640], ewv[:, :, 512:640], eng="pool")
                K.dma(wp[:, :, 640:768], ewv[:, :, 640:768], eng="pool")
                qcol0, kcol0, vcol0 = 0, 512, 640
                gq, gk = gsc[:, 0:1], vecs[:, 33:34]
            else:
                K.dma(wp[:, :, 0:256], ewv[:, :, 768 + 256 * half: 768 + 256 * half + 256], eng="pool")
                K.dma(wp[:, :, 256:512], ewv[:, :, 1280 + 256 * half: 1280 + 256 * half + 256], eng="pool")
                K.dma(wp[:, :, 512:768], ewv[:, :, 1792 + 256 * half: 1792 + 256 * half + 256], eng="pool")
                qcol0, kcol0, vcol0 = 0, 256, 512
                gq, gk = gsc[:, 1:2], vecs[:, 35:36]
            K.memset("pool", VV[:, :, 0:nV, 64:128], 1.0)
            blocks = []
            blocks.append(("hb", 256, 0, None, 0, 0))
            for b in range(4):
                blocks.append((b, 512, HAL + b * 512, b * 512, 0, HAL + b * 512))
            blocks.append(("ha", 256, HAL + T, None, 0, HAL + T))
            blocks.append(("ctx", 256, 2 * HAL + T, T, 1, None))
            for (bid, nt, e0, q0, j, rc0) in blocks:
                if bid == "hb":
                    K.dma(xhs, xhv[:, :, 0:256])
                    src = xhs
                elif bid == "ha":
                    K.dma(xhs, xhv[:, :, 256:512])
                    src = xhs
                elif bid == "ctx":
                    src = xT[:, :, T:NQ]
                else:
                    src = xT[:, :, bid * 512:(bid + 1) * 512]
                rope = isA and (rc0 is not None)
                if rope:
                    K.dma(ropetab[:, :, :nt], ropeAv[:, :, rc0:rc0 + nt])
                norm_block(src, nt, 0, 0, j, hblk[:, :, :nt])
                if q0 is not None:
                    for qc in range(nQc):
                        acc = K.rot("pacc", [PS[0], PS[1], PS[2]])
                        for k in range(8):
                            K.mm(acc[:, :nt], wp[:, k, qcol0 + qc * 128: qcol0 + (qc + 1) * 128], hblk[:, k, :nt], start=(k == 0), stop=(k == 7))
                        post_chunk(acc, nt, QT[:, qc, q0:q0 + nt], gq, rope, rc0)
                for kc in range(nKc):
                    acc = K.rot("pacc", [PS[0], PS[1], PS[2]])
                    for k in range(8):
                        K.mm(acc[:, :nt], wp[:, k, kcol0 + kc * 128: kcol0 + (kc + 1) * 128], hblk[:, k, :nt], start=(k == 0), stop=(k == 7))
                    post_chunk(acc, nt, KT[:, kc, e0:e0 + nt], gk, rope, rc0)
                for tt_ in range(nt // 128):
                    acc = PS[5]
                    for k in range(8):
                        K.mm(acc[:, 0:nV * 64], hblk[:, k, tt_ * 128:(tt_ + 1) * 128], wp[:, k, vcol0:vcol0 + nV * 64], start=(k == 0), stop=(k == 7))
                    ec = e0 // 128 + tt_
                    K.copy("act", VV[:, ec, 0:nV, 0:64], v3(acc[:, 0:nV * 64], nV))

            if not do_attn:
                return
            if isA:
                K.dma(amask, amask_d)

            def finalize(O, heads_hc, sink_cols):
                rec = K.rot("rsb", rsb)
                if sink_cols is not None:
                    for hh in range(4):
                        K.ts("dve", rec[64:128, hh * 128:(hh + 1) * 128], O[64:128, hh * 128:(hh + 1) * 128], sinkx[64:128, sink_cols[hh]:sink_cols[hh] + 1], ADD)
                    K.recip(rec[64:128, :], rec[64:128, :])
                else:
                    K.recip(rec[64:128, :], O[64:128, :])
                return rec

            def attn_tile_A(q0, chunks):
                sts = [chunks[i:i + 2] for i in range(0, len(chunks), 2)]
                for g in range(2):
                    O = K.rot("Ob", [PS[4], PS[5]])
                    cur = {}

                    def issue(i, g=g, cur=cur):
                        S2 = K.rot("S2", [PD[0], PD[1]])
                        cur[i] = S2
                        for ii, (ec, mv) in enumerate(sts[i]):
                            if mv is not None:
                                K.mm(S2[:, ii, :], ident, amask[:, mv, :], start=True, stop=False, skip=True)
                            K.mm(S2[:, ii, :], KT[64 * g:64 * g + 64, 0, ec * 128:(ec + 1) * 128], QT[64 * g:64 * g + 64, 0:4, q0:q0 + 128],
                                 start=(mv is None), stop=True, skip=True)

                    def consume(i, g=g, cur=cur, O=O):
                        S2 = cur[i]
                        n = len(sts[i])
                        pt = K.rot("PT2", PT2)
                        K.actv(v3(pt, 2)[:, 0:n, :], S2[:, 0:n, :], AF.Exp)
                        for ii, (ec, mv) in enumerate(sts[i]):
                            first = (i == 0 and ii == 0)
                            last = (i == len(sts) - 1 and ii == n - 1)
                            K.mm(O, VV[:, ec, g, :], pt[:, ii * 512:(ii + 1) * 512], start=first, stop=last)

                    pipeline(len(sts), issue, consume)
                    rec = finalize(O, None, [4 * g + hh for hh in range(4)])
                    for hh in range(4):
                        hc = 4 * g + hh
                        dst = A36v[64 * (hc % 2):64 * (hc % 2) + 64, hc // 2, q0:q0 + 128]
                        K.tt("dve", dst, O[0:64, hh * 128:(hh + 1) * 128], rec[64:128, hh * 128:(hh + 1) * 128], MULT)

            def attn_tile_B(q0, chunks, bias):
                O = K.rot("Ob", [PS[4], PS[5]])
                K.memset("dve", O, 0.0)
                cur = {}

                def issue(i):
                    (ec, bi) = chunks[i]
                    S2 = K.rot("S2", [PD[0], PD[1]])
                    cur[i] = S2
                    for s_ in range(2):
                        if bi is not None:
                            K.mm(S2[:, s_, 0:256], ident, bias[:, bi, s_ * 256:(s_ + 1) * 256], start=True, stop=False, skip=True)
                        for cc in range(2):
                            K.mm(S2[:, s_, cc * 128:(cc + 1) * 128], KT[64 * s_:64 * s_ + 64, cc, ec * 128:(ec + 1) * 128],
                                 QT[64 * s_:64 * s_ + 64, cc, q0:q0 + 128], start=(bi is None), stop=True, skip=True)

                def consume(i):
                    (ec, bi) = chunks[i]
                    S2 = cur[i]
                    pt = K.rot("PT2", PT2)
                    K.actv(v3(pt[:, 0:512], 2), S2[:, :, 0:256], AF.Exp)
                    for hh in range(4):
                        pos = (hh % 2) * 2 + hh // 2
                        K.mm(O[:, hh * 128:(hh + 1) * 128], VV[:, ec, hh, :], pt[:, pos * 128:(pos + 1) * 128], start=False, stop=False, skip=True)

                pipeline(len(chunks), issue, consume)
                rec = finalize(O, None, None)
                for hh in range(4):
                    hc = 8 + 4 * half + hh
                    dst = A36v[64 * (hc % 2):64 * (hc % 2) + 64, hc // 2, q0:q0 + 128]
                    K.tt("dve", dst, O[0:64, hh * 128:(hh + 1) * 128], rec[64:128, hh * 128:(hh + 1) * 128], MULT)

            CTXC = [(20, None), (21, None)]
            for jt in range(16):
                q0 = jt * 128
                if isA:
                    chunks = [(2 + jt - 1, 2 if jt == 0 else 0), (2 + jt, None), (2 + jt + 1, 3 if jt == 15 else 1)] + CTXC
                    attn_tile_A(q0, chunks)
                else:
                    if jt == 0:
                        ms, var = list(range(-2, 4)), 0
                    elif jt == 15:
                        ms, var = list(range(12, 18)), 4
                    else:
                        ms = list(range(jt - 2, jt + 3))
                        var = 1 if jt == 1 else (3 if jt == 14 else 2)
                    bias = K.rot("biasb", biasb)
                    K.dma(bias, bbias_d[var][:, :, 512 * half:512 * half + 512])
                    chunks = [(m + 2, ci) for ci, m in enumerate(ms)] + CTXC
                    attn_tile_B(q0, chunks, bias)
            for ct in range(2):
                q0 = T + ct * 128
                if isA:
                    attn_tile_A(q0, CTXC)
                else:
                    attn_tile_B(q0, CTXC, None)

        if stop == 2:
            l0_pass(0, False)
            return finish()
        if stop == 3:
            l0_pass(0)
            return finish()
        if stop == 4:
            l0_pass(1, False)
            return finish()
        if stop == 5:
            l0_pass(1)
            return finish()
        for pid in range(3):
            l0_pass(pid)
        if stop == 6:
            return finish()

        wout = arv(0, [8, 1024])
        K.dma(wout, ewout_d.rearrange("(k p) f -> p k f", p=128), eng="pool")
        for (t0, nt, j) in OWN_BLOCKS + [CTX_BLOCK]:
            for m in range(8):
                acc = K.rot("oacc", [PS[5], PS[6], PS[7]])
                for k in range(8):
                    K.mm(acc[:, :nt], wout[:, k, m * 128:(m + 1) * 128], A36v[:, k, t0:t0 + nt], start=(k == 0), stop=(k == 7))
                K.stt("dve", xT[:, m, t0:t0 + nt], acc[:, :nt], mod_ap(0, 2, m, j), xT[:, m, t0:t0 + nt], MULT, ADD)
        if stop == 7:
            return finish()
        for (t0, nt, j) in OWN_BLOCKS + [CTX_BLOCK]:
            norm_block(xT[:, :, t0:t0 + nt], nt, 0, 1, j, A36v[:, :, t0:t0 + nt])
        mlp(0, OWN_BLOCKS + [CTX_BLOCK], A36v)

        if stop == 8:
            return finish()
        win1 = arv(0, [8, 672])
        hb1 = arv(5376, [8, 512])
        CQN = arv(9472, [3, T])
        CKVN = arv(15616, [2, NQ])
        KR = arv(20224, [NQ], p0=0, p1=32)
        K.dma(win1, owin_d.rearrange("(k p) c -> p k c", p=128), eng="pool")
        rawv = arv(1024, [3, 512], t=ARF)
        rk = arv(0, [2, 512], p0=0, p1=32, t=ARF)
        ropeKv = ropeK_d.rearrange("c p t -> p c t")
        for (t0, nt, j) in OWN_BLOCKS + [CTX_BLOCK]:
            norm_block(xT[:, :, t0:t0 + nt], nt, 1, 0, j, hb1[:, :, :nt])
            groups = [(384, 2, 256.0, 39, CKVN)]
            if j == 0:
                groups = [(0, 3, 384.0, 36, CQN)] + groups
            for (c0, ncn, dn, gcol, dstT) in groups:
                ss = PS[3]
                for c in range(ncn):
                    acc = K.rot("pacc", [PS[0], PS[1], PS[2]])
                    for k in range(8):
                        K.mm(acc[:, :nt], win1[:, k, c0 + c * 128:c0 + (c + 1) * 128], hb1[:, k, :nt], start=(k == 0), stop=(k == 7))
                    sq = K.rot("sqb", sqb)
                    K.actv(sq[:, :nt], acc[:, :nt], AF.Square)
                    K.copy("dve", rawv[:, c, :nt], acc[:, :nt])
                    K.mm(ss[:, :nt], ones128[:], sq[:, :nt], start=(c == 0), stop=(c == ncn - 1))
                rs = K.rot("rsb", rsb)
                K.actv(rs[:, :nt], ss[:, :nt], AF.Sqrt, scale=1.0 / dn, bias=epsv[:, 0:1])
                K.recip(rs[:, :nt], rs[:, :nt])
                for c in range(ncn):
                    K.stt("dve", dstT[:, c, t0:t0 + nt], rawv[:, c, :nt], vecs[:, gcol + c:gcol + c + 1], rs[:, :nt], MULT, MULT)
            acc = K.rot("pacc", [PS[0], PS[1], PS[2]])
            for k in range(8):
                K.mm(acc[0:32, :nt], win1[:, k, 640:672], hb1[:, k, :nt], start=(k == 0), stop=(k == 7))
            sq = K.rot("sqb", sqb)
            raw = K.rot("ftmp", ftmp)
            K.actv(sq[0:32, :nt], acc[0:32, :nt], AF.Square)
            K.copy("dve", raw[0:32, :nt], acc[0:32, :nt])
            ss = PS[3]
            K.mm(ss[0:32, :nt], ones128[0:32, 0:32], sq[0:32, :nt])
            rs = K.rot("rsb", rsb)
            K.actv(rs[0:32, :nt], ss[0:32, :nt], AF.Sqrt, scale=1.0 / 32.0, bias=epsv[0:32, 0:1])
            K.recip(rs[0:32, :nt], rs[0:32, :nt])
            if j == 1:
                K.stt("dve", KR[:, t0:t0 + nt], raw[0:32, :nt], vecs[0:32, 43:44], rs[0:32, :nt], MULT, MULT)
            else:
                K.dma(rk[:, :, :nt], ropeKv[:, :, t0:t0 + nt])
                qn = K.rot("sqb", sqb)
                K.stt("dve", qn[0:32, :nt], raw[0:32, :nt], vecs[0:32, 43:44], rs[0:32, :nt], MULT, MULT)
                sw = PS[4]
                K.mm(sw[0:32, :nt], cmat[0:32, 2, 0:32], qn[0:32, :nt])
                t1 = K.rot("ftmp", ftmp)
                t2 = K.rot("ftmp", ftmp)
                K.tt("pool", t1[0:32, :nt], qn[0:32, :nt], rk[:, 0, :nt], MULT)
                K.tt("dve", t2[0:32, :nt], sw[0:32, :nt], rk[:, 1, :nt], MULT)
                K.tt("pool", KR[:, t0:t0 + nt], t1[0:32, :nt], t2[0:32, :nt], ADD)
        d1 = K.dma(xchg_o[0:256, :].rearrange("(c p) t -> p c t", p=128), CKVN)
        d2 = K.dma(xchg_o[256:288, :], KR)
        if mode == "A":
            K.final += [d1, d2]
            K.final.append(K.dma(cqn_o.rearrange("(c p) t -> p c t", p=128), CQN))
            x1v = x1_o.rearrange("(k p) t -> p k t", p=128)
            for k in range(8):
                K.final.append(K.dma(x1v[:, k, :], xT[:, k, 0:T]))

    if B_:
        CQN = arv(9472, [3, T])
        CKVALL = v3(A36[:, 0:2 * NKEY], 2)
        VVh = arv(0, [66, 128])
        KTh = arv(15616, [NKEY], p0=0, p1=96)
        QTh = arv(24064, [T], p0=0, p1=96)
        ropeQ = arv(26112, [2, T], p0=64, p1=96)
        PT2 = [arv(30208 + i * 1024, [1024]) for i in range(2)]
        wuqh = [arv(32256 + i * 288, [3, 96]) for i in range(2)]
        wukvh = [arv(32832 + i * 256, [2, 128]) for i in range(2)]
        woh = [arv(33344 + i * 1024, [1024], p0=0, p1=64) for i in range(2)]
        OTh = [arv(35392 + i * 512, [512], p0=0, p1=64) for i in range(2)]
        if mode == "B":
            x1v = x1_d.rearrange("(k p) t -> p k t", p=128)
            for k in range(8):
                K.dma(xT[:, k, 0:T], x1v[:, k, :])
            K.dma(CQN, cqn_d.rearrange("(c p) t -> p c t", p=128))
            mo = K.sb("mo", [128, 96], F32)
            K.dma(mo[:], mod1_d)
            K.copy("dve", mod[:, 1, :, :], v3(mo[:, :], 48))
            for which in range(2):
                sck = 1 if which == 0 else 4
                nv = vecs[:, (2 + which) * 8:(2 + which) * 8 + 8]
                K.stt("dve", Amat[:, 1, which, 0, :], mod[:, 1, sck * 8:sck * 8 + 8, 0], 1.0, nv, ADD, MULT)
            for c in range(2):
                for kq in range(4):
                    K.dma(CKVALL[:, c, kq * 2112:(kq + 1) * 2112], kvall_d[c * 128:(c + 1) * 128, kq * 2112:(kq + 1) * 2112])
            K.dma(KTh[64:96, :], kvall_d[256:288, :])
        else:
            gat = K.S.add("pool", lambda e: e.collective_compute("AllGather", ALU.bypass, replica_groups=[[0, 1, 2, 3], [4, 5, 6, 7]],
                                                                 ins=[xchg_o[:, :]], outs=[kvg_i[:, :]]),
                          [xchg_o[:, :]], [kvg_i[:, :]], dma=True)
            for rr in range(4):
                for c in range(2):
                    K.dma(CKVALL[:, c, rr * T:(rr + 1) * T], kvg_i[rr * 288 + c * 128: rr * 288 + (c + 1) * 128, 0:T])
                K.dma(KTh[64:96, rr * T:(rr + 1) * T], kvg_i[rr * 288 + 256: rr * 288 + 288, 0:T])
            for c in range(2):
                K.dma(CKVALL[:, c, 4 * T:NKEY], xchg_o[c * 128:(c + 1) * 128, T:T + CT])
            K.dma(KTh[64:96, 4 * T:NKEY], xchg_o[256:288, T:T + CT])
        K.dma(ropeQ, ropeQ_d.rearrange("c p t -> p c t"))
        K.memset("pool", VVh[:, :, 64:128], 1.0)
        wuqv = wuq_d.rearrange("(c p) f -> p c f", p=128)
        wukvv = wukv_d.rearrange("(c p) f -> p c f", p=128)
        MISC = [PS[6], PS[7]]
        KBLK = [(kb * 512, 512) for kb in range(16)] + [(8192, 256)]
        def load_head(h):
            K.dma(wuqh[h % 2], wuqv[:, :, h * 96:(h + 1) * 96], eng="pool")
            K.dma(wukvh[h % 2], wukvv[:, :, h * 128:(h + 1) * 128], eng="pool")
            K.dma(woh[h % 2], owout_d[h * 64:(h + 1) * 64, :], eng="pool")
        load_head(0)
        for h in range(16):
            wq = wuqh[h % 2]
            wkv = wukvh[h % 2]
            wo = woh[h % 2]
            if h + 1 < 16:
                load_head(h + 1)
            for (k0, nk) in KBLK:
                acc = K.rot("misc", MISC)
                for c in range(2):
                    K.mm(acc[0:64, :nk], wkv[:, c, 0:64], CKVALL[:, c, k0:k0 + nk], start=(c == 0), stop=(c == 1))
                raw = K.rot("ftmp", ftmp)
                K.copy("dve", raw[0:64, :nk], acc[0:64, :nk])
                sq = K.rot("sqb", sqb)
                K.tt("pool", sq[0:64, :nk], raw[0:64, :nk], raw[0:64, :nk], MULT)
                ss = K.rot("misc", MISC)
                K.mm(ss[0:64, :nk], ones128[0:64, 0:64], sq[0:64, :nk])
                rs = K.rot("rsb", rsb)
                K.actv(rs[0:64, :nk], ss[0:64, :nk], AF.Sqrt, scale=1.0 / 64.0, bias=epsv[0:64, 0:1])
                K.recip(rs[0:64, :nk], rs[0:64, :nk])
                K.stt("dve", KTh[0:64, k0:k0 + nk], raw[0:64, :nk], vecs[0:64, 42:43], rs[0:64, :nk], MULT, MULT)
            for c4 in range(0, 66, 4):
                n4 = min(4, 66 - c4)
                acc = K.rot("misc", MISC)
                for i in range(n4):
                    for c in range(2):
                        K.mm(acc[:, i * 64:(i + 1) * 64], CKVALL[:, c, (c4 + i) * 128:(c4 + i + 1) * 128], wkv[:, c, 64:128],
                             start=(c == 0), stop=(c == 1), skip=True)
                K.copy("dve", VVh[:, c4:c4 + n4, 0:64], v3(acc[:, 0:n4 * 64], n4))
            for qb in range(4):
                acc = K.rot("misc", MISC)
                for c in range(3):
                    K.mm(acc[0:96, :], wq[:, c, :], CQN[:, c, qb * 512:(qb + 1) * 512], start=(c == 0), stop=(c == 2))
                raw = K.rot("ftmp", ftmp)
                K.copy("dve", raw[0:96, :], acc[0:96, :])
                sq = K.rot("sqb", sqb)
                K.tt("pool", sq[0:96, :], raw[0:96, :], raw[0:96, :], MULT)
                ss = K.rot("misc", MISC)
                K.mm(ss[0:96, :], blk96[0:96, 0:96], sq[0:96, :])
                rs = K.rot("rsb", rsb)
                K.actv(rs[0:96, :], ss[0:96, :], AF.Sqrt, scale=vecs[0:96, 44:45], bias=epsv[0:96, 0:1])
                K.recip(rs[0:96, :], rs[0:96, :])
                qd = QTh[0:96, qb * 512:(qb + 1) * 512]
                K.stt("dve", qd, raw[0:96, :], gsc[0:96, 2:3], rs[0:96, :], MULT, MULT)
                sw = K.rot("misc", MISC)
                qr = QTh[64:96, qb * 512:(qb + 1) * 512]
                K.mm(sw[0:32, :], cmat[64:96, 3, 0:32], qr)
                t1 = K.rot("ftmp", ftmp)
                t2 = K.rot("ftmp", ftmp)
                K.tt("pool", t1[64:96, :], qr, ropeQ[:, 0, qb * 512:(qb + 1) * 512], MULT)
                K.tt("dve", t2[64:96, :], sw[0:32, :], ropeQ[:, 1, qb * 512:(qb + 1) * 512], MULT)
                K.tt("pool", qr, t1[64:96, :], t2[64:96, :], ADD)
            for qb in range(4):
                O = K.rot("Ob", [PS[4], PS[5]])
                cur = {}

                def issue(i, qb=qb, cur=cur):
                    S2 = K.rot("S2", [PD[0], PD[1]])
                    cur[i] = S2
                    for ii in range(2):
                        c = 2 * i + ii
                        K.mm(S2[:, ii, :], KTh[0:96, c * 128:(c + 1) * 128], QTh[0:96, qb * 512:(qb + 1) * 512])

                def consume(i, cur=cur, O=O):
                    S2 = cur[i]
                    pt = K.rot("PT2", PT2)
                    K.actv(v3(pt, 2), S2[:, :, :], AF.Exp)
                    for ii in range(2):
                        c = 2 * i + ii
                        K.mm(O, VVh[:, c, :], pt[:, ii * 512:(ii + 1) * 512], start=(c == 0), stop=(c == 65))

                pipeline(33, issue, consume)
                rec = K.rot("rsb", rsb)
                K.recip(rec[64:128, :], O[64:128, :])
                ot = K.rot("OTh", OTh)
                K.tt("dve", ot, O[0:64, :], rec[64:128, :], MULT)
                for m in range(8):
                    Y = K.rot("misc", MISC)
                    K.mm(Y[:, :], wo[:, m * 128:(m + 1) * 128], ot)
                    K.stt("dve", xT[:, m, qb * 512:(qb + 1) * 512], Y[:, :], mod_ap(1, 2, m, 0), xT[:, m, qb * 512:(qb + 1) * 512], MULT, ADD)
        for (t0, nt, j) in OWN_BLOCKS:
            norm_block(xT[:, :, t0:t0 + nt], nt, 1, 1, 0, A36v[:, :, t0:t0 + nt])
        mlp(1, OWN_BLOCKS, A36v)
        ov = out_o.rearrange("(k p) t -> p k t", p=128)
        for k in range(8):
            K.final.append(K.dma(ov[:, k, :], xT[:, k, 0:T]))

    return finish()


_BF = ml_dtypes.bfloat16


def _fm(v):
    return np.ascontiguousarray(np.asarray(v, np.float32).reshape(-1, 128).T)


def _rope_tab(pos, hw):
    inv = (np.float32(10000.0) ** (-np.arange(hw, dtype=np.float32) / np.float32(hw))).astype(np.float32)
    ang = pos.astype(np.float32)[None, :] * inv[:, None]
    return np.cos(ang).astype(np.float32), np.sin(ang).astype(np.float32)


def _perm_signed(blocks, n):
    P = np.zeros((n, n), np.float32)
    for (b, hw) in blocks:
        for i in range(hw):
            P[b + hw + i, b + i] = -1.0
            P[b + i, b + hw + i] = 1.0
    return P


def _host_common(inp):
    f32 = np.float32
    vecs = np.zeros((128, 48), f32)
    vecs[:, 0:8] = _fm(inp["norm_mix"][0])
    vecs[:, 8:16] = _fm(inp["norm_mlp"][0])
    vecs[:, 16:24] = _fm(inp["norm_mix"][1])
    vecs[:, 24:32] = _fm(inp["norm_mlp"][1])
    rep64 = lambda v: np.tile(np.asarray(v, f32).reshape(64), 2)
    vecs[:, 32] = rep64(inp["a_q_norm"][0])
    vecs[:, 33] = rep64(inp["a_k_norm"][0])
    vecs[:, 34] = rep64(inp["b_q_norm"][0])
    vecs[:, 35] = rep64(inp["b_k_norm"][0])
    vecs[:, 36:39] = _fm(inp["o_qa_norm"][0])
    vecs[:, 39:41] = _fm(inp["o_kva_norm"][0])
    vecs[0:64, 41] = inp["o_qn_nope"][0]
    vecs[64:96, 41] = inp["o_qn_rope"][0]
    vecs[:, 42] = rep64(inp["o_kn_nope"][0])
    vecs[:, 43] = np.tile(np.asarray(inp["o_kn_rope"][0], f32), 4)
    vecs[0:64, 44] = 1.0 / 64.0
    vecs[64:128, 44] = 1.0 / 32.0
    cmat = np.zeros((4, 128, 128), f32)
    cmat[0] = np.eye(128, dtype=f32)
    cmat[1] = _perm_signed([(0, 16), (32, 16), (64, 16), (96, 16)], 128)
    p32 = _perm_signed([(0, 8), (16, 8)], 32)
    cmat[2, 0:32, 0:32] = p32
    cmat[3, 64:96, 0:32] = p32
    return dict(vecs=vecs, cmat=cmat.astype(_BF),
                mlp_w1=np.ascontiguousarray(inp["mlp_w1"], f32), mlp_w2=np.ascontiguousarray(inp["mlp_w2"], f32))


def _rope32_tables(tok):
    row, col = tok // 64, tok % 64
    cr, sr = _rope_tab(row, 8)
    cc, sc = _rope_tab(col, 8)
    cos = np.concatenate([cr, cr, cc, cc], 0)
    sin = np.concatenate([sr, sr, sc, sc], 0)
    return np.stack([cos, sin], 0).astype(np.float32)


def _host_A(inp, core):
    f32 = np.float32
    b, r = core // 4, core % 4
    x = inp["x"][b]
    d = {}
    d["xT"] = np.ascontiguousarray(x[r * T:(r + 1) * T].T, f32)
    xh = np.zeros((1024, 2 * HAL), f32)
    if r > 0:
        xh[:, 0:HAL] = x[r * T - HAL:r * T].T
    if r < 3:
        xh[:, HAL:] = x[(r + 1) * T:(r + 1) * T + HAL].T
    d["xhT"] = xh
    d["ctxT"] = np.ascontiguousarray(inp["ctx"][b].T, f32)
    cond = np.zeros((128, 8, 2), f32)
    cond[:, :, 0] = _fm(inp["c"][b])
    cond[:, :, 1] = _fm(inp["c_ctx"])
    d["condT"] = cond.reshape(128, 16)
    d["ada_w"] = np.ascontiguousarray(inp["ada_w"], f32)
    d["adab"] = np.concatenate([_fm(inp["ada_b"][0]), _fm(inp["ada_b"][1])], 1)
    d["e_w_in"] = np.ascontiguousarray(inp["e_w_in"][0], f32)
    d["e_w_out"] = np.ascontiguousarray(inp["e_w_out"][0], f32)
    d["o_w_in"] = np.ascontiguousarray(inp["o_w_in"][0], f32)
    tok = np.arange(r * T - HAL, (r + 1) * T + HAL)
    tokc = np.clip(tok, 0, 8191)
    cr, sr = _rope_tab(tokc // 64, 16)
    cc, sc = _rope_tab(tokc % 64, 16)
    cos64 = np.concatenate([cr, cr, cc, cc], 0)
    sin64 = np.concatenate([sr, sr, sc, sc], 0)
    d["ropeA"] = np.stack([np.tile(cos64, (2, 1)), np.tile(sin64, (2, 1))], 0).astype(f32)
    d["ropeK"] = _rope32_tables(np.arange(r * T, (r + 1) * T))
    kk = np.arange(128)[:, None]
    qq = np.arange(128)[None, :]
    prev = np.where(kk >= qq, 0.0, NEG).astype(f32)
    nxt = np.where(kk <= qq, 0.0, NEG).astype(f32)
    allneg = np.full((128, 128), NEG, f32)
    var = [prev, nxt, allneg if r == 0 else prev, allneg if r == 3 else nxt]
    d["amask"] = np.stack([np.tile(v, (1, 4)) for v in var], 1).astype(_BF)
    rpb = np.asarray(inp["b_rpb"][0], f32)
    bb = np.full((5, 128, 6, 8, 128), NEG, f32)
    k_i = np.arange(128)
    q_i = np.arange(128)
    for vi, jt in enumerate([0, 1, 5, 14, 15]):
        if jt == 0:
            ms = list(range(-2, 4))
        elif jt == 15:
            ms = list(range(12, 18))
        else:
            ms = list(range(jt - 2, jt + 3))
        rr = r if vi != 2 else 1
        gq = 32 * rr + 2 * jt + q_i // 64
        cq = q_i % 64
        start = np.clip(gq - 4, 0, 120)
        c0 = np.clip(cq - 8, 0, 48)
        for ci, m in enumerate(ms):
            gk = 32 * rr + 2 * m + k_i // 64
            ck = k_i % 64
            valid = ((gk[:, None] >= 0) & (gk[:, None] < 128) & (gk[:, None] >= start[None, :]) & (gk[:, None] < start[None, :] + 8)
                     & (ck[:, None] >= c0[None, :]) & (ck[:, None] < c0[None, :] + 16))
            dri = np.clip(gk[:, None] - gq[None, :] + 7, 0, 14)
            dci = np.clip(ck[:, None] - cq[None, :], -15, 15) + 15
            g = rpb[:, dri, dci]
            bb[vi, :, ci, :, :] = np.where(valid[None], g, NEG).transpose(1, 0, 2)
    bb = bb[:, :, :, [0, 2, 1, 3, 4, 6, 5, 7], :]
    d["bbias"] = np.ascontiguousarray(bb).reshape(5, 128, 6, 1024).astype(_BF)
    d["sink"] = np.tile(np.asarray(inp["a_sink"][0], f32)[None, :], (128, 1))
    return d


def _host_B(inp, core):
    f32 = np.float32
    r = core % 4
    d = {}
    d["o_w_uq"] = np.ascontiguousarray(inp["o_w_uq"][0], f32)
    d["o_w_ukv"] = np.ascontiguousarray(inp["o_w_ukv"][0], f32)
    d["o_w_out"] = np.ascontiguousarray(inp["o_w_out"][0], f32)
    d["ropeQ"] = _rope32_tables(np.arange(r * T, (r + 1) * T)).astype(_BF)
    return d


_NC_CACHE = {}


def _get_nc(mode):
    if mode not in _NC_CACHE:
        _NC_CACHE[mode] = build(mode).nc
    return _NC_CACHE[mode]


FUSED = False


def kernel(**inputs):
    inp = {k: np.asarray(v) for k, v in inputs.items()}
    common = _host_common(inp)
    out = np.empty((2, 8192, 1024), np.float32)
    if FUSED:
        maps = []
        for c in range(NCORES):
            m = dict(common)
            m.update(_host_A(inp, c))
            m.update(_host_B(inp, c))
            maps.append(m)
        res = run_bass_kernel_spmd(_get_nc("F"), maps, core_ids=list(range(NCORES)))
        for c in range(NCORES):
            out[c // 4, (c % 4) * T:(c % 4 + 1) * T, :] = np.asarray(res.results[c]["outT"]).T
        return out
    mapsA = []
    for c in range(NCORES):
        m = dict(common)
        m.update(_host_A(inp, c))
        mapsA.append(m)
    resA = run_bass_kernel_spmd(_get_nc("A"), mapsA, core_ids=list(range(NCORES)))
    ra = resA.results
    mapsB = []
    for c in range(NCORES):
        b = c // 4
        m = dict(common)
        m.update(_host_B(inp, c))
        m["x1T"] = np.asarray(ra[c]["x1T"])
        m["cqn"] = np.asarray(ra[c]["cqn"])
        m["mod1"] = np.asarray(ra[c]["mod1"])
        kv = np.concatenate([np.asarray(ra[4 * b + rr]["xchg"])[:, 0:T] for rr in range(4)] + [np.asarray(ra[c]["xchg"])[:, T:T + CT]], axis=1)
        m["kvall"] = np.ascontiguousarray(kv)
        mapsB.append(m)
    resB = run_bass_kernel_spmd(_get_nc("B"), mapsB, core_ids=list(range(NCORES)))
    for c in range(NCORES):
        out[c // 4, (c % 4) * T:(c % 4 + 1) * T, :] = np.asarray(resB.results[c]["outT"]).T
    return out
```

```python
import contextlib
import numpy as np
import ml_dtypes
import concourse.bass as bass
import concourse.mybir as mybir
from concourse.bass_utils import run_bass_kernel_spmd

F32 = mybir.dt.float32
BF16 = mybir.dt.bfloat16
AF = mybir.ActivationFunctionType
ALU = mybir.AluOpType

NCORES = 8
T = 2048
HAL = 256
CT = 256
E = HAL + T + HAL + CT
NQ = T + CT
NKEY = 8192 + CT
EPS = 1e-6
NEG = -30000.0


def _region(ap):
    name = ap.name
    space = str(ap.space)
    dims = ap.ap
    off = int(ap.offset)
    if space == "DRAM":
        lo = off
        hi = off + sum(int(s) * (int(c) - 1) for s, c in dims if int(s) > 0) + 1
        return (name, "DRAM", 0, 1, lo, hi)
    if space == "PSUM":
        fszp = 1
        for d in ap.tensor.shape[1:]:
            fszp *= int(d)
        g0 = off % fszp
        g1 = g0 + sum(int(s) * (int(c) - 1) for s, c in dims[1:] if int(s) > 0) + 1
        return (name, "PSUM", 0, 128, g0 // 512, (g1 - 1) // 512 + 1)
    pstep, pcnt = int(dims[0][0]), int(dims[0][1])
    fsz = 1
    for d in ap.tensor.shape[1:]:
        fsz *= int(d)
    p0 = off // fsz
    f0 = off % fsz
    p1 = p0 + 1 if pstep == 0 else p0 + (pstep // fsz) * (pcnt - 1) + 1
    f1 = f0 + sum(int(s) * (int(c) - 1) for s, c in dims[1:] if int(s) > 0) + 1
    return (name, "SB", p0, p1, f0, f1)


def _overlap(a, b):
    return a[2] < b[3] and b[2] < a[3] and a[4] < b[5] and b[4] < a[5]


def _covers(a, b):
    return a[2] <= b[2] and a[3] >= b[3] and a[4] <= b[4] and a[5] >= b[5]


class Sched:
    ENGS = ("pe", "act", "dve", "pool", "sp")

    def __init__(self, nc, n_dma_sems=12):
        self.nc = nc
        self.ops = []
        self.track = {}
        self.n_dma_sems = n_dma_sems

    def add(self, eng, fn, reads=(), writes=(), dma=False):
        idx = len(self.ops)
        rr = list(dict.fromkeys(_region(a) for a in reads))
        ww = list(dict.fromkeys(_region(a) for a in writes))
        deps = set()
        for r in rr:
            lst = self.track.setdefault(r[0], [])
            psum = r[1] == "PSUM"
            for (box, oi, kind) in lst:
                if (kind == "w" or psum) and _overlap(box, r):
                    deps.add(oi)
        for w in ww:
            lst = self.track.setdefault(w[0], [])
            for (box, oi, kind) in lst:
                if _overlap(box, w):
                    deps.add(oi)
        for r in rr:
            lst = self.track[r[0]]
            if r[1] == "PSUM":
                lst[:] = [t for t in lst if not _covers(r, t[0])]
                lst.append((r, idx, "w"))
            else:
                lst[:] = [t for t in lst if not (t[2] == "r" and t[1] < idx and self.ops[t[1]]["eng"] == eng
                                                 and not self.ops[t[1]]["dma"] and not dma and _covers(r, t[0]))]
                lst.append((r, idx, "r"))
        for w in ww:
            lst = self.track[w[0]]
            lst[:] = [t for t in lst if not _covers(w, t[0])]
            lst.append((w, idx, "w"))
        deps.discard(idx)
        self.ops.append(dict(eng=eng, fn=fn, deps=deps, dma=dma, sig=False, rr=rr, ww=ww))
        return idx

    def dma(self, out, in_, eng="sp"):
        return self.add(eng, lambda e: e.dma_start(out=out, in_=in_), [in_], [out], dma=True)

    def emit(self, final_wait_ops=()):
        nc = self.nc
        ops = self.ops

        def needs_wait(x, y):
            X, Y = ops[x], ops[y]
            if Y["dma"] or X["dma"]:
                return True
            if X["eng"] == Y["eng"]:
                if X["eng"] == "pe":
                    return False
                for w in Y["ww"]:
                    for r in X["rr"]:
                        if w[0] == r[0] and _overlap(w, r):
                            return True
                return False
            return True

        for i, X in enumerate(ops):
            X["wdeps"] = [y for y in X["deps"] if needs_wait(i, y)]
            for y in X["wdeps"]:
                ops[y]["sig"] = True
        for i in final_wait_ops:
            ops[i]["sig"] = True
        cnt = {e: 0 for e in self.ENGS}
        dma_k = {e: 0 for e in self.ENGS}
        dma_semcnt = {}
        for X in ops:
            if X["dma"]:
                q = X["eng"]
                k = dma_k[q]
                dma_k[q] += 1
                s = (q, k % self.n_dma_sems)
                dma_semcnt[s] = dma_semcnt.get(s, 0) + 1
                X["dsem"] = s
                X["dval"] = 16 * dma_semcnt[s]
            elif X["sig"]:
                cnt[X["eng"]] += 1
                X["cnt"] = cnt[X["eng"]]
        with contextlib.ExitStack() as st:
            sems = {e: st.enter_context(nc.semaphore("s_" + e)) for e in ("pe", "act", "dve", "pool")}
            dsems = {}
            for q in self.ENGS:
                for j in range(min(self.n_dma_sems, dma_k[q])):
                    dsems[(q, j)] = st.enter_context(nc.semaphore("d_%s_%d" % (q, j)))
            block = st.enter_context(nc.Block())
            per_eng = {e: [i for i, X in enumerate(ops) if X["eng"] == e] for e in self.ENGS}

            def run_stream(ename, e):
                known = {}

                def wait(key, semh, val):
                    if known.get(key, 0) >= val:
                        return
                    e.wait_ge(semh, val)
                    known[key] = val

                def wait_op(Y):
                    if Y["dma"]:
                        wait(Y["dsem"], dsems[Y["dsem"]], Y["dval"])
                    else:
                        wait(Y["eng"], sems[Y["eng"]], Y["cnt"])

                for i in per_eng[ename]:
                    X = ops[i]
                    for y in sorted(X["wdeps"]):
                        wait_op(ops[y])
                    if X["dma"]:
                        if X["dval"] > 16:
                            wait(X["dsem"], dsems[X["dsem"]], X["dval"] - 16)
                        X["fn"](e).then_inc(dsems[X["dsem"]], 16)
                    else:
                        ins = X["fn"](e)
                        if X["sig"]:
                            ins.then_inc(sems[ename], 1)
                if ename == "sp":
                    for i in final_wait_ops:
                        wait_op(ops[i])

            @block.tensor
            def _(e):
                run_stream("pe", e)

            @block.scalar
            def _(e):
                run_stream("act", e)

            @block.vector
            def _(e):
                run_stream("dve", e)

            @block.gpsimd
            def _(e):
                run_stream("pool", e)

            @block.sync
            def _(e):
                run_stream("sp", e)
        self.stats = dict(n_ops=len(ops), cnt=cnt, dma=dma_k)


class KB:
    def __init__(self, mode):
        self.mode = mode
        self.nc = bass.Bass("TRN2", target_bir_lowering=False)
        self.S = Sched(self.nc)
        self.st = contextlib.ExitStack()
        self.final = []
        self._rot = {}

    def din(self, name, shape, dt=F32):
        return self.nc.dram_tensor(name, list(shape), dt, kind="ExternalInput").ap()

    def dout(self, name, shape, dt=F32):
        return self.nc.dram_tensor(name, list(shape), dt, kind="ExternalOutput").ap()

    def dint(self, name, shape, dt=F32):
        return self.nc.dram_tensor(name, list(shape), dt, kind="Internal").ap()

    def sb(self, name, shape, dt):
        return self.st.enter_context(self.nc.sbuf_tensor(name, list(shape), dt))

    def rot(self, key, lst):
        i = self._rot.get(key, 0)
        self._rot[key] = i + 1
        return lst[i % len(lst)]

    def mm(self, out, lhsT, rhs, start=True, stop=True, skip=False):
        kw = dict(skip_group_check=True) if skip else {}
        return self.S.add("pe", lambda e: e.matmul(out, lhsT=lhsT, rhs=rhs, start=start, stop=stop, **kw),
                          [lhsT, rhs], [out])

    def actv(self, out, in_, func, scale=1.0, bias=None):
        reads = [in_]
        kw = {}
        if isinstance(scale, float) or isinstance(scale, int):
            kw["scale"] = float(scale)
        else:
            kw["scale"] = scale
            reads.append(scale)
        if bias is not None:
            kw["bias"] = bias
            if not isinstance(bias, float):
                reads.append(bias)
        return self.S.add("act", lambda e: e.activation(out=out, in_=in_, func=func, **kw), reads, [out])

    def tt(self, eng, out, in0, in1, op):
        return self.S.add(eng, lambda e: e.tensor_tensor(out=out, in0=in0, in1=in1, op=op), [in0, in1], [out])

    def stt(self, eng, out, in0, scalar, in1, op0, op1):
        reads = [in0, in1]
        if not isinstance(scalar, float):
            reads.append(scalar)
        return self.S.add(eng, lambda e: e.scalar_tensor_tensor(out=out, in0=in0, scalar=scalar, in1=in1, op0=op0, op1=op1),
                          reads, [out])

    def ts(self, eng, out, in0, s1, op0, s2=None, op1=None):
        reads = [in0]
        if not isinstance(s1, float):
            reads.append(s1)
        if s2 is not None and not isinstance(s2, float):
            reads.append(s2)
        if op1 is None:
            return self.S.add(eng, lambda e: e.tensor_scalar(out=out, in0=in0, scalar1=s1, scalar2=None, op0=op0), reads, [out])
        return self.S.add(eng, lambda e: e.tensor_scalar(out=out, in0=in0, scalar1=s1, scalar2=s2, op0=op0, op1=op1), reads, [out])

    def copy(self, eng, out, in_):
        if eng == "act":
            return self.S.add("act", lambda e: e.activation(out=out, in_=in_, func=AF.Copy), [in_], [out])
        return self.S.add(eng, lambda e: e.tensor_copy(out=out, in_=in_), [in_], [out])

    def recip(self, out, in_):
        return self.S.add("dve", lambda e: e.reciprocal(out=out, in_=in_), [in_], [out])

    def memset(self, eng, out, val):
        return self.S.add(eng, lambda e: e.memset(out, val), [], [out])

    def dma(self, out, in_, eng="sp"):
        return self.S.dma(out, in_, eng)


def v3(ap2, a):
    return ap2.rearrange("p (a b) -> p a b", a=a)


MULT, ADD = ALU.mult, ALU.add
NA = 36480


def build(mode, stop=0):
    K = KB(mode)
    nc, S = K.nc, K.S
    A_ = mode in ("A", "F")
    B_ = mode in ("B", "F")

    vecs_d = K.din("vecs", [128, 48])
    cmat_d = K.din("cmat", [4, 128, 128], BF16)
    w1_d = K.din("mlp_w1", [2, 1024, 4096])
    w2_d = K.din("mlp_w2", [2, 4096, 1024])
    if A_:
        xT_d = K.din("xT", [1024, T])
        xh_d = K.din("xhT", [1024, 2 * HAL])
        ctx_d = K.din("ctxT", [1024, CT])
        cond_d = K.din("condT", [128, 16])
        adaw_d = K.din("ada_w", [2, 1024, 6144])
        adab_d = K.din("adab", [128, 96])
        ewin_d = K.din("e_w_in", [1024, 2304])
        ewout_d = K.din("e_w_out", [1024, 1024])
        owin_d = K.din("o_w_in", [1024, 672])
        ropeA_d = K.din("ropeA", [2, 128, 2 * HAL + T])
        ropeK_d = K.din("ropeK", [2, 32, T])
        amask_d = K.din("amask", [128, 4, 512], BF16)
        bbias_d = K.din("bbias", [5, 128, 6, 1024], BF16)
        sink_d = K.din("sink", [128, 8])
    if B_:
        wuq_d = K.din("o_w_uq", [384, 1536])
        wukv_d = K.din("o_w_ukv", [256, 2048])
        owout_d = K.din("o_w_out", [1024, 1024])
        ropeQ_d = K.din("ropeQ", [2, 32, T], BF16)
        out_o = K.dout("outT", [1024, T])
    if mode == "A":
        x1_o = K.dout("x1T", [1024, T])
        cqn_o = K.dout("cqn", [384, T], BF16)
        xchg_o = K.dout("xchg", [288, T + CT], BF16)
        mod1_o = K.dout("mod1", [128, 96])
    if mode == "B":
        x1_d = K.din("x1T", [1024, T])
        cqn_d = K.din("cqn", [384, T], BF16)
        kvall_d = K.din("kvall", [288, NKEY], BF16)
        mod1_d = K.din("mod1", [128, 96])
    if mode == "F":
        xchg_o = K.dint("xchg_i", [288, T + CT], BF16)
        kvg_i = K.dint("kvg_i", [4 * 288, T + CT], BF16)

    if stop:
        dbg_o = K.dout("dbg", [128, NA], BF16)
        dbg36_o = K.dout("dbg36", [128, 8 * NQ], BF16)
        dbgx_o = K.dout("dbgx", [128, 8 * NQ])

    def finish():
        if stop:
            K.final.append(K.dma(dbg_o, AR[:, :]))
            K.final.append(K.dma(dbg36_o, A36[:, :]))
            K.final.append(K.dma(dbgx_o, xT[:, :, :].rearrange("p a b -> p (a b)")))
        S.emit(final_wait_ops=K.final)
        return K

    xT = K.sb("xTs", [128, 8, NQ], F32)
    A36 = K.sb("A36", [128, 8 * NQ], BF16)
    A36v = v3(A36[:, :], 8)
    mod = K.sb("mod", [128, 2, 48, 2], F32)
    Amat = K.sb("Amat", [128, 2, 2, 2, 8], F32)
    vecs = K.sb("vecs_s", [128, 48], F32)
    gsc = K.sb("gsc", [128, 4], F32)
    ones128 = K.sb("ones128", [128, 128], BF16)
    blk = K.sb("blk", [128, 128], BF16)
    blk96 = K.sb("blk96", [128, 128], BF16)
    cmat = K.sb("cmat_s", [128, 4, 128], BF16)
    ident = cmat[:, 0, :]
    sqb = [K.sb("sqb%d" % i, [128, 512], BF16) for i in range(2)]
    ftmp = [K.sb("ftmp%d" % i, [128, 512], F32) for i in range(3)]
    rsb = [K.sb("rsb%d" % i, [128, 512], F32) for i in range(2)]
    AR = K.sb("AR", [128, NA], BF16)
    ARF = K.sb("ARF", [128, 3072], F32)
    PD = [K.st.enter_context(nc.psum_tensor("PD%d" % i, [128, 2, 512], F32)) for i in range(4)]
    PS = [PD[i // 2][:, i % 2, :] for i in range(8)]

    def pipeline(n, issue, consume, look=1):
        for i in range(min(look, n)):
            issue(i)
        for i in range(n):
            if i + look < n:
                issue(i + look)
            consume(i)

    def arv(off, dims, p0=0, p1=128, t=AR):
        n = 1
        for d in dims:
            n *= d
        ap = t[p0:p1, off:off + n]
        if len(dims) == 2:
            ap = ap.rearrange("p (a b) -> p a b", a=dims[0])
        elif len(dims) == 3:
            ap = ap.rearrange("p (a b c) -> p a b c", a=dims[0], b=dims[1])
        return ap

    K.dma(vecs[:], vecs_d)
    K.dma(cmat[:], cmat_d.rearrange("c p q -> p c q"))
    epsv = K.sb("epsv", [128, 1], F32)
    K.memset("dve", epsv[:], EPS)
    K.memset("dve", ones128[:], 1.0)
    K.memset("dve", blk[:], 0.0)
    K.memset("dve", blk[0:64, 0:64], 1.0)
    K.memset("dve", blk[64:128, 64:128], 1.0)
    K.memset("dve", blk96[:], 0.0)
    K.memset("dve", blk96[0:64, 0:64], 1.0)
    K.memset("dve", blk96[64:96, 64:96], 1.0)
    K.ts("dve", gsc[:, 0:1], vecs[:, 32:33], 0.125, MULT)
    K.ts("dve", gsc[:, 1:2], vecs[:, 34:35], 0.125, MULT)
    K.ts("dve", gsc[:, 2:3], vecs[:, 41:42], float(96 ** -0.5), MULT)

    def mod_ap(l, kind, m, j):
        return mod[:, l, kind * 8 + m, j:j + 1]

    def norm_block(src, nt, l, which, j, dst):
        ss = K.rot("ssb", [PS[6], PS[7]])
        for k in range(8):
            sq = K.rot("sqb", sqb)
            K.tt("pool", sq[:, :nt], src[:, k, :], src[:, k, :], MULT)
            K.mm(ss[:, :nt], ones128[:], sq[:, :nt], start=(k == 0), stop=(k == 7))
        rs = K.rot("rsb", rsb)
        K.actv(rs[:, :nt], ss[:, :nt], AF.Sqrt, scale=1.0 / 1024.0, bias=epsv[:, 0:1])
        K.recip(rs[:, :nt], rs[:, :nt])
        shift_kind = 0 if which == 0 else 3
        for k in range(8):
            t = K.rot("ftmp", ftmp)
            K.stt("dve", t[:, :nt], src[:, k, :], Amat[:, l, which, j, k:k + 1], rs[:, :nt], MULT, MULT)
            K.actv(dst[:, k, :], t[:, :nt], AF.Identity, scale=1.0, bias=mod_ap(l, shift_kind, k, j))

    def mlp(l, nblocks, h2T):
        W1v = w1_d[l].rearrange("(k p) f -> p k f", p=128)
        W2v = w2_d[l].rearrange("(k p) f -> p k f", p=128)
        w1b = [arv(0, [8, 512]), arv(4096, [8, 512])]
        w2b = [arv(8192, [4, 1024]), arv(12288, [4, 1024])]
        ub = [arv(16384, [4, 512]), arv(18432, [4, 512])]
        rb = [arv(20480 + i * 512, [512]) for i in range(2)]
        def load_e8(e8):
            K.dma(w1b[e8 % 2], W1v[:, :, e8 * 512:(e8 + 1) * 512], eng="pool")
            K.dma(w2b[e8 % 2], W2v[:, e8 * 4:(e8 + 1) * 4, :], eng="pool")
        load_e8(0)
        for e8 in range(8):
            w1 = w1b[e8 % 2]
            w2 = w2b[e8 % 2]
            if e8 + 1 < 8:
                load_e8(e8 + 1)
            for (t0, nt, j) in nblocks:
                u = K.rot("ub", ub)
                for fc in range(4):
                    acc = K.rot("mlpacc", [PS[0], PS[1], PS[2]])
                    for k in range(8):
                        K.mm(acc[:, :nt], w1[:, k, fc * 128:(fc + 1) * 128], h2T[:, k, t0:t0 + nt], start=(k == 0), stop=(k == 7))
                    r = K.rot("rb", rb)
                    K.actv(r[:, :nt], acc[:, :nt], AF.Relu)
                    K.tt("pool", u[:, fc, :nt], r[:, :nt], r[:, :nt], MULT)
                for m in range(8):
                    acc = K.rot("mlpacc2", [PS[3], PS[4], PS[5]])
                    for fc in range(4):
                        K.mm(acc[:, :nt], w2[:, fc, m * 128:(m + 1) * 128], u[:, fc, :nt], start=(fc == 0), stop=(fc == 3))
                    K.stt("dve", xT[:, m, t0:t0 + nt], acc[:, :nt], mod_ap(l, 5, m, j), xT[:, m, t0:t0 + nt], MULT, ADD)

    OWN_BLOCKS = [(b * 512, 512, 0) for b in range(4)]
    CTX_BLOCK = (T, CT, 1)

    if A_:
        xTv = xT_d.rearrange("(k p) t -> p k t", p=128)
        for k in range(8):
            K.dma(xT[:, k, 0:T], xTv[:, k, :])
        K.dma(xT[:, :, T:NQ], ctx_d.rearrange("(k p) t -> p k t", p=128))
        cond = K.sb("cond", [128, 16], F32)
        silu = K.sb("silu", [128, 16], BF16)
        adab = K.sb("adab_s", [128, 96], F32)
        sinkx = K.sb("sinkx", [128, 8], F32)
        K.dma(cond[:], cond_d)
        K.dma(adab[:], adab_d)
        K.dma(sinkx[:], sink_d)
        K.actv(silu[:], cond[:], AF.Silu)
        K.actv(sinkx[:], sinkx[:], AF.Exp)
        siluv = v3(silu[:, :], 8)
        for l in range(2):
            Wv = adaw_d[l].rearrange("(k p) f -> p k f", p=128)
            acc = PS[0] if l == 0 else PS[1]
            adawb = [arv(0, [8, 1024]), arv(8192, [8, 1024])]
            if l == 0:
                K.dma(adawb[0], Wv[:, :, 0:1024], eng="pool")
            for piece in range(6):
                wb = adawb[(l * 6 + piece) % 2]
                nxt = l * 6 + piece + 1
                if nxt < 12:
                    Wn = adaw_d[nxt // 6].rearrange("(k p) f -> p k f", p=128)
                    K.dma(adawb[nxt % 2], Wn[:, :, (nxt % 6) * 1024:(nxt % 6 + 1) * 1024], eng="pool")
                for m in range(8):
                    f = piece * 8 + m
                    for k in range(8):
                        K.mm(acc[:, 2 * f:2 * f + 2], wb[:, k, m * 128:(m + 1) * 128], siluv[:, k, :], start=(k == 0), stop=(k == 7), skip=True)
            K.tt("dve", mod[:, l, :, :], v3(acc[:, 0:96], 48), adab[:, l * 48:(l + 1) * 48].unsqueeze(2).broadcast_to([128, 48, 2]), ADD)
            for which in range(2):
                sck = 1 if which == 0 else 4
                nv = vecs[:, (l * 2 + which) * 8:(l * 2 + which) * 8 + 8]
                for j in range(2):
                    K.stt("dve", Amat[:, l, which, j, :], mod[:, l, sck * 8:sck * 8 + 8, j], 1.0, nv, ADD, MULT)
        if mode == "A":
            mo = K.sb("mo", [128, 96], F32)
            K.copy("dve", v3(mo[:, :], 48), mod[:, 1, :, :])
            K.final.append(K.dma(mod1_o, mo[:]))
        if stop == 1:
            return finish()

        QT = arv(0, [4, NQ])
        KT = arv(9216, [2, E])
        VV = arv(14848, [22, 4, 128])
        wp = arv(26112, [8, 768])
        hblk = arv(32256, [8, 512])
        PT2 = [arv(26112 + i * 1024, [1024]) for i in range(2)]
        biasb = [arv(28160 + i * 3072, [6, 512]) for i in range(2)]
        amask = arv(34304, [4, 512])
        ropetab = arv(0, [2, 512], t=ARF)
        xhs = arv(1024, [8, 256], t=ARF)
        ewv = ewin_d.rearrange("(k p) c -> p k c", p=128)
        xhv = xh_d.rearrange("(k p) t -> p k t", p=128)
        ropeAv = ropeA_d.rearrange("c p t -> p c t")

        def post_chunk(acc, nt, dst, gain, rope, tabc0):
            sq = K.rot("sqb", sqb)
            raw = K.rot("ftmp", ftmp)
            K.actv(sq[:, :nt], acc[:, :nt], AF.Square)
            K.copy("dve", raw[:, :nt], acc[:, :nt])
            ss = PS[3]
            K.mm(ss[:, :nt], blk[:], sq[:, :nt])
            rs = K.rot("rsb", rsb)
            K.actv(rs[:, :nt], ss[:, :nt], AF.Sqrt, scale=1.0 / 64.0, bias=epsv[:, 0:1])
            K.recip(rs[:, :nt], rs[:, :nt])
            if not rope:
                K.stt("dve", dst, raw[:, :nt], gain, rs[:, :nt], MULT, MULT)
                return
            qn = K.rot("sqb", sqb)
            K.stt("dve", qn[:, :nt], raw[:, :nt], gain, rs[:, :nt], MULT, MULT)
            sw = PS[4]
            K.mm(sw[:, :nt], cmat[:, 1, :], qn[:, :nt])
            t1 = K.rot("ftmp", ftmp)
            t2 = K.rot("ftmp", ftmp)
            K.tt("pool", t1[:, :nt], qn[:, :nt], ropetab[:, 0, :nt], MULT)
            K.tt("dve", t2[:, :nt], sw[:, :nt], ropetab[:, 1, :nt], MULT)
            K.tt("pool", dst, t1[:, :nt], t2[:, :nt], ADD)

        def l0_pass(pid, do_attn=True):
            isA = pid == 0
            half = pid - 1
            nQc = 4 if isA else 2
            nKc = 1 if isA else 2
            nV = 2 if isA else 4
            if isA:
                for s in range(2):
                    for jj in range(4):
                        K.dma(wp[:, :, jj * 128 + s * 64: jj * 128 + s * 64 + 64], ewv[:, :, s * 256 + jj * 64: s * 256 + jj * 64 + 64], eng="pool")
                K.dma(wp[:, :, 512:640], ewv[:, :, 512:640], eng="pool")
                K.dma(wp[:, :, 640:768], ewv[:, :, 640:768], eng="pool")
                qcol0, kcol0, vcol0 = 0, 512, 640
                gq, gk = gsc[:, 0:1], vecs[:, 33:34]
            else:
                K.dma(wp[:, :, 0:256], ewv[:, :, 768 + 256 * half: 768 + 256 * half + 256], eng="pool")
                K.dma(wp[:, :, 256:512], ewv[:, :, 1280 + 256 * half: 1280 + 256 * half + 256], eng="pool")
                K.dma(wp[:, :, 512:768], ewv[:, :, 1792 + 256 * half: 1792 + 256 * half + 256], eng="pool")
                qcol0, kcol0, vcol0 = 0, 256, 512
                gq, gk = gsc[:, 1:2], vecs[:, 35:36]
            K.memset("pool", VV[:, :, 0:nV, 64:128], 1.0)
            blocks = []
            blocks.append(("hb", 256, 0, None, 0, 0))
            for b in range(4):
                blocks.append((b, 512, HAL + b * 512, b * 512, 0, HAL + b * 512))
            blocks.append(("ha", 256, HAL + T, None, 0, HAL + T))
            blocks.append(("ctx", 256, 2 * HAL + T, T, 1, None))
            for (bid, nt, e0, q0, j, rc0) in blocks:
                if bid == "hb":
                    K.dma(xhs, xhv[:, :, 0:256])
                    src = xhs
                elif bid == "ha":
                    K.dma(xhs, xhv[:, :, 256:512])
                    src = xhs
                elif bid == "ctx":
                    src = xT[:, :, T:NQ]
                else:
                    src = xT[:, :, bid * 512:(bid + 1) * 512]
                rope = isA and (rc0 is not None)
                if rope:
                    K.dma(ropetab[:, :, :nt], ropeAv[:, :, rc0:rc0 + nt])
                norm_block(src, nt, 0, 0, j, hblk[:, :, :nt])
                if q0 is not None:
                    for qc in range(nQc):
                        acc = K.rot("pacc", [PS[0], PS[1], PS[2]])
                        for k in range(8):
                            K.mm(acc[:, :nt], wp[:, k, qcol0 + qc * 128: qcol0 + (qc + 1) * 128], hblk[:, k, :nt], start=(k == 0), stop=(k == 7))
                        post_chunk(acc, nt, QT[:, qc, q0:q0 + nt], gq, rope, rc0)
                for kc in range(nKc):
                    acc = K.rot("pacc", [PS[0], PS[1], PS[2]])
                    for k in range(8):
                        K.mm(acc[:, :nt], wp[:, k, kcol0 + kc * 128: kcol0 + (kc + 1) * 128], hblk[:, k, :nt], start=(k == 0), stop=(k == 7))
                    post_chunk(acc, nt, KT[:, kc, e0:e0 + nt], gk, rope, rc0)
                for tt_ in range(nt // 128):
                    acc = PS[5]
                    for k in range(8):
                        K.mm(acc[:, 0:nV * 64], hblk[:, k, tt_ * 128:(tt_ + 1) * 128], wp[:, k, vcol0:vcol0 + nV * 64], start=(k == 0), stop=(k == 7))
                    ec = e0 // 128 + tt_
                    K.copy("act", VV[:, ec, 0:nV, 0:64], v3(acc[:, 0:nV * 64], nV))

            if not do_attn:
                return
            if isA:
                K.dma(amask, amask_d)

            def finalize(O, heads_hc, sink_cols):
                rec = K.rot("rsb", rsb)
                if sink_cols is not None:
                    for hh in range(4):
                        K.ts("dve", rec[64:128, hh * 128:(hh + 1) * 128], O[64:128, hh * 128:(hh + 1) * 128], sinkx[64:128, sink_cols[hh]:sink_cols[hh] + 1], ADD)
                    K.recip(rec[64:128, :], rec[64:128, :])
                else:
                    K.recip(rec[64:128, :], O[64:128, :])
                return rec

            def attn_tile_A(q0, chunks):
                sts = [chunks[i:i + 2] for i in range(0, len(chunks), 2)]
                for g in range(2):
                    O = K.rot("Ob", [PS[4], PS[5]])
                    cur = {}

                    def issue(i, g=g, cur=cur):
                        S2 = K.rot("S2", [PD[0], PD[1]])
                        cur[i] = S2
                        for ii, (ec, mv) in enumerate(sts[i]):
                            if mv is not None:
                                K.mm(S2[:, ii, :], ident, amask[:, mv, :], start=True, stop=False, skip=True)
                            K.mm(S2[:, ii, :], KT[64 * g:64 * g + 64, 0, ec * 128:(ec + 1) * 128], QT[64 * g:64 * g + 64, 0:4, q0:q0 + 128],
                                 start=(mv is None), stop=True, skip=True)

                    def consume(i, g=g, cur=cur, O=O):
                        S2 = cur[i]
                        n = len(sts[i])
                        pt = K.rot("PT2", PT2)
                        K.actv(v3(pt, 2)[:, 0:n, :], S2[:, 0:n, :], AF.Exp)
                        for ii, (ec, mv) in enumerate(sts[i]):
                            first = (i == 0 and ii == 0)
                            last = (i == len(sts) - 1 and ii == n - 1)
                            K.mm(O, VV[:, ec, g, :], pt[:, ii * 512:(ii + 1) * 512], start=first, stop=last)

                    pipeline(len(sts), issue, consume)
                    rec = finalize(O, None, [4 * g + hh for hh in range(4)])
                    for hh in range(4):
                        hc = 4 * g + hh
                        dst = A36v[64 * (hc % 2):64 * (hc % 2) + 64, hc // 2, q0:q0 + 128]
                        K.tt("dve", dst, O[0:64, hh * 128:(hh + 1) * 128], rec[64:128, hh * 128:(hh + 1) * 128], MULT)

            def attn_tile_B(q0, chunks, bias):
                O = K.rot("Ob", [PS[4], PS[5]])
                K.memset("dve", O, 0.0)
                cur = {}

                def issue(i):
                    (ec, bi) = chunks[i]
                    S2 = K.rot("S2", [PD[0], PD[1]])
                    cur[i] = S2
                    for s_ in range(2):
                        if bi is not None:
                            K.mm(S2[:, s_, 0:256], ident, bias[:, bi, s_ * 256:(s_ + 1) * 256], start=True, stop=False, skip=True)
                        for cc in range(2):
                            K.mm(S2[:, s_, cc * 128:(cc + 1) * 128], KT[64 * s_:64 * s_ + 64, cc, ec * 128:(ec + 1) * 128],
                                 QT[64 * s_:64 * s_ + 64, cc, q0:q0 + 128], start=(bi is None), stop=True, skip=True)

                def consume(i):
                    (ec, bi) = chunks[i]
                    S2 = cur[i]
                    pt = K.rot("PT2", PT2)
                    K.actv(v3(pt[:, 0:512], 2), S2[:, :, 0:256], AF.Exp)
                    for hh in range(4):
                        pos = (hh % 2) * 2 + hh // 2
                        K.mm(O[:, hh * 128:(hh + 1) * 128], VV[:, ec, hh, :], pt[:, pos * 128:(pos + 1) * 128], start=False, stop=False, skip=True)

                pipeline(len(chunks), issue, consume)
                rec = finalize(O, None, None)
                for hh in range(4):
                    hc = 8 + 4 * half + hh
                    dst = A36v[64 * (hc % 2):64 * (hc % 2) + 64, hc // 2, q0:q0 + 128]
                    K.tt("dve", dst, O[0:64, hh * 128:(hh + 1) * 128], rec[64:128, hh * 128:(hh + 1) * 128], MULT)

            CTXC = [(20, None), (21, None)]
            for jt in range(16):
                q0 = jt * 128
                if isA:
                    chunks = [(2 + jt - 1, 2 if jt == 0 else 0), (2 + jt, None), (2 + jt + 1, 3 if jt == 15 else 1)] + CTXC
                    attn_tile_A(q0, chunks)
                else:
                    if jt == 0:
                        ms, var = list(range(-2, 4)), 0
                    elif jt == 15:
                        ms, var = list(range(12, 18)), 4
                    else:
                        ms = list(range(jt - 2, jt + 3))
                        var = 1 if jt == 1 else (3 if jt == 14 else 2)
                    bias = K.rot("biasb", biasb)
                    K.dma(bias, bbias_d[var][:, :, 512 * half:512 * half + 512])
                    chunks = [(m + 2, ci) for ci, m in enumerate(ms)] + CTXC
                    attn_tile_B(q0, chunks, bias)
            for ct in range(2):
                q0 = T + ct * 128
                if isA:
                    attn_tile_A(q0, CTXC)
                else:
                    attn_tile_B(q0, CTXC, None)

        if stop == 2:
            l0_pass(0, False)
            return finish()
        if stop == 3:
            l0_pass(0)
            return finish()
        if stop == 4:
            l0_pass(1, False)
            return finish()
        if stop == 5:
            l0_pass(1)
            return finish()
        for pid in range(3):
            l0_pass(pid)
        if stop == 6:
            return finish()

        wout = arv(0, [8, 1024])
        K.dma(wout, ewout_d.rearrange("(k p) f -> p k f", p=128), eng="pool")
        for (t0, nt, j) in OWN_BLOCKS + [CTX_BLOCK]:
            for m in range(8):
                acc = K.rot("oacc", [PS[5], PS[6], PS[7]])
                for k in range(8):
                    K.mm(acc[:, :nt], wout[:, k, m * 128:(m + 1) * 128], A36v[:, k, t0:t0 + nt], start=(k == 0), stop=(k == 7))
                K.stt("dve", xT[:, m, t0:t0 + nt], acc[:, :nt], mod_ap(0, 2, m, j), xT[:, m, t0:t0 + nt], MULT, ADD)
        if stop == 7:
            return finish()
        for (t0, nt, j) in OWN_BLOCKS + [CTX_BLOCK]:
            norm_block(xT[:, :, t0:t0 + nt], nt, 0, 1, j, A36v[:, :, t0:t0 + nt])
        mlp(0, OWN_BLOCKS + [CTX_BLOCK], A36v)

        if stop == 8:
            return finish()
        win1 = arv(0, [8, 672])
        hb1 = arv(5376, [8, 512])
        CQN = arv(9472, [3, T])
        CKVN = arv(15616, [2, NQ])
        KR = arv(20224, [NQ], p0=0, p1=32)
        K.dma(win1, owin_d.rearrange("(k p) c -> p k c", p=128), eng="pool")
        rawv = arv(1024, [3, 512], t=ARF)
        rk = arv(0, [2, 512], p0=0, p1=32, t=ARF)
        ropeKv = ropeK_d.rearrange("c p t -> p c t")
        for (t0, nt, j) in OWN_BLOCKS + [CTX_BLOCK]:
            norm_block(xT[:, :, t0:t0 + nt], nt, 1, 0, j, hb1[:, :, :nt])
            groups = [(384, 2, 256.0, 39, CKVN)]
            if j == 0:
                groups = [(0, 3, 384.0, 36, CQN)] + groups
            for (c0, ncn, dn, gcol, dstT) in groups:
                ss = PS[3]
                for c in range(ncn):
                    acc = K.rot("pacc", [PS[0], PS[1], PS[2]])
                    for k in range(8):
                        K.mm(acc[:, :nt], win1[:, k, c0 + c * 128:c0 + (c + 1) * 128], hb1[:, k, :nt], start=(k == 0), stop=(k == 7))
                    sq = K.rot("sqb", sqb)
                    K.actv(sq[:, :nt], acc[:, :nt], AF.Square)
                    K.copy("dve", rawv[:, c, :nt], acc[:, :nt])
                    K.mm(ss[:, :nt], ones128[:], sq[:, :nt], start=(c == 0), stop=(c == ncn - 1))
                rs = K.rot("rsb", rsb)
                K.actv(rs[:, :nt], ss[:, :nt], AF.Sqrt, scale=1.0 / dn, bias=epsv[:, 0:1])
                K.recip(rs[:, :nt], rs[:, :nt])
                for c in range(ncn):
                    K.stt("dve", dstT[:, c, t0:t0 + nt], rawv[:, c, :nt], vecs[:, gcol + c:gcol + c + 1], rs[:, :nt], MULT, MULT)
            acc = K.rot("pacc", [PS[0], PS[1], PS[2]])
            for k in range(8):
                K.mm(acc[0:32, :nt], win1[:, k, 640:672], hb1[:, k, :nt], start=(k == 0), stop=(k == 7))
            sq = K.rot("sqb", sqb)
            raw = K.rot("ftmp", ftmp)
            K.actv(sq[0:32, :nt], acc[0:32, :nt], AF.Square)
            K.copy("dve", raw[0:32, :nt], acc[0:32, :nt])
            ss = PS[3]
            K.mm(ss[0:32, :nt], ones128[0:32, 0:32], sq[0:32, :nt])
            rs = K.rot("rsb", rsb)
            K.actv(rs[0:32, :nt], ss[0:32, :nt], AF.Sqrt, scale=1.0 / 32.0, bias=epsv[0:32, 0:1])
            K.recip(rs[0:32, :nt], rs[0:32, :nt])
            if j == 1:
                K.stt("dve", KR[:, t0:t0 + nt], raw[0:32, :nt], vecs[0:32, 43:44], rs[0:32, :nt], MULT, MULT)
            else:
                K.dma(rk[:, :, :nt], ropeKv[:, :, t0:t0 + nt])
                qn = K.rot("sqb", sqb)
                K.stt("dve", qn[0:32, :nt], raw[0:32, :nt], vecs[0:32, 43:44], rs[0:32, :nt], MULT, MULT)
                sw = PS[4]
                K.mm(sw[0:32, :nt], cmat[0:32, 2, 0:32], qn[0:32, :nt])
                t1 = K.rot("ftmp", ftmp)
                t2 = K.rot("ftmp", ftmp)
                K.tt("pool", t1[0:32, :nt], qn[0:32, :nt], rk[:, 0, :nt], MULT)
                K.tt("dve", t2[0:32, :nt], sw[0:32, :nt], rk[:, 1, :nt], MULT)
                K.tt("pool", KR[:, t0:t0 + nt], t1[0:32, :nt], t2[0:32, :nt], ADD)
        d1 = K.dma(xchg_o[0:256, :].rearrange("(c p) t -> p c t", p=128), CKVN)
        d2 = K.dma(xchg_o[256:288, :], KR)
        if mode == "A":
            K.final += [d1, d2]
            K.final.append(K.dma(cqn_o.rearrange("(c p) t -> p c t", p=128), CQN))
            x1v = x1_o.rearrange("(k p) t -> p k t", p=128)
            for k in range(8):
                K.final.append(K.dma(x1v[:, k, :], xT[:, k, 0:T]))

    if B_:
        CQN = arv(9472, [3, T])
        CKVALL = v3(A36[:, 0:2 * NKEY], 2)
        VVh = arv(0, [66, 128])
        KTh = arv(15616, [NKEY], p0=0, p1=96)
        QTh = arv(24064, [T], p0=0, p1=96)
        ropeQ = arv(26112, [2, T], p0=64, p1=96)
        PT2 = [arv(30208 + i * 1024, [1024]) for i in range(2)]
        wuqh = [arv(32256 + i * 288, [3, 96]) for i in range(2)]
        wukvh = [arv(32832 + i * 256, [2, 128]) for i in range(2)]
        woh = [arv(33344 + i * 1024, [1024], p0=0, p1=64) for i in range(2)]
        OTh = [arv(35392 + i * 512, [512], p0=0, p1=64) for i in range(2)]
        if mode == "B":
            x1v = x1_d.rearrange("(k p) t -> p k t", p=128)
            for k in range(8):
                K.dma(xT[:, k, 0:T], x1v[:, k, :])
            K.dma(CQN, cqn_d.rearrange("(c p) t -> p c t", p=128))
            mo = K.sb("mo", [128, 96], F32)
            K.dma(mo[:], mod1_d)
            K.copy("dve", mod[:, 1, :, :], v3(mo[:, :], 48))
            for which in range(2):
                sck = 1 if which == 0 else 4
                nv = vecs[:, (2 + which) * 8:(2 + which) * 8 + 8]
                K.stt("dve", Amat[:, 1, which, 0, :], mod[:, 1, sck * 8:sck * 8 + 8, 0], 1.0, nv, ADD, MULT)
            for c in range(2):
                for kq in range(4):
                    K.dma(CKVALL[:, c, kq * 2112:(kq + 1) * 2112], kvall_d[c * 128:(c + 1) * 128, kq * 2112:(kq + 1) * 2112])
            K.dma(KTh[64:96, :], kvall_d[256:288, :])
        else:
            gat = K.S.add("pool", lambda e: e.collective_compute("AllGather", ALU.bypass, replica_groups=[[0, 1, 2, 3], [4, 5, 6, 7]],
                                                                 ins=[xchg_o[:, :]], outs=[kvg_i[:, :]]),
                          [xchg_o[:, :]], [kvg_i[:, :]], dma=True)
            for rr in range(4):
                for c in range(2):
                    K.dma(CKVALL[:, c, rr * T:(rr + 1) * T], kvg_i[rr * 288 + c * 128: rr * 288 + (c + 1) * 128, 0:T])
                K.dma(KTh[64:96, rr * T:(rr + 1) * T], kvg_i[rr * 288 + 256: rr * 288 + 288, 0:T])
            for c in range(2):
                K.dma(CKVALL[:, c, 4 * T:NKEY], xchg_o[c * 128:(c + 1) * 128, T:T + CT])
            K.dma(KTh[64:96, 4 * T:NKEY], xchg_o[256:288, T:T + CT])
        K.dma(ropeQ, ropeQ_d.rearrange("c p t -> p c t"))
        K.memset("pool", VVh[:, :, 64:128], 1.0)
        wuqv = wuq_d.rearrange("(c p) f -> p c f", p=128)
        wukvv = wukv_d.rearrange("(c p) f -> p c f", p=128)
        MISC = [PS[6], PS[7]]
        KBLK = [(kb * 512, 512) for kb in range(16)] + [(8192, 256)]
        def load_head(h):
            K.dma(wuqh[h % 2], wuqv[:, :, h * 96:(h + 1) * 96], eng="pool")
            K.dma(wukvh[h % 2], wukvv[:, :, h * 128:(h + 1) * 128], eng="pool")
            K.dma(woh[h % 2], owout_d[h * 64:(h + 1) * 64, :], eng="pool")
        lnb = [arv(2048 + i * 512, [512], t=ARF) for i in range(2)]

        def rstd_from(ss_ap, npart, n, scale):
            ln_ = K.rot("lnb", lnb)
            K.actv(ln_[0:npart, :n], ss_ap, AF.Ln, scale=scale, bias=epsv[0:npart, 0:1])
            rs = K.rot("rsb", rsb)
            K.actv(rs[0:npart, :n], ln_[0:npart, :n], AF.Exp, scale=-0.5)
            return rs

        def prod_K(h, kbi):
            wkv = wukvh[h % 2]
            (k0, nk) = KBLK[kbi]
            acc = K.rot("misc", MISC)
            for c in range(2):
                K.mm(acc[0:64, :nk], wkv[:, c, 0:64], CKVALL[:, c, k0:k0 + nk], start=(c == 0), stop=(c == 1))
            raw = K.rot("ftmp", ftmp)
            K.copy("dve", raw[0:64, :nk], acc[0:64, :nk])
            sq = K.rot("sqb", sqb)
            K.tt("pool", sq[0:64, :nk], raw[0:64, :nk], raw[0:64, :nk], MULT)
            ss = K.rot("misc", MISC)
            K.mm(ss[0:64, :nk], ones128[0:64, 0:64], sq[0:64, :nk])
            rs = rstd_from(ss[0:64, :nk], 64, nk, 1.0 / 64.0)
            K.stt("dve", KTh[0:64, k0:k0 + nk], raw[0:64, :nk], vecs[0:64, 42:43], rs[0:64, :nk], MULT, MULT)

        def prod_V(h, c4):
            wkv = wukvh[h % 2]
            n4 = min(4, 66 - c4)
            acc = K.rot("misc", MISC)
            for i in range(n4):
                for c in range(2):
                    K.mm(acc[:, i * 64:(i + 1) * 64], CKVALL[:, c, (c4 + i) * 128:(c4 + i + 1) * 128], wkv[:, c, 64:128],
                         start=(c == 0), stop=(c == 1), skip=True)
            K.copy("dve", VVh[:, c4:c4 + n4, 0:64], v3(acc[:, 0:n4 * 64], n4))

        def prod_Q(h, qb):
            wq = wuqh[h % 2]
            acc = K.rot("misc", MISC)
            for c in range(3):
                K.mm(acc[0:96, :], wq[:, c, :], CQN[:, c, qb * 512:(qb + 1) * 512], start=(c == 0), stop=(c == 2))
            raw = K.rot("ftmp", ftmp)
            K.copy("dve", raw[0:96, :], acc[0:96, :])
            sq = K.rot("sqb", sqb)
            K.tt("pool", sq[0:96, :], raw[0:96, :], raw[0:96, :], MULT)
            ss = K.rot("misc", MISC)
            K.mm(ss[0:96, :], blk96[0:96, 0:96], sq[0:96, :])
            rs = rstd_from(ss[0:96, :], 96, 512, vecs[0:96, 44:45])
            qd = QTh[0:96, qb * 512:(qb + 1) * 512]
            K.stt("dve", qd, raw[0:96, :], gsc[0:96, 2:3], rs[0:96, :], MULT, MULT)
            sw = K.rot("misc", MISC)
            qr = QTh[64:96, qb * 512:(qb + 1) * 512]
            K.mm(sw[0:32, :], cmat[64:96, 3, 0:32], qr)
            t1 = K.rot("ftmp", ftmp)
            t2 = K.rot("ftmp", ftmp)
            K.tt("pool", t1[64:96, :], qr, ropeQ[:, 0, qb * 512:(qb + 1) * 512], MULT)
            K.tt("dve", t2[64:96, :], sw[0:32, :], ropeQ[:, 1, qb * 512:(qb + 1) * 512], MULT)
            K.tt("pool", qr, t1[64:96, :], t2[64:96, :], ADD)

        load_head(0)
        for kbi in range(17):
            prod_K(0, kbi)
            prod_V(0, 4 * kbi)
        for qb in range(4):
            prod_Q(0, qb)
        for h in range(16):
            wo = woh[h % 2]
            nxt = h + 1 < 16
            if nxt:
                load_head(h + 1)
            for qb in range(4):
                O = K.rot("Ob", [PS[4], PS[5]])
                cur = {}

                def issue(i, qb=qb, cur=cur):
                    S2 = K.rot("S2", [PD[0], PD[1]])
                    cur[i] = S2
                    for ii in range(2):
                        c = 2 * i + ii
                        K.mm(S2[:, ii, :], KTh[0:96, c * 128:(c + 1) * 128], QTh[0:96, qb * 512:(qb + 1) * 512])

                def consume(i, qb=qb, cur=cur, O=O, h=h, nxt=nxt):
                    S2 = cur[i]
                    pt = K.rot("PT2", PT2)
                    K.actv(v3(pt, 2), S2[:, :, :], AF.Exp)
                    for ii in range(2):
                        c = 2 * i + ii
                        K.mm(O, VVh[:, c, :], pt[:, ii * 512:(ii + 1) * 512], start=(c == 0), stop=(c == 65))
                    if nxt and qb == 3 and (i % 2 == 1 or i == 32):
                        j = i // 2
                        prod_K(h + 1, j)
                        prod_V(h + 1, 4 * j)

                pipeline(33, issue, consume)
                rec = K.rot("rsb", rsb)
                K.recip(rec[64:128, :], O[64:128, :])
                ot = K.rot("OTh", OTh)
                K.tt("dve", ot, O[0:64, :], rec[64:128, :], MULT)
                for m in range(8):
                    Y = K.rot("misc", MISC)
                    K.mm(Y, wo[:, m * 128:(m + 1) * 128], ot)
                    K.stt("dve", xT[:, m, qb * 512:(qb + 1) * 512], Y, mod_ap(1, 2, m, 0), xT[:, m, qb * 512:(qb + 1) * 512], MULT, ADD)
                if nxt:
                    prod_Q(h + 1, qb)
        for (t0, nt, j) in OWN_BLOCKS:
            norm_block(xT[:, :, t0:t0 + nt], nt, 1, 1, 0, A36v[:, :, t0:t0 + nt])
        mlp(1, OWN_BLOCKS, A36v)
        ov = out_o.rearrange("(k p) t -> p k t", p=128)
        for k in range(8):
            K.final.append(K.dma(ov[:, k, :], xT[:, k, 0:T]))

    return finish()


_BF = ml_dtypes.bfloat16


def _fm(v):
    return np.ascontiguousarray(np.asarray(v, np.float32).reshape(-1, 128).T)


def _rope_tab(pos, hw):
    inv = (np.float32(10000.0) ** (-np.arange(hw, dtype=np.float32) / np.float32(hw))).astype(np.float32)
    ang = pos.astype(np.float32)[None, :] * inv[:, None]
    return np.cos(ang).astype(np.float32), np.sin(ang).astype(np.float32)


def _perm_signed(blocks, n):
    P = np.zeros((n, n), np.float32)
    for (b, hw) in blocks:
        for i in range(hw):
            P[b + hw + i, b + i] = -1.0
            P[b + i, b + hw + i] = 1.0
    return P


def _host_common(inp):
    f32 = np.float32
    vecs = np.zeros((128, 48), f32)
    vecs[:, 0:8] = _fm(inp["norm_mix"][0])
    vecs[:, 8:16] = _fm(inp["norm_mlp"][0])
    vecs[:, 16:24] = _fm(inp["norm_mix"][1])
    vecs[:, 24:32] = _fm(inp["norm_mlp"][1])
    rep64 = lambda v: np.tile(np.asarray(v, f32).reshape(64), 2)
    vecs[:, 32] = rep64(inp["a_q_norm"][0])
    vecs[:, 33] = rep64(inp["a_k_norm"][0])
    vecs[:, 34] = rep64(inp["b_q_norm"][0])
    vecs[:, 35] = rep64(inp["b_k_norm"][0])
    vecs[:, 36:39] = _fm(inp["o_qa_norm"][0])
    vecs[:, 39:41] = _fm(inp["o_kva_norm"][0])
    vecs[0:64, 41] = inp["o_qn_nope"][0]
    vecs[64:96, 41] = inp["o_qn_rope"][0]
    vecs[:, 42] = rep64(inp["o_kn_nope"][0])
    vecs[:, 43] = np.tile(np.asarray(inp["o_kn_rope"][0], f32), 4)
    vecs[0:64, 44] = 1.0 / 64.0
    vecs[64:128, 44] = 1.0 / 32.0
    cmat = np.zeros((4, 128, 128), f32)
    cmat[0] = np.eye(128, dtype=f32)
    cmat[1] = _perm_signed([(0, 16), (32, 16), (64, 16), (96, 16)], 128)
    p32 = _perm_signed([(0, 8), (16, 8)], 32)
    cmat[2, 0:32, 0:32] = p32
    cmat[3, 64:96, 0:32] = p32
    return dict(vecs=vecs, cmat=cmat.astype(_BF),
                mlp_w1=np.ascontiguousarray(inp["mlp_w1"], f32), mlp_w2=np.ascontiguousarray(inp["mlp_w2"], f32))


def _rope32_tables(tok):
    row, col = tok // 64, tok % 64
    cr, sr = _rope_tab(row, 8)
    cc, sc = _rope_tab(col, 8)
    cos = np.concatenate([cr, cr, cc, cc], 0)
    sin = np.concatenate([sr, sr, sc, sc], 0)
    return np.stack([cos, sin], 0).astype(np.float32)


def _host_A(inp, core):
    f32 = np.float32
    b, r = core // 4, core % 4
    x = inp["x"][b]
    d = {}
    d["xT"] = np.ascontiguousarray(x[r * T:(r + 1) * T].T, f32)
    xh = np.zeros((1024, 2 * HAL), f32)
    if r > 0:
        xh[:, 0:HAL] = x[r * T - HAL:r * T].T
    if r < 3:
        xh[:, HAL:] = x[(r + 1) * T:(r + 1) * T + HAL].T
    d["xhT"] = xh
    d["ctxT"] = np.ascontiguousarray(inp["ctx"][b].T, f32)
    cond = np.zeros((128, 8, 2), f32)
    cond[:, :, 0] = _fm(inp["c"][b])
    cond[:, :, 1] = _fm(inp["c_ctx"])
    d["condT"] = cond.reshape(128, 16)
    d["ada_w"] = np.ascontiguousarray(inp["ada_w"], f32)
    d["adab"] = np.concatenate([_fm(inp["ada_b"][0]), _fm(inp["ada_b"][1])], 1)
    d["e_w_in"] = np.ascontiguousarray(inp["e_w_in"][0], f32)
    d["e_w_out"] = np.ascontiguousarray(inp["e_w_out"][0], f32)
    d["o_w_in"] = np.ascontiguousarray(inp["o_w_in"][0], f32)
    tok = np.arange(r * T - HAL, (r + 1) * T + HAL)
    tokc = np.clip(tok, 0, 8191)
    cr, sr = _rope_tab(tokc // 64, 16)
    cc, sc = _rope_tab(tokc % 64, 16)
    cos64 = np.concatenate([cr, cr, cc, cc], 0)
    sin64 = np.concatenate([sr, sr, sc, sc], 0)
    d["ropeA"] = np.stack([np.tile(cos64, (2, 1)), np.tile(sin64, (2, 1))], 0).astype(f32)
    d["ropeK"] = _rope32_tables(np.arange(r * T, (r + 1) * T))
    kk = np.arange(128)[:, None]
    qq = np.arange(128)[None, :]
    prev = np.where(kk >= qq, 0.0, NEG).astype(f32)
    nxt = np.where(kk <= qq, 0.0, NEG).astype(f32)
    allneg = np.full((128, 128), NEG, f32)
    var = [prev, nxt, allneg if r == 0 else prev, allneg if r == 3 else nxt]
    d["amask"] = np.stack([np.tile(v, (1, 4)) for v in var], 1).astype(_BF)
    rpb = np.asarray(inp["b_rpb"][0], f32)
    bb = np.full((5, 128, 6, 8, 128), NEG, f32)
    k_i = np.arange(128)
    q_i = np.arange(128)
    for vi, jt in enumerate([0, 1, 5, 14, 15]):
        if jt == 0:
            ms = list(range(-2, 4))
        elif jt == 15:
            ms = list(range(12, 18))
        else:
            ms = list(range(jt - 2, jt + 3))
        rr = r if vi != 2 else 1
        gq = 32 * rr + 2 * jt + q_i // 64
        cq = q_i % 64
        start = np.clip(gq - 4, 0, 120)
        c0 = np.clip(cq - 8, 0, 48)
        for ci, m in enumerate(ms):
            gk = 32 * rr + 2 * m + k_i // 64
            ck = k_i % 64
            valid = ((gk[:, None] >= 0) & (gk[:, None] < 128) & (gk[:, None] >= start[None, :]) & (gk[:, None] < start[None, :] + 8)
                     & (ck[:, None] >= c0[None, :]) & (ck[:, None] < c0[None, :] + 16))
            dri = np.clip(gk[:, None] - gq[None, :] + 7, 0, 14)
            dci = np.clip(ck[:, None] - cq[None, :], -15, 15) + 15
            g = rpb[:, dri, dci]
            bb[vi, :, ci, :, :] = np.where(valid[None], g, NEG).transpose(1, 0, 2)
    bb = bb[:, :, :, [0, 2, 1, 3, 4, 6, 5, 7], :]
    d["bbias"] = np.ascontiguousarray(bb).reshape(5, 128, 6, 1024).astype(_BF)
    d["sink"] = np.tile(np.asarray(inp["a_sink"][0], f32)[None, :], (128, 1))
    return d


def _host_B(inp, core):
    f32 = np.float32
    r = core % 4
    d = {}
    d["o_w_uq"] = np.ascontiguousarray(inp["o_w_uq"][0], f32)
    d["o_w_ukv"] = np.ascontiguousarray(inp["o_w_ukv"][0], f32)
    d["o_w_out"] = np.ascontiguousarray(inp["o_w_out"][0], f32)
    d["ropeQ"] = _rope32_tables(np.arange(r * T, (r + 1) * T)).astype(_BF)
    return d


_NC_CACHE = {}


def _get_nc(mode):
    if mode not in _NC_CACHE:
        _NC_CACHE[mode] = build(mode).nc
    return _NC_CACHE[mode]


FUSED = False


def kernel(**inputs):
    inp = {k: np.asarray(v) for k, v in inputs.items()}
    common = _host_common(inp)
    out = np.empty((2, 8192, 1024), np.float32)
    if FUSED:
        maps = []
        for c in range(NCORES):
            m = dict(common)
            m.update(_host_A(inp, c))
            m.update(_host_B(inp, c))
            maps.append(m)
        res = run_bass_kernel_spmd(_get_nc("F"), maps, core_ids=list(range(NCORES)))
        for c in range(NCORES):
            out[c // 4, (c % 4) * T:(c % 4 + 1) * T, :] = np.asarray(res.results[c]["outT"]).T
        return out
    mapsA = []
    for c in range(NCORES):
        m = dict(common)
        m.update(_host_A(inp, c))
        mapsA.append(m)
    resA = run_bass_kernel_spmd(_get_nc("A"), mapsA, core_ids=list(range(NCORES)))
    ra = resA.results
    mapsB = []
    for c in range(NCORES):
        b = c // 4
        m = dict(common)
        m.update(_host_B(inp, c))
        m["x1T"] = np.asarray(ra[c]["x1T"])
        m["cqn"] = np.asarray(ra[c]["cqn"])
        m["mod1"] = np.asarray(ra[c]["mod1"])
        kv = np.concatenate([np.asarray(ra[4 * b + rr]["xchg"])[:, 0:T] for rr in range(4)] + [np.asarray(ra[c]["xchg"])[:, T:T + CT]], axis=1)
        m["kvall"] = np.ascontiguousarray(kv)
        mapsB.append(m)
    resB = run_bass_kernel_spmd(_get_nc("B"), mapsB, core_ids=list(range(NCORES)))
    for c in range(NCORES):
        out[c // 4, (c % 4) * T:(c % 4 + 1) * T, :] = np.asarray(resB.results[c]["outT"]).T
    return out
```

```python
import contextlib
import numpy as np
import ml_dtypes
import concourse.bass as bass
import concourse.mybir as mybir
from concourse.bass_utils import run_bass_kernel_spmd

F32 = mybir.dt.float32
BF16 = mybir.dt.bfloat16
AF = mybir.ActivationFunctionType
ALU = mybir.AluOpType

NCORES = 8
T = 2048
HAL = 256
CT = 256
E = HAL + T + HAL + CT
NQ = T + CT
NKEY = 8192 + CT
EPS = 1e-6
NEG = -30000.0


def _region(ap):
    name = ap.name
    space = str(ap.space)
    dims = ap.ap
    off = int(ap.offset)
    if space == "DRAM":
        lo = off
        hi = off + sum(int(s) * (int(c) - 1) for s, c in dims if int(s) > 0) + 1
        return (name, "DRAM", 0, 1, lo, hi)
    if space == "PSUM":
        fszp = 1
        for d in ap.tensor.shape[1:]:
            fszp *= int(d)
        g0 = off % fszp
        g1 = g0 + sum(int(s) * (int(c) - 1) for s, c in dims[1:] if int(s) > 0) + 1
        return (name, "PSUM", 0, 128, g0 // 512, (g1 - 1) // 512 + 1)
    pstep, pcnt = int(dims[0][0]), int(dims[0][1])
    fsz = 1
    for d in ap.tensor.shape[1:]:
        fsz *= int(d)
    p0 = off // fsz
    f0 = off % fsz
    p1 = p0 + 1 if pstep == 0 else p0 + (pstep // fsz) * (pcnt - 1) + 1
    f1 = f0 + sum(int(s) * (int(c) - 1) for s, c in dims[1:] if int(s) > 0) + 1
    return (name, "SB", p0, p1, f0, f1)


def _overlap(a, b):
    return a[2] < b[3] and b[2] < a[3] and a[4] < b[5] and b[4] < a[5]


def _covers(a, b):
    return a[2] <= b[2] and a[3] >= b[3] and a[4] <= b[4] and a[5] >= b[5]


class Sched:
    ENGS = ("pe", "act", "dve", "pool", "sp")

    def __init__(self, nc, n_dma_sems=12):
        self.nc = nc
        self.ops = []
        self.track = {}
        self.n_dma_sems = n_dma_sems

    def add(self, eng, fn, reads=(), writes=(), dma=False):
        idx = len(self.ops)
        rr = list(dict.fromkeys(_region(a) for a in reads))
        ww = list(dict.fromkeys(_region(a) for a in writes))
        deps = set()
        for r in rr:
            lst = self.track.setdefault(r[0], [])
            psum = r[1] == "PSUM"
            for (box, oi, kind) in lst:
                if (kind == "w" or psum) and _overlap(box, r):
                    deps.add(oi)
        for w in ww:
            lst = self.track.setdefault(w[0], [])
            for (box, oi, kind) in lst:
                if _overlap(box, w):
                    deps.add(oi)
        for r in rr:
            lst = self.track[r[0]]
            if r[1] == "PSUM":
                lst[:] = [t for t in lst if not _covers(r, t[0])]
                lst.append((r, idx, "w"))
            else:
                lst[:] = [t for t in lst if not (t[2] == "r" and t[1] < idx and self.ops[t[1]]["eng"] == eng
                                                 and not self.ops[t[1]]["dma"] and not dma and _covers(r, t[0]))]
                lst.append((r, idx, "r"))
        for w in ww:
            lst = self.track[w[0]]
            lst[:] = [t for t in lst if not _covers(w, t[0])]
            lst.append((w, idx, "w"))
        deps.discard(idx)
        self.ops.append(dict(eng=eng, fn=fn, deps=deps, dma=dma, sig=False, rr=rr, ww=ww))
        return idx

    def dma(self, out, in_, eng="sp"):
        return self.add(eng, lambda e: e.dma_start(out=out, in_=in_), [in_], [out], dma=True)

    def emit(self, final_wait_ops=()):
        nc = self.nc
        ops = self.ops

        def needs_wait(x, y):
            X, Y = ops[x], ops[y]
            if Y["dma"] or X["dma"]:
                return True
            if X["eng"] == Y["eng"]:
                if X["eng"] == "pe":
                    return False
                for w in Y["ww"]:
                    for r in X["rr"]:
                        if w[0] == r[0] and _overlap(w, r):
                            return True
                return False
            return True

        for i, X in enumerate(ops):
            X["wdeps"] = [y for y in X["deps"] if needs_wait(i, y)]
            for y in X["wdeps"]:
                ops[y]["sig"] = True
        for i in final_wait_ops:
            ops[i]["sig"] = True
        cnt = {e: 0 for e in self.ENGS}
        dma_k = {e: 0 for e in self.ENGS}
        dma_semcnt = {}
        for X in ops:
            if X["dma"]:
                q = X["eng"]
                k = dma_k[q]
                dma_k[q] += 1
                s = (q, k % self.n_dma_sems)
                dma_semcnt[s] = dma_semcnt.get(s, 0) + 1
                X["dsem"] = s
                X["dval"] = 16 * dma_semcnt[s]
            elif X["sig"]:
                cnt[X["eng"]] += 1
                X["cnt"] = cnt[X["eng"]]
        with contextlib.ExitStack() as st:
            sems = {e: st.enter_context(nc.semaphore("s_" + e)) for e in ("pe", "act", "dve", "pool")}
            dsems = {}
            for q in self.ENGS:
                for j in range(min(self.n_dma_sems, dma_k[q])):
                    dsems[(q, j)] = st.enter_context(nc.semaphore("d_%s_%d" % (q, j)))
            block = st.enter_context(nc.Block())
            per_eng = {e: [i for i, X in enumerate(ops) if X["eng"] == e] for e in self.ENGS}

            def run_stream(ename, e):
                known = {}

                def wait(key, semh, val):
                    if known.get(key, 0) >= val:
                        return
                    e.wait_ge(semh, val)
                    known[key] = val

                def wait_op(Y):
                    if Y["dma"]:
                        wait(Y["dsem"], dsems[Y["dsem"]], Y["dval"])
                    else:
                        wait(Y["eng"], sems[Y["eng"]], Y["cnt"])

                for i in per_eng[ename]:
                    X = ops[i]
                    for y in sorted(X["wdeps"]):
                        wait_op(ops[y])
                    if X["dma"]:
                        if X["dval"] > 16:
                            wait(X["dsem"], dsems[X["dsem"]], X["dval"] - 16)
                        X["fn"](e).then_inc(dsems[X["dsem"]], 16)
                    else:
                        ins = X["fn"](e)
                        if X["sig"]:
                            ins.then_inc(sems[ename], 1)
                if ename == "sp":
                    for i in final_wait_ops:
                        wait_op(ops[i])

            @block.tensor
            def _(e):
                run_stream("pe", e)

            @block.scalar
            def _(e):
                run_stream("act", e)

            @block.vector
            def _(e):
                run_stream("dve", e)

            @block.gpsimd
            def _(e):
                run_stream("pool", e)

            @block.sync
            def _(e):
                run_stream("sp", e)
        self.stats = dict(n_ops=len(ops), cnt=cnt, dma=dma_k)


class KB:
    def __init__(self, mode):
        self.mode = mode
        self.nc = bass.Bass("TRN2", target_bir_lowering=False)
        self.S = Sched(self.nc)
        self.st = contextlib.ExitStack()
        self.final = []
        self._rot = {}

    def din(self, name, shape, dt=F32):
        return self.nc.dram_tensor(name, list(shape), dt, kind="ExternalInput").ap()

    def dout(self, name, shape, dt=F32):
        return self.nc.dram_tensor(name, list(shape), dt, kind="ExternalOutput").ap()

    def dint(self, name, shape, dt=F32):
        return self.nc.dram_tensor(name, list(shape), dt, kind="Internal").ap()

    def sb(self, name, shape, dt):
        return self.st.enter_context(self.nc.sbuf_tensor(name, list(shape), dt))

    def rot(self, key, lst):
        i = self._rot.get(key, 0)
        self._rot[key] = i + 1
        return lst[i % len(lst)]

    def mm(self, out, lhsT, rhs, start=True, stop=True, skip=False):
        kw = dict(skip_group_check=True) if skip else {}
        return self.S.add("pe", lambda e: e.matmul(out, lhsT=lhsT, rhs=rhs, start=start, stop=stop, **kw),
                          [lhsT, rhs], [out])

    def actv(self, out, in_, func, scale=1.0, bias=None):
        reads = [in_]
        kw = {}
        if isinstance(scale, float) or isinstance(scale, int):
            kw["scale"] = float(scale)
        else:
            kw["scale"] = scale
            reads.append(scale)
        if bias is not None:
            kw["bias"] = bias
            if not isinstance(bias, float):
                reads.append(bias)
        return self.S.add("act", lambda e: e.activation(out=out, in_=in_, func=func, **kw), reads, [out])

    def tt(self, eng, out, in0, in1, op):
        return self.S.add(eng, lambda e: e.tensor_tensor(out=out, in0=in0, in1=in1, op=op), [in0, in1], [out])

    def stt(self, eng, out, in0, scalar, in1, op0, op1):
        reads = [in0, in1]
        if not isinstance(scalar, float):
            reads.append(scalar)
        return self.S.add(eng, lambda e: e.scalar_tensor_tensor(out=out, in0=in0, scalar=scalar, in1=in1, op0=op0, op1=op1),
                          reads, [out])

    def ts(self, eng, out, in0, s1, op0, s2=None, op1=None):
        reads = [in0]
        if not isinstance(s1, float):
            reads.append(s1)
        if s2 is not None and not isinstance(s2, float):
            reads.append(s2)
        if op1 is None:
            return self.S.add(eng, lambda e: e.tensor_scalar(out=out, in0=in0, scalar1=s1, scalar2=None, op0=op0), reads, [out])
        return self.S.add(eng, lambda e: e.tensor_scalar(out=out, in0=in0, scalar1=s1, scalar2=s2, op0=op0, op1=op1), reads, [out])

    def copy(self, eng, out, in_):
        if eng == "act":
            return self.S.add("act", lambda e: e.activation(out=out, in_=in_, func=AF.Copy), [in_], [out])
        return self.S.add(eng, lambda e: e.tensor_copy(out=out, in_=in_), [in_], [out])

    def recip(self, out, in_):
        return self.S.add("dve", lambda e: e.reciprocal(out=out, in_=in_), [in_], [out])

    def memset(self, eng, out, val):
        return self.S.add(eng, lambda e: e.memset(out, val), [], [out])

    def dma(self, out, in_, eng="sp"):
        return self.S.dma(out, in_, eng)


def v3(ap2, a):
    return ap2.rearrange("p (a b) -> p a b", a=a)


MULT, ADD = ALU.mult, ALU.add
NA = 36480


def build(mode, stop=0):
    K = KB(mode)
    nc, S = K.nc, K.S
    A_ = mode in ("A", "F")
    B_ = mode in ("B", "F")

    vecs_d = K.din("vecs", [128, 48])
    cmat_d = K.din("cmat", [4, 128, 128], BF16)
    w1_d = K.din("mlp_w1", [2, 1024, 4096])
    w2_d = K.din("mlp_w2", [2, 4096, 1024])
    if A_:
        xT_d = K.din("xT", [1024, T])
        xh_d = K.din("xhT", [1024, 2 * HAL])
        ctx_d = K.din("ctxT", [1024, CT])
        cond_d = K.din("condT", [128, 16])
        adaw_d = K.din("ada_w", [2, 1024, 6144])
        adab_d = K.din("adab", [128, 96])
        ewin_d = K.din("e_w_in", [1024, 2304])
        ewout_d = K.din("e_w_out", [1024, 1024])
        owin_d = K.din("o_w_in", [1024, 672])
        ropeA_d = K.din("ropeA", [2, 128, 2 * HAL + T])
        ropeK_d = K.din("ropeK", [2, 32, T])
        amask_d = K.din("amask", [128, 4, 512], BF16)
        bbias_d = K.din("bbias", [5, 128, 6, 1024], BF16)
        sink_d = K.din("sink", [128, 8])
    if B_:
        wuq_d = K.din("o_w_uq", [384, 1536])
        wukv_d = K.din("o_w_ukv", [256, 2048])
        owout_d = K.din("o_w_out", [1024, 1024])
        ropeQ_d = K.din("ropeQ", [2, 32, T], BF16)
        out_o = K.dout("outT", [1024, T])
    if mode == "A":
        x1_o = K.dout("x1T", [1024, T])
        cqn_o = K.dout("cqn", [384, T], BF16)
        xchg_o = K.dout("xchg", [288, T + CT], BF16)
        mod1_o = K.dout("mod1", [128, 96])
    if mode == "B":
        x1_d = K.din("x1T", [1024, T])
        cqn_d = K.din("cqn", [384, T], BF16)
        kvall_d = K.din("kvall", [288, NKEY], BF16)
        mod1_d = K.din("mod1", [128, 96])
    if mode == "F":
        xchg_o = K.dint("xchg_i", [288, T + CT], BF16)
        kvg_i = K.dint("kvg_i", [4 * 288, T + CT], BF16)

    if stop:
        dbg_o = K.dout("dbg", [128, NA], BF16)
        dbg36_o = K.dout("dbg36", [128, 8 * NQ], BF16)
        dbgx_o = K.dout("dbgx", [128, 8 * NQ])

    def finish():
        if stop:
            K.final.append(K.dma(dbg_o, AR[:, :]))
            K.final.append(K.dma(dbg36_o, A36[:, :]))
            K.final.append(K.dma(dbgx_o, xT[:, :, :].rearrange("p a b -> p (a b)")))
        S.emit(final_wait_ops=K.final)
        return K

    xT = K.sb("xTs", [128, 8, NQ], F32)
    A36 = K.sb("A36", [128, 8 * NQ], BF16)
    A36v = v3(A36[:, :], 8)
    mod = K.sb("mod", [128, 2, 48, 2], F32)
    Amat = K.sb("Amat", [128, 2, 2, 2, 8], F32)
    vecs = K.sb("vecs_s", [128, 48], F32)
    gsc = K.sb("gsc", [128, 4], F32)
    ones128 = K.sb("ones128", [128, 128], BF16)
    blk = K.sb("blk", [128, 128], BF16)
    blk96 = K.sb("blk96", [128, 128], BF16)
    cmat = K.sb("cmat_s", [128, 4, 128], BF16)
    ident = cmat[:, 0, :]
    sqb = [K.sb("sqb%d" % i, [128, 512], BF16) for i in range(2)]
    ftmp = [K.sb("ftmp%d" % i, [128, 512], F32) for i in range(3)]
    rsb = [K.sb("rsb%d" % i, [128, 512], F32) for i in range(2)]
    AR = K.sb("AR", [128, NA], BF16)
    ARF = K.sb("ARF", [128, 3072], F32)
    PD = [K.st.enter_context(nc.psum_tensor("PD%d" % i, [128, 2, 512], F32)) for i in range(4)]
    PS = [PD[i // 2][:, i % 2, :] for i in range(8)]

    def pipeline(n, issue, consume, look=1):
        for i in range(min(look, n)):
            issue(i)
        for i in range(n):
            if i + look < n:
                issue(i + look)
            consume(i)

    def arv(off, dims, p0=0, p1=128, t=AR):
        n = 1
        for d in dims:
            n *= d
        ap = t[p0:p1, off:off + n]
        if len(dims) == 2:
            ap = ap.rearrange("p (a b) -> p a b", a=dims[0])
        elif len(dims) == 3:
            ap = ap.rearrange("p (a b c) -> p a b c", a=dims[0], b=dims[1])
        return ap

    K.dma(vecs[:], vecs_d)
    K.dma(cmat[:], cmat_d.rearrange("c p q -> p c q"))
    epsv = K.sb("epsv", [128, 1], F32)
    K.memset("dve", epsv[:], EPS)
    K.memset("dve", ones128[:], 1.0)
    K.memset("dve", blk[:], 0.0)
    K.memset("dve", blk[0:64, 0:64], 1.0)
    K.memset("dve", blk[64:128, 64:128], 1.0)
    K.memset("dve", blk96[:], 0.0)
    K.memset("dve", blk96[0:64, 0:64], 1.0)
    K.memset("dve", blk96[64:96, 64:96], 1.0)
    K.ts("dve", gsc[:, 0:1], vecs[:, 32:33], 0.125, MULT)
    K.ts("dve", gsc[:, 1:2], vecs[:, 34:35], 0.125, MULT)
    K.ts("dve", gsc[:, 2:3], vecs[:, 41:42], float(96 ** -0.5), MULT)

    def mod_ap(l, kind, m, j):
        return mod[:, l, kind * 8 + m, j:j + 1]

    def norm_block(src, nt, l, which, j, dst):
        ss = K.rot("ssb", [PS[6], PS[7]])
        for k in range(8):
            sq = K.rot("sqb", sqb)
            K.tt("pool", sq[:, :nt], src[:, k, :], src[:, k, :], MULT)
            K.mm(ss[:, :nt], ones128[:], sq[:, :nt], start=(k == 0), stop=(k == 7))
        rs = K.rot("rsb", rsb)
        K.actv(rs[:, :nt], ss[:, :nt], AF.Sqrt, scale=1.0 / 1024.0, bias=epsv[:, 0:1])
        K.recip(rs[:, :nt], rs[:, :nt])
        shift_kind = 0 if which == 0 else 3
        for k in range(8):
            t = K.rot("ftmp", ftmp)
            K.stt("dve", t[:, :nt], src[:, k, :], Amat[:, l, which, j, k:k + 1], rs[:, :nt], MULT, MULT)
            K.actv(dst[:, k, :], t[:, :nt], AF.Identity, scale=1.0, bias=mod_ap(l, shift_kind, k, j))

    def mlp(l, nblocks, h2T):
        W1v = w1_d[l].rearrange("(k p) f -> p k f", p=128)
        W2v = w2_d[l].rearrange("(k p) f -> p k f", p=128)
        w1b = [arv(0, [8, 512]), arv(4096, [8, 512])]
        w2b = [arv(8192, [4, 1024]), arv(12288, [4, 1024])]
        ub = [arv(16384, [4, 512]), arv(18432, [4, 512])]
        rb = [arv(20480 + i * 512, [512]) for i in range(2)]
        def load_e8(e8):
            K.dma(w1b[e8 % 2], W1v[:, :, e8 * 512:(e8 + 1) * 512], eng="pool")
            K.dma(w2b[e8 % 2], W2v[:, e8 * 4:(e8 + 1) * 4, :], eng="pool")
        load_e8(0)
        for e8 in range(8):
            w1 = w1b[e8 % 2]
            w2 = w2b[e8 % 2]
            if e8 + 1 < 8:
                load_e8(e8 + 1)
            for (t0, nt, j) in nblocks:
                u = K.rot("ub", ub)
                for fc in range(4):
                    acc = K.rot("mlpacc", [PS[0], PS[1], PS[2]])
                    for k in range(8):
                        K.mm(acc[:, :nt], w1[:, k, fc * 128:(fc + 1) * 128], h2T[:, k, t0:t0 + nt], start=(k == 0), stop=(k == 7))
                    r = K.rot("rb", rb)
                    K.actv(r[:, :nt], acc[:, :nt], AF.Relu)
                    K.tt("pool", u[:, fc, :nt], r[:, :nt], r[:, :nt], MULT)
                for m in range(8):
                    acc = K.rot("mlpacc2", [PS[3], PS[4], PS[5]])
                    for fc in range(4):
                        K.mm(acc[:, :nt], w2[:, fc, m * 128:(m + 1) * 128], u[:, fc, :nt], start=(fc == 0), stop=(fc == 3))
                    K.stt("dve", xT[:, m, t0:t0 + nt], acc[:, :nt], mod_ap(l, 5, m, j), xT[:, m, t0:t0 + nt], MULT, ADD)

    OWN_BLOCKS = [(b * 512, 512, 0) for b in range(4)]
    CTX_BLOCK = (T, CT, 1)

    if A_:
        xTv = xT_d.rearrange("(k p) t -> p k t", p=128)
        for k in range(8):
            K.dma(xT[:, k, 0:T], xTv[:, k, :])
        K.dma(xT[:, :, T:NQ], ctx_d.rearrange("(k p) t -> p k t", p=128))
        cond = K.sb("cond", [128, 16], F32)
        silu = K.sb("silu", [128, 16], BF16)
        adab = K.sb("adab_s", [128, 96], F32)
        sinkx = K.sb("sinkx", [128, 8], F32)
        K.dma(cond[:], cond_d)
        K.dma(adab[:], adab_d)
        K.dma(sinkx[:], sink_d)
        K.actv(silu[:], cond[:], AF.Silu)
        K.actv(sinkx[:], sinkx[:], AF.Exp)
        siluv = v3(silu[:, :], 8)
        for l in range(2):
            Wv = adaw_d[l].rearrange("(k p) f -> p k f", p=128)
            acc = PS[0] if l == 0 else PS[1]
            adawb = [arv(0, [8, 1024]), arv(8192, [8, 1024])]
            if l == 0:
                K.dma(adawb[0], Wv[:, :, 0:1024], eng="pool")
            for piece in range(6):
                wb = adawb[(l * 6 + piece) % 2]
                nxt = l * 6 + piece + 1
                if nxt < 12:
                    Wn = adaw_d[nxt // 6].rearrange("(k p) f -> p k f", p=128)
                    K.dma(adawb[nxt % 2], Wn[:, :, (nxt % 6) * 1024:(nxt % 6 + 1) * 1024], eng="pool")
                for m in range(8):
                    f = piece * 8 + m
                    for k in range(8):
                        K.mm(acc[:, 2 * f:2 * f + 2], wb[:, k, m * 128:(m + 1) * 128], siluv[:, k, :], start=(k == 0), stop=(k == 7), skip=True)
            K.tt("dve", mod[:, l, :, :], v3(acc[:, 0:96], 48), adab[:, l * 48:(l + 1) * 48].unsqueeze(2).broadcast_to([128, 48, 2]), ADD)
            for which in range(2):
                sck = 1 if which == 0 else 4
                nv = vecs[:, (l * 2 + which) * 8:(l * 2 + which) * 8 + 8]
                for j in range(2):
                    K.stt("dve", Amat[:, l, which, j, :], mod[:, l, sck * 8:sck * 8 + 8, j], 1.0, nv, ADD, MULT)
        if mode == "A":
            mo = K.sb("mo", [128, 96], F32)
            K.copy("dve", v3(mo[:, :], 48), mod[:, 1, :, :])
            K.final.append(K.dma(mod1_o, mo[:]))
        if stop == 1:
            return finish()

        QT = arv(0, [4, NQ])
        KT = arv(9216, [2, E])
        VV = arv(14848, [22, 4, 128])
        wp = arv(26112, [8, 768])
        hblk = arv(32256, [8, 512])
        PT2 = [arv(26112 + i * 1024, [1024]) for i in range(2)]
        biasb = [arv(28160 + i * 3072, [6, 512]) for i in range(2)]
        amask = arv(34304, [4, 512])
        ropetab = arv(0, [2, 512], t=ARF)
        xhs = arv(1024, [8, 256], t=ARF)
        ewv = ewin_d.rearrange("(k p) c -> p k c", p=128)
        xhv = xh_d.rearrange("(k p) t -> p k t", p=128)
        ropeAv = ropeA_d.rearrange("c p t -> p c t")

        def post_chunk(acc, nt, dst, gain, rope, tabc0):
            sq = K.rot("sqb", sqb)
            raw = K.rot("ftmp", ftmp)
            K.actv(sq[:, :nt], acc[:, :nt], AF.Square)
            K.copy("dve", raw[:, :nt], acc[:, :nt])
            ss = PS[3]
            K.mm(ss[:, :nt], blk[:], sq[:, :nt])
            rs = K.rot("rsb", rsb)
            K.actv(rs[:, :nt], ss[:, :nt], AF.Sqrt, scale=1.0 / 64.0, bias=epsv[:, 0:1])
            K.recip(rs[:, :nt], rs[:, :nt])
            if not rope:
                K.stt("dve", dst, raw[:, :nt], gain, rs[:, :nt], MULT, MULT)
                return
            qn = K.rot("sqb", sqb)
            K.stt("dve", qn[:, :nt], raw[:, :nt], gain, rs[:, :nt], MULT, MULT)
            sw = PS[4]
            K.mm(sw[:, :nt], cmat[:, 1, :], qn[:, :nt])
            t1 = K.rot("ftmp", ftmp)
            t2 = K.rot("ftmp", ftmp)
            K.tt("pool", t1[:, :nt], qn[:, :nt], ropetab[:, 0, :nt], MULT)
            K.tt("dve", t2[:, :nt], sw[:, :nt], ropetab[:, 1, :nt], MULT)
            K.tt("pool", dst, t1[:, :nt], t2[:, :nt], ADD)

        def l0_pass(pid, do_attn=True):
            isA = pid == 0
            half = pid - 1
            nQc = 4 if isA else 2
            nKc = 1 if isA else 2
            nV = 2 if isA else 4
            if isA:
                for s in range(2):
                    for jj in range(4):
                        K.dma(wp[:, :, jj * 128 + s * 64: jj * 128 + s * 64 + 64], ewv[:, :, s * 256 + jj * 64: s * 256 + jj * 64 + 64], eng="pool")
                K.dma(wp[:, :, 512:640], ewv[:, :, 512:640], eng="pool")
                K.dma(wp[:, :, 640:768], ewv[:, :, 640:768], eng="pool")
                qcol0, kcol0, vcol0 = 0, 512, 640
                gq, gk = gsc[:, 0:1], vecs[:, 33:34]
            else:
                K.dma(wp[:, :, 0:256], ewv[:, :, 768 + 256 * half: 768 + 256 * half + 256], eng="pool")
                K.dma(wp[:, :, 256:512], ewv[:, :, 1280 + 256 * half: 1280 + 256 * half + 256], eng="pool")
                K.dma(wp[:, :, 512:768], ewv[:, :, 1792 + 256 * half: 1792 + 256 * half + 256], eng="pool")
                qcol0, kcol0, vcol0 = 0, 256, 512
                gq, gk = gsc[:, 1:2], vecs[:, 35:36]
            K.memset("pool", VV[:, :, 0:nV, 64:128], 1.0)
            blocks = []
            blocks.append(("hb", 256, 0, None, 0, 0))
            for b in range(4):
                blocks.append((b, 512, HAL + b * 512, b * 512, 0, HAL + b * 512))
            blocks.append(("ha", 256, HAL + T, None, 0, HAL + T))
            blocks.append(("ctx", 256, 2 * HAL + T, T, 1, None))
            for (bid, nt, e0, q0, j, rc0) in blocks:
                if bid == "hb":
                    K.dma(xhs, xhv[:, :, 0:256])
                    src = xhs
                elif bid == "ha":
                    K.dma(xhs, xhv[:, :, 256:512])
                    src = xhs
                elif bid == "ctx":
                    src = xT[:, :, T:NQ]
                else:
                    src = xT[:, :, bid * 512:(bid + 1) * 512]
                rope = isA and (rc0 is not None)
                if rope:
                    K.dma(ropetab[:, :, :nt], ropeAv[:, :, rc0:rc0 + nt])
                norm_block(src, nt, 0, 0, j, hblk[:, :, :nt])
                if q0 is not None:
                    for qc in range(nQc):
                        acc = K.rot("pacc", [PS[0], PS[1], PS[2]])
                        for k in range(8):
                            K.mm(acc[:, :nt], wp[:, k, qcol0 + qc * 128: qcol0 + (qc + 1) * 128], hblk[:, k, :nt], start=(k == 0), stop=(k == 7))
                        post_chunk(acc, nt, QT[:, qc, q0:q0 + nt], gq, rope, rc0)
                for kc in range(nKc):
                    acc = K.rot("pacc", [PS[0], PS[1], PS[2]])
                    for k in range(8):
                        K.mm(acc[:, :nt], wp[:, k, kcol0 + kc * 128: kcol0 + (kc + 1) * 128], hblk[:, k, :nt], start=(k == 0), stop=(k == 7))
                    post_chunk(acc, nt, KT[:, kc, e0:e0 + nt], gk, rope, rc0)
                for tt_ in range(nt // 128):
                    acc = PS[5]
                    for k in range(8):
                        K.mm(acc[:, 0:nV * 64], hblk[:, k, tt_ * 128:(tt_ + 1) * 128], wp[:, k, vcol0:vcol0 + nV * 64], start=(k == 0), stop=(k == 7))
                    ec = e0 // 128 + tt_
                    K.copy("act", VV[:, ec, 0:nV, 0:64], v3(acc[:, 0:nV * 64], nV))

            if not do_attn:
                return
            if isA:
                K.dma(amask, amask_d)

            def finalize(O, heads_hc, sink_cols):
                rec = K.rot("rsb", rsb)
                if sink_cols is not None:
                    for hh in range(4):
                        K.ts("dve", rec[64:128, hh * 128:(hh + 1) * 128], O[64:128, hh * 128:(hh + 1) * 128], sinkx[64:128, sink_cols[hh]:sink_cols[hh] + 1], ADD)
                    K.recip(rec[64:128, :], rec[64:128, :])
                else:
                    K.recip(rec[64:128, :], O[64:128, :])
                return rec

            def attn_tile_A(q0, chunks):
                sts = [chunks[i:i + 2] for i in range(0, len(chunks), 2)]
                for g in range(2):
                    O = K.rot("Ob", [PS[4], PS[5]])
                    cur = {}

                    def issue(i, g=g, cur=cur):
                        S2 = K.rot("S2", [PD[0], PD[1]])
                        cur[i] = S2
                        for ii, (ec, mv) in enumerate(sts[i]):
                            if mv is not None:
                                K.mm(S2[:, ii, :], ident, amask[:, mv, :], start=True, stop=False, skip=True)
                            K.mm(S2[:, ii, :], KT[64 * g:64 * g + 64, 0, ec * 128:(ec + 1) * 128], QT[64 * g:64 * g + 64, 0:4, q0:q0 + 128],
                                 start=(mv is None), stop=True, skip=True)

                    def consume(i, g=g, cur=cur, O=O):
                        S2 = cur[i]
                        n = len(sts[i])
                        pt = K.rot("PT2", PT2)
                        K.actv(v3(pt, 2)[:, 0:n, :], S2[:, 0:n, :], AF.Exp)
                        for ii, (ec, mv) in enumerate(sts[i]):
                            first = (i == 0 and ii == 0)
                            last = (i == len(sts) - 1 and ii == n - 1)
                            K.mm(O, VV[:, ec, g, :], pt[:, ii * 512:(ii + 1) * 512], start=first, stop=last)

                    pipeline(len(sts), issue, consume)
                    rec = finalize(O, None, [4 * g + hh for hh in range(4)])
                    for hh in range(4):
                        hc = 4 * g + hh
                        dst = A36v[64 * (hc % 2):64 * (hc % 2) + 64, hc // 2, q0:q0 + 128]
                        K.tt("dve", dst, O[0:64, hh * 128:(hh + 1) * 128], rec[64:128, hh * 128:(hh + 1) * 128], MULT)

            def attn_tile_B(q0, chunks, bias):
                O = K.rot("Ob", [PS[4], PS[5]])
                K.memset("dve", O, 0.0)
                cur = {}

                def issue(i):
                    (ec, bi) = chunks[i]
                    S2 = K.rot("S2", [PD[0], PD[1]])
                    cur[i] = S2
                    for s_ in range(2):
                        if bi is not None:
                            K.mm(S2[:, s_, 0:256], ident, bias[:, bi, s_ * 256:(s_ + 1) * 256], start=True, stop=False, skip=True)
                        for cc in range(2):
                            K.mm(S2[:, s_, cc * 128:(cc + 1) * 128], KT[64 * s_:64 * s_ + 64, cc, ec * 128:(ec + 1) * 128],
                                 QT[64 * s_:64 * s_ + 64, cc, q0:q0 + 128], start=(bi is None), stop=True, skip=True)

                def consume(i):
                    (ec, bi) = chunks[i]
                    S2 = cur[i]
                    pt = K.rot("PT2", PT2)
                    K.actv(v3(pt[:, 0:512], 2), S2[:, :, 0:256], AF.Exp)
                    for hh in range(4):
                        pos = (hh % 2) * 2 + hh // 2
                        K.mm(O[:, hh * 128:(hh + 1) * 128], VV[:, ec, hh, :], pt[:, pos * 128:(pos + 1) * 128], start=False, stop=False, skip=True)

                pipeline(len(chunks), issue, consume)
                rec = finalize(O, None, None)
                for hh in range(4):
                    hc = 8 + 4 * half + hh
                    dst = A36v[64 * (hc % 2):64 * (hc % 2) + 64, hc // 2, q0:q0 + 128]
                    K.tt("dve", dst, O[0:64, hh * 128:(hh + 1) * 128], rec[64:128, hh * 128:(hh + 1) * 128], MULT)

            CTXC = [(20, None), (21, None)]
            for jt in range(16):
                q0 = jt * 128
                if isA:
                    chunks = [(2 + jt - 1, 2 if jt == 0 else 0), (2 + jt, None), (2 + jt + 1, 3 if jt == 15 else 1)] + CTXC
                    attn_tile_A(q0, chunks)
                else:
                    if jt == 0:
                        ms, var = list(range(-2, 4)), 0
                    elif jt == 15:
                        ms, var = list(range(12, 18)), 4
                    else:
                        ms = list(range(jt - 2, jt + 3))
                        var = 1 if jt == 1 else (3 if jt == 14 else 2)
                    bias = K.rot("biasb", biasb)
                    K.dma(bias, bbias_d[var][:, :, 512 * half:512 * half + 512])
                    chunks = [(m + 2, ci) for ci, m in enumerate(ms)] + CTXC
                    attn_tile_B(q0, chunks, bias)
            for ct in range(2):
                q0 = T + ct * 128
                if isA:
                    attn_tile_A(q0, CTXC)
                else:
                    attn_tile_B(q0, CTXC, None)

        if stop == 2:
            l0_pass(0, False)
            return finish()
        if stop == 3:
            l0_pass(0)
            return finish()
        if stop == 4:
            l0_pass(1, False)
            return finish()
        if stop == 5:
            l0_pass(1)
            return finish()
        for pid in range(3):
            l0_pass(pid)
        if stop == 6:
            return finish()

        wout = arv(0, [8, 1024])
        K.dma(wout, ewout_d.rearrange("(k p) f -> p k f", p=128), eng="pool")
        for (t0, nt, j) in OWN_BLOCKS + [CTX_BLOCK]:
            for m in range(8):
                acc = K.rot("oacc", [PS[5], PS[6], PS[7]])
                for k in range(8):
                    K.mm(acc[:, :nt], wout[:, k, m * 128:(m + 1) * 128], A36v[:, k, t0:t0 + nt], start=(k == 0), stop=(k == 7))
                K.stt("dve", xT[:, m, t0:t0 + nt], acc[:, :nt], mod_ap(0, 2, m, j), xT[:, m, t0:t0 + nt], MULT, ADD)
        if stop == 7:
            return finish()
        for (t0, nt, j) in OWN_BLOCKS + [CTX_BLOCK]:
            norm_block(xT[:, :, t0:t0 + nt], nt, 0, 1, j, A36v[:, :, t0:t0 + nt])
        mlp(0, OWN_BLOCKS + [CTX_BLOCK], A36v)

        if stop == 8:
            return finish()
        win1 = arv(0, [8, 672])
        hb1 = arv(5376, [8, 512])
        CQN = arv(9472, [3, T])
        CKVN = arv(15616, [2, NQ])
        KR = arv(20224, [NQ], p0=0, p1=32)
        K.dma(win1, owin_d.rearrange("(k p) c -> p k c", p=128), eng="pool")
        rawv = arv(1024, [3, 512], t=ARF)
        rk = arv(0, [2, 512], p0=0, p1=32, t=ARF)
        ropeKv = ropeK_d.rearrange("c p t -> p c t")
        for (t0, nt, j) in OWN_BLOCKS + [CTX_BLOCK]:
            norm_block(xT[:, :, t0:t0 + nt], nt, 1, 0, j, hb1[:, :, :nt])
            groups = [(384, 2, 256.0, 39, CKVN)]
            if j == 0:
                groups = [(0, 3, 384.0, 36, CQN)] + groups
            for (c0, ncn, dn, gcol, dstT) in groups:
                ss = PS[3]
                for c in range(ncn):
                    acc = K.rot("pacc", [PS[0], PS[1], PS[2]])
                    for k in range(8):
                        K.mm(acc[:, :nt], win1[:, k, c0 + c * 128:c0 + (c + 1) * 128], hb1[:, k, :nt], start=(k == 0), stop=(k == 7))
                    sq = K.rot("sqb", sqb)
                    K.actv(sq[:, :nt], acc[:, :nt], AF.Square)
                    K.copy("dve", rawv[:, c, :nt], acc[:, :nt])
                    K.mm(ss[:, :nt], ones128[:], sq[:, :nt], start=(c == 0), stop=(c == ncn - 1))
                rs = K.rot("rsb", rsb)
                K.actv(rs[:, :nt], ss[:, :nt], AF.Sqrt, scale=1.0 / dn, bias=epsv[:, 0:1])
                K.recip(rs[:, :nt], rs[:, :nt])
                for c in range(ncn):
                    K.stt("dve", dstT[:, c, t0:t0 + nt], rawv[:, c, :nt], vecs[:, gcol + c:gcol + c + 1], rs[:, :nt], MULT, MULT)
            acc = K.rot("pacc", [PS[0], PS[1], PS[2]])
            for k in range(8):
                K.mm(acc[0:32, :nt], win1[:, k, 640:672], hb1[:, k, :nt], start=(k == 0), stop=(k == 7))
            sq = K.rot("sqb", sqb)
            raw = K.rot("ftmp", ftmp)
            K.actv(sq[0:32, :nt], acc[0:32, :nt], AF.Square)
            K.copy("dve", raw[0:32, :nt], acc[0:32, :nt])
            ss = PS[3]
            K.mm(ss[0:32, :nt], ones128[0:32, 0:32], sq[0:32, :nt])
            rs = K.rot("rsb", rsb)
            K.actv(rs[0:32, :nt], ss[0:32, :nt], AF.Sqrt, scale=1.0 / 32.0, bias=epsv[0:32, 0:1])
            K.recip(rs[0:32, :nt], rs[0:32, :nt])
            if j == 1:
                K.stt("dve", KR[:, t0:t0 + nt], raw[0:32, :nt], vecs[0:32, 43:44], rs[0:32, :nt], MULT, MULT)
            else:
                K.dma(rk[:, :, :nt], ropeKv[:, :, t0:t0 + nt])
                qn = K.rot("sqb", sqb)
                K.stt("dve", qn[0:32, :nt], raw[0:32, :nt], vecs[0:32, 43:44], rs[0:32, :nt], MULT, MULT)
                sw = PS[4]
                K.mm(sw[0:32, :nt], cmat[0:32, 2, 0:32], qn[0:32, :nt])
                t1 = K.rot("ftmp", ftmp)
                t2 = K.rot("ftmp", ftmp)
                K.tt("pool", t1[0:32, :nt], qn[0:32, :nt], rk[:, 0, :nt], MULT)
                K.tt("dve", t2[0:32, :nt], sw[0:32, :nt], rk[:, 1, :nt], MULT)
                K.tt("pool", KR[:, t0:t0 + nt], t1[0:32, :nt], t2[0:32, :nt], ADD)
        d1 = K.dma(xchg_o[0:256, :].rearrange("(c p) t -> p c t", p=128), CKVN)
        d2 = K.dma(xchg_o[256:288, :], KR)
        if mode == "A":
            K.final += [d1, d2]
            K.final.append(K.dma(cqn_o.rearrange("(c p) t -> p c t", p=128), CQN))
            x1v = x1_o.rearrange("(k p) t -> p k t", p=128)
            for k in range(8):
                K.final.append(K.dma(x1v[:, k, :], xT[:, k, 0:T]))

    if B_:
        CQN = arv(9472, [3, T])
        CKVALL = v3(A36[:, 0:2 * NKEY], 2)
        VVh = arv(0, [66, 128])
        KTh = arv(15616, [NKEY], p0=0, p1=96)
        QTh = arv(24064, [T], p0=0, p1=96)
        ropeQ = arv(26112, [2, T], p0=64, p1=96)
        PT2 = [arv(30208 + i * 1024, [1024]) for i in range(2)]
        wuqh = [arv(32256 + i * 288, [3, 96]) for i in range(2)]
        wukvh = [arv(32832 + i * 256, [2, 128]) for i in range(2)]
        woh = [arv(33344 + i * 1024, [1024], p0=0, p1=64) for i in range(2)]
        OTh = [arv(35392 + i * 512, [512], p0=0, p1=64) for i in range(2)]
        if mode == "B":
            x1v = x1_d.rearrange("(k p) t -> p k t", p=128)
            for k in range(8):
                K.dma(xT[:, k, 0:T], x1v[:, k, :])
            K.dma(CQN, cqn_d.rearrange("(c p) t -> p c t", p=128))
            mo = K.sb("mo", [128, 96], F32)
            K.dma(mo[:], mod1_d)
            K.copy("dve", mod[:, 1, :, :], v3(mo[:, :], 48))
            for which in range(2):
                sck = 1 if which == 0 else 4
                nv = vecs[:, (2 + which) * 8:(2 + which) * 8 + 8]
                K.stt("dve", Amat[:, 1, which, 0, :], mod[:, 1, sck * 8:sck * 8 + 8, 0], 1.0, nv, ADD, MULT)
            for c in range(2):
                for kq in range(4):
                    K.dma(CKVALL[:, c, kq * 2112:(kq + 1) * 2112], kvall_d[c * 128:(c + 1) * 128, kq * 2112:(kq + 1) * 2112])
            K.dma(KTh[64:96, :], kvall_d[256:288, :])
        else:
            gat = K.S.add("pool", lambda e: e.collective_compute("AllGather", ALU.bypass, replica_groups=[[0, 1, 2, 3], [4, 5, 6, 7]],
                                                                 ins=[xchg_o[:, :]], outs=[kvg_i[:, :]]),
                          [xchg_o[:, :]], [kvg_i[:, :]], dma=True)
            for rr in range(4):
                for c in range(2):
                    K.dma(CKVALL[:, c, rr * T:(rr + 1) * T], kvg_i[rr * 288 + c * 128: rr * 288 + (c + 1) * 128, 0:T])
                K.dma(KTh[64:96, rr * T:(rr + 1) * T], kvg_i[rr * 288 + 256: rr * 288 + 288, 0:T])
            for c in range(2):
                K.dma(CKVALL[:, c, 4 * T:NKEY], xchg_o[c * 128:(c + 1) * 128, T:T + CT])
            K.dma(KTh[64:96, 4 * T:NKEY], xchg_o[256:288, T:T + CT])
        K.dma(ropeQ, ropeQ_d.rearrange("c p t -> p c t"))
        K.memset("pool", VVh[:, :, 64:128], 1.0)
        wuqv = wuq_d.rearrange("(c p) f -> p c f", p=128)
        wukvv = wukv_d.rearrange("(c p) f -> p c f", p=128)
        MISC = [PS[6], PS[7]]
        KBLK = [(kb * 512, 512) for kb in range(16)] + [(8192, 256)]
        def load_head(h):
            K.dma(wuqh[h % 2], wuqv[:, :, h * 96:(h + 1) * 96], eng="pool")
            K.dma(wukvh[h % 2], wukvv[:, :, h * 128:(h + 1) * 128], eng="pool")
            K.dma(woh[h % 2], owout_d[h * 64:(h + 1) * 64, :], eng="pool")
        lnb = [arv(2048 + i * 512, [512], t=ARF) for i in range(2)]

        def rstd_from(ss_ap, npart, n, scale):
            ln_ = K.rot("lnb", lnb)
            K.actv(ln_[0:npart, :n], ss_ap, AF.Ln, scale=scale, bias=epsv[0:npart, 0:1])
            rs = K.rot("rsb", rsb)
            K.actv(rs[0:npart, :n], ln_[0:npart, :n], AF.Exp, scale=-0.5)
            return rs

        def tasks_K(h, kbi):
            wkv = wukvh[h % 2]
            (k0, nk) = KBLK[kbi]
            st_ = {}

            def k1():
                acc = K.rot("misc", MISC)
                for c in range(2):
                    K.mm(acc[0:64, :nk], wkv[:, c, 0:64], CKVALL[:, c, k0:k0 + nk], start=(c == 0), stop=(c == 1))
                raw = ftmp[st_["slot"]]
                K.copy("dve", raw[0:64, :nk], acc[0:64, :nk])
                sq = sqb[st_["slot"]]
                K.tt("pool", sq[0:64, :nk], raw[0:64, :nk], raw[0:64, :nk], MULT)
                st_["raw"], st_["sq"] = raw, sq

            def k2():
                ss = K.rot("misc", MISC)
                K.mm(ss[0:64, :nk], ones128[0:64, 0:64], st_["sq"][0:64, :nk])
                rs = rstd_from(ss[0:64, :nk], 64, nk, 1.0 / 64.0)
                K.stt("dve", KTh[0:64, k0:k0 + nk], st_["raw"][0:64, :nk], vecs[0:64, 42:43], rs[0:64, :nk], MULT, MULT)

            return ([k1, k2], [2], st_)

        def tasks_V(h, c4):
            wkv = wukvh[h % 2]
            n4 = min(4, 66 - c4)

            def v1():
                acc = K.rot("misc", MISC)
                for i in range(n4):
                    for c in range(2):
                        K.mm(acc[:, i * 64:(i + 1) * 64], CKVALL[:, c, (c4 + i) * 128:(c4 + i + 1) * 128], wkv[:, c, 64:128],
                             start=(c == 0), stop=(c == 1), skip=True)
                K.copy("dve", VVh[:, c4:c4 + n4, 0:64], v3(acc[:, 0:n4 * 64], n4))

            return ([v1], [], None)

        def tasks_Q(h, qb):
            wq = wuqh[h % 2]
            st_ = {}
            qd = QTh[0:96, qb * 512:(qb + 1) * 512]
            qr = QTh[64:96, qb * 512:(qb + 1) * 512]

            def q1():
                acc = K.rot("misc", MISC)
                for c in range(3):
                    K.mm(acc[0:96, :], wq[:, c, :], CQN[:, c, qb * 512:(qb + 1) * 512], start=(c == 0), stop=(c == 2))
                raw = ftmp[st_["slot"]]
                K.copy("dve", raw[0:96, :], acc[0:96, :])
                sq = sqb[st_["slot"]]
                K.tt("pool", sq[0:96, :], raw[0:96, :], raw[0:96, :], MULT)
                st_["raw"], st_["sq"] = raw, sq

            def q2():
                ss = K.rot("misc", MISC)
                K.mm(ss[0:96, :], blk96[0:96, 0:96], st_["sq"][0:96, :])
                rs = rstd_from(ss[0:96, :], 96, 512, vecs[0:96, 44:45])
                K.stt("dve", qd, st_["raw"][0:96, :], gsc[0:96, 2:3], rs[0:96, :], MULT, MULT)

            def q4():
                sw = K.rot("misc", MISC)
                K.mm(sw[0:32, :], cmat[64:96, 3, 0:32], qr)
                t1 = ftmp[2]
                t2 = K.rot("rsb", rsb)
                K.tt("pool", t1[64:96, :], qr, ropeQ[:, 0, qb * 512:(qb + 1) * 512], MULT)
                K.tt("dve", t2[64:96, :], sw[0:32, :], ropeQ[:, 1, qb * 512:(qb + 1) * 512], MULT)
                K.tt("pool", qr, t1[64:96, :], t2[64:96, :], ADD)

            return ([q1, q2, q4], [2, 2], st_)

        def tasks_fin(O, qb, h, wo):
            st_ = {}

            def f1():
                rec = K.rot("rsb", rsb)
                K.recip(rec[64:128, :], O[64:128, :])
                ot = K.rot("OTh", OTh)
                K.tt("dve", ot, O[0:64, :], rec[64:128, :], MULT)
                st_["ot"] = ot

            def fm(m):
                def f():
                    Y = K.rot("misc", MISC)
                    K.mm(Y, wo[:, m * 128:(m + 1) * 128], st_["ot"])
                    K.stt("dve", xT[:, m, qb * 512:(qb + 1) * 512], Y, mod_ap(1, 2, m, 0), xT[:, m, qb * 512:(qb + 1) * 512], MULT, ADD)
                return f

            return ([f1] + [fm(m) for m in range(8)], [2] + [1] * 7, None)

        tq = []
        tstat = dict(enq=0, done=0, step=0)

        slots_free = [0, 1]

        def enq(task):
            stages, gaps, ctx = task
            tstat["enq"] += 1
            tq.append([stages, gaps, 0, tstat["step"], tstat["enq"], ctx])

        def run_tasks(n):
            ran = 0
            for t_ in list(tq):
                if ran >= n:
                    break
                if t_[3] > tstat["step"]:
                    continue
                ctx = t_[5]
                if ctx is not None and t_[2] == 0:
                    if not slots_free:
                        continue
                    ctx["slot"] = slots_free.pop(0)
                t_[0][t_[2]]()
                ran += 1
                if ctx is not None and t_[2] == 1:
                    slots_free.append(ctx["slot"])
                if t_[2] == len(t_[0]) - 1:
                    tq.remove(t_)
                else:
                    t_[3] = tstat["step"] + t_[1][t_[2]]
                    t_[2] += 1
            tstat["step"] += 1

        def drain_until(mk):
            while any(t_[4] <= mk for t_ in tq):
                run_tasks(4)

        def run_now(task):
            if task[2] is not None:
                task[2]["slot"] = 0
            for f in task[0]:
                f()

        def load_head_qkv(h):
            K.dma(wuqh[h % 2], wuqv[:, :, h * 96:(h + 1) * 96], eng="pool")
            K.dma(wukvh[h % 2], wukvv[:, :, h * 128:(h + 1) * 128], eng="pool")

        def load_head_wo(h):
            K.dma(woh[h % 2], owout_d[h * 64:(h + 1) * 64, :], eng="pool")

        load_head_qkv(0)
        load_head_wo(0)
        for kbi in range(17):
            run_now(tasks_K(0, kbi))
            run_now(tasks_V(0, 4 * kbi))
        for qb in range(4):
            run_now(tasks_Q(0, qb))
        fin3_mark = 0
        for h in range(16):
            wo = woh[h % 2]
            nxt = h + 1 < 16
            for qb in range(4):
                if nxt and qb == 0:
                    load_head_qkv(h + 1)
                if nxt and qb == 1:
                    drain_until(fin3_mark)
                    load_head_wo(h + 1)
                O = K.rot("Ob", [PS[4], PS[5]])
                cur = {}

                def issue(i, qb=qb, cur=cur):
                    S2 = K.rot("S2", [PD[0], PD[1]])
                    cur[i] = S2
                    for ii in range(2):
                        c = 2 * i + ii
                        K.mm(S2[:, ii, :], KTh[0:96, c * 128:(c + 1) * 128], QTh[0:96, qb * 512:(qb + 1) * 512])

                def consume(i, qb=qb, cur=cur, O=O, h=h, nxt=nxt):
                    S2 = cur[i]
                    pt = K.rot("PT2", PT2)
                    K.actv(v3(pt, 2), S2[:, :, :], AF.Exp)
                    for ii in range(2):
                        c = 2 * i + ii
                        K.mm(O, VVh[:, c, :], pt[:, ii * 512:(ii + 1) * 512], start=(c == 0), stop=(c == 65))
                    if nxt and qb == 3 and (i % 2 == 1 or i == 32):
                        j = i // 2
                        enq(tasks_K(h + 1, j))
                        enq(tasks_V(h + 1, 4 * j))
                    run_tasks(3)

                pipeline(33, issue, consume)
                enq(tasks_fin(O, qb, h, wo))
                if nxt:
                    enq(tasks_Q(h + 1, qb))
                if qb == 3:
                    fin3_mark = tstat["enq"]
        while tq:
            run_tasks(4)
        for (t0, nt, j) in OWN_BLOCKS:
            norm_block(xT[:, :, t0:t0 + nt], nt, 1, 1, 0, A36v[:, :, t0:t0 + nt])
        mlp(1, OWN_BLOCKS, A36v)
        ov = out_o.rearrange("(k p) t -> p k t", p=128)
        for k in range(8):
            K.final.append(K.dma(ov[:, k, :], xT[:, k, 0:T]))

    return finish()


_BF = ml_dtypes.bfloat16


def _fm(v):
    return np.ascontiguousarray(np.asarray(v, np.float32).reshape(-1, 128).T)


def _rope_tab(pos, hw):
    inv = (np.float32(10000.0) ** (-np.arange(hw, dtype=np.float32) / np.float32(hw))).astype(np.float32)
    ang = pos.astype(np.float32)[None, :] * inv[:, None]
    return np.cos(ang).astype(np.float32), np.sin(ang).astype(np.float32)


def _perm_signed(blocks, n):
    P = np.zeros((n, n), np.float32)
    for (b, hw) in blocks:
        for i in range(hw):
            P[b + hw + i, b + i] = -1.0
            P[b + i, b + hw + i] = 1.0
    return P


def _host_common(inp):
    f32 = np.float32
    vecs = np.zeros((128, 48), f32)
    vecs[:, 0:8] = _fm(inp["norm_mix"][0])
    vecs[:, 8:16] = _fm(inp["norm_mlp"][0])
    vecs[:, 16:24] = _fm(inp["norm_mix"][1])
    vecs[:, 24:32] = _fm(inp["norm_mlp"][1])
    rep64 = lambda v: np.tile(np.asarray(v, f32).reshape(64), 2)
    vecs[:, 32] = rep64(inp["a_q_norm"][0])
    vecs[:, 33] = rep64(inp["a_k_norm"][0])
    vecs[:, 34] = rep64(inp["b_q_norm"][0])
    vecs[:, 35] = rep64(inp["b_k_norm"][0])
    vecs[:, 36:39] = _fm(inp["o_qa_norm"][0])
    vecs[:, 39:41] = _fm(inp["o_kva_norm"][0])
    vecs[0:64, 41] = inp["o_qn_nope"][0]
    vecs[64:96, 41] = inp["o_qn_rope"][0]
    vecs[:, 42] = rep64(inp["o_kn_nope"][0])
    vecs[:, 43] = np.tile(np.asarray(inp["o_kn_rope"][0], f32), 4)
    vecs[0:64, 44] = 1.0 / 64.0
    vecs[64:128, 44] = 1.0 / 32.0
    cmat = np.zeros((4, 128, 128), f32)
    cmat[0] = np.eye(128, dtype=f32)
    cmat[1] = _perm_signed([(0, 16), (32, 16), (64, 16), (96, 16)], 128)
    p32 = _perm_signed([(0, 8), (16, 8)], 32)
    cmat[2, 0:32, 0:32] = p32
    cmat[3, 64:96, 0:32] = p32
    return dict(vecs=vecs, cmat=cmat.astype(_BF),
                mlp_w1=np.ascontiguousarray(inp["mlp_w1"], f32), mlp_w2=np.ascontiguousarray(inp["mlp_w2"], f32))


def _rope32_tables(tok):
    row, col = tok // 64, tok % 64
    cr, sr = _rope_tab(row, 8)
    cc, sc = _rope_tab(col, 8)
    cos = np.concatenate([cr, cr, cc, cc], 0)
    sin = np.concatenate([sr, sr, sc, sc], 0)
    return np.stack([cos, sin], 0).astype(np.float32)


def _host_A(inp, core):
    f32 = np.float32
    b, r = core // 4, core % 4
    x = inp["x"][b]
    d = {}
    d["xT"] = np.ascontiguousarray(x[r * T:(r + 1) * T].T, f32)
    xh = np.zeros((1024, 2 * HAL), f32)
    if r > 0:
        xh[:, 0:HAL] = x[r * T - HAL:r * T].T
    if r < 3:
        xh[:, HAL:] = x[(r + 1) * T:(r + 1) * T + HAL].T
    d["xhT"] = xh
    d["ctxT"] = np.ascontiguousarray(inp["ctx"][b].T, f32)
    cond = np.zeros((128, 8, 2), f32)
    cond[:, :, 0] = _fm(inp["c"][b])
    cond[:, :, 1] = _fm(inp["c_ctx"])
    d["condT"] = cond.reshape(128, 16)
    d["ada_w"] = np.ascontiguousarray(inp["ada_w"], f32)
    d["adab"] = np.concatenate([_fm(inp["ada_b"][0]), _fm(inp["ada_b"][1])], 1)
    d["e_w_in"] = np.ascontiguousarray(inp["e_w_in"][0], f32)
    d["e_w_out"] = np.ascontiguousarray(inp["e_w_out"][0], f32)
    d["o_w_in"] = np.ascontiguousarray(inp["o_w_in"][0], f32)
    tok = np.arange(r * T - HAL, (r + 1) * T + HAL)
    tokc = np.clip(tok, 0, 8191)
    cr, sr = _rope_tab(tokc // 64, 16)
    cc, sc = _rope_tab(tokc % 64, 16)
    cos64 = np.concatenate([cr, cr, cc, cc], 0)
    sin64 = np.concatenate([sr, sr, sc, sc], 0)
    d["ropeA"] = np.stack([np.tile(cos64, (2, 1)), np.tile(sin64, (2, 1))], 0).astype(f32)
    d["ropeK"] = _rope32_tables(np.arange(r * T, (r + 1) * T))
    kk = np.arange(128)[:, None]
    qq = np.arange(128)[None, :]
    prev = np.where(kk >= qq, 0.0, NEG).astype(f32)
    nxt = np.where(kk <= qq, 0.0, NEG).astype(f32)
    allneg = np.full((128, 128), NEG, f32)
    var = [prev, nxt, allneg if r == 0 else prev, allneg if r == 3 else nxt]
    d["amask"] = np.stack([np.tile(v, (1, 4)) for v in var], 1).astype(_BF)
    rpb = np.asarray(inp["b_rpb"][0], f32)
    bb = np.full((5, 128, 6, 8, 128), NEG, f32)
    k_i = np.arange(128)
    q_i = np.arange(128)
    for vi, jt in enumerate([0, 1, 5, 14, 15]):
        if jt == 0:
            ms = list(range(-2, 4))
        elif jt == 15:
            ms = list(range(12, 18))
        else:
            ms = list(range(jt - 2, jt + 3))
        rr = r if vi != 2 else 1
        gq = 32 * rr + 2 * jt + q_i // 64
        cq = q_i % 64
        start = np.clip(gq - 4, 0, 120)
        c0 = np.clip(cq - 8, 0, 48)
        for ci, m in enumerate(ms):
            gk = 32 * rr + 2 * m + k_i // 64
            ck = k_i % 64
            valid = ((gk[:, None] >= 0) & (gk[:, None] < 128) & (gk[:, None] >= start[None, :]) & (gk[:, None] < start[None, :] + 8)
                     & (ck[:, None] >= c0[None, :]) & (ck[:, None] < c0[None, :] + 16))
            dri = np.clip(gk[:, None] - gq[None, :] + 7, 0, 14)
            dci = np.clip(ck[:, None] - cq[None, :], -15, 15) + 15
            g = rpb[:, dri, dci]
            bb[vi, :, ci, :, :] = np.where(valid[None], g, NEG).transpose(1, 0, 2)
    bb = bb[:, :, :, [0, 2, 1, 3, 4, 6, 5, 7], :]
    d["bbias"] = np.ascontiguousarray(bb).reshape(5, 128, 6, 1024).astype(_BF)
    d["sink"] = np.tile(np.asarray(inp["a_sink"][0], f32)[None, :], (128, 1))
    return d


def _host_B(inp, core):
    f32 = np.float32
    r = core % 4
    d = {}
    d["o_w_uq"] = np.ascontiguousarray(inp["o_w_uq"][0], f32)
    d["o_w_ukv"] = np.ascontiguousarray(inp["o_w_ukv"][0], f32)
    d["o_w_out"] = np.ascontiguousarray(inp["o_w_out"][0], f32)
    d["ropeQ"] = _rope32_tables(np.arange(r * T, (r + 1) * T)).astype(_BF)
    return d


_NC_CACHE = {}


def _get_nc(mode):
    if mode not in _NC_CACHE:
        _NC_CACHE[mode] = build(mode).nc
    return _NC_CACHE[mode]


FUSED = False


def kernel(**inputs):
    inp = {k: np.asarray(v) for k, v in inputs.items()}
    common = _host_common(inp)
    out = np.empty((2, 8192, 1024), np.float32)
    if FUSED:
        maps = []
        for c in range(NCORES):
            m = dict(common)
            m.update(_host_A(inp, c))
            m.update(_host_B(inp, c))
            maps.append(m)
        res = run_bass_kernel_spmd(_get_nc("F"), maps, core_ids=list(range(NCORES)))
        for c in range(NCORES):
            out[c // 4, (c % 4) * T:(c % 4 + 1) * T, :] = np.asarray(res.results[c]["outT"]).T
        return out
    mapsA = []
    for c in range(NCORES):
        m = dict(common)
        m.update(_host_A(inp, c))
        mapsA.append(m)
    resA = run_bass_kernel_spmd(_get_nc("A"), mapsA, core_ids=list(range(NCORES)))
    ra = resA.results
    mapsB = []
    for c in range(NCORES):
        b = c // 4
        m = dict(common)
        m.update(_host_B(inp, c))
        m["x1T"] = np.asarray(ra[c]["x1T"])
        m["cqn"] = np.asarray(ra[c]["cqn"])
        m["mod1"] = np.asarray(ra[c]["mod1"])
        kv = np.concatenate([np.asarray(ra[4 * b + rr]["xchg"])[:, 0:T] for rr in range(4)] + [np.asarray(ra[c]["xchg"])[:, T:T + CT]], axis=1)
        m["kvall"] = np.ascontiguousarray(kv)
        mapsB.append(m)
    resB = run_bass_kernel_spmd(_get_nc("B"), mapsB, core_ids=list(range(NCORES)))
    for c in range(NCORES):
        out[c // 4, (c % 4) * T:(c % 4 + 1) * T, :] = np.asarray(resB.results[c]["outT"]).T
    return out
```

```python
import contextlib
import numpy as np
import ml_dtypes
import concourse.bass as bass
import concourse.mybir as mybir
from concourse.bass_utils import run_bass_kernel_spmd

F32 = mybir.dt.float32
BF16 = mybir.dt.bfloat16
AF = mybir.ActivationFunctionType
ALU = mybir.AluOpType

NCORES = 8
T = 2048
HAL = 256
CT = 256
E = HAL + T + HAL + CT
NQ = T + CT
NKEY = 8192 + CT
EPS = 1e-6
NEG = -30000.0


def _region(ap):
    name = ap.name
    space = str(ap.space)
    dims = ap.ap
    off = int(ap.offset)
    if space == "DRAM":
        lo = off
        hi = off + sum(int(s) * (int(c) - 1) for s, c in dims if int(s) > 0) + 1
        return (name, "DRAM", 0, 1, lo, hi)
    if space == "PSUM":
        fszp = 1
        for d in ap.tensor.shape[1:]:
            fszp *= int(d)
        g0 = off % fszp
        g1 = g0 + sum(int(s) * (int(c) - 1) for s, c in dims[1:] if int(s) > 0) + 1
        return (name, "PSUM", 0, 128, g0 // 512, (g1 - 1) // 512 + 1)
    pstep, pcnt = int(dims[0][0]), int(dims[0][1])
    fsz = 1
    for d in ap.tensor.shape[1:]:
        fsz *= int(d)
    p0 = off // fsz
    f0 = off % fsz
    p1 = p0 + 1 if pstep == 0 else p0 + (pstep // fsz) * (pcnt - 1) + 1
    f1 = f0 + sum(int(s) * (int(c) - 1) for s, c in dims[1:] if int(s) > 0) + 1
    return (name, "SB", p0, p1, f0, f1)


def _overlap(a, b):
    return a[2] < b[3] and b[2] < a[3] and a[4] < b[5] and b[4] < a[5]


def _covers(a, b):
    return a[2] <= b[2] and a[3] >= b[3] and a[4] <= b[4] and a[5] >= b[5]


class Sched:
    ENGS = ("pe", "act", "dve", "pool", "sp")

    def __init__(self, nc, n_dma_sems=12):
        self.nc = nc
        self.ops = []
        self.track = {}
        self.n_dma_sems = n_dma_sems

    def add(self, eng, fn, reads=(), writes=(), dma=False):
        idx = len(self.ops)
        rr = list(dict.fromkeys(_region(a) for a in reads))
        ww = list(dict.fromkeys(_region(a) for a in writes))
        deps = set()
        for r in rr:
            lst = self.track.setdefault(r[0], [])
            psum = r[1] == "PSUM"
            for (box, oi, kind) in lst:
                if (kind == "w" or psum) and _overlap(box, r):
                    deps.add(oi)
        for w in ww:
            lst = self.track.setdefault(w[0], [])
            for (box, oi, kind) in lst:
                if _overlap(box, w):
                    deps.add(oi)
        for r in rr:
            lst = self.track[r[0]]
            if r[1] == "PSUM":
                lst[:] = [t for t in lst if not _covers(r, t[0])]
                lst.append((r, idx, "w"))
            else:
                lst[:] = [t for t in lst if not (t[2] == "r" and t[1] < idx and self.ops[t[1]]["eng"] == eng
                                                 and not self.ops[t[1]]["dma"] and not dma and _covers(r, t[0]))]
                lst.append((r, idx, "r"))
        for w in ww:
            lst = self.track[w[0]]
            lst[:] = [t for t in lst if not _covers(w, t[0])]
            lst.append((w, idx, "w"))
        deps.discard(idx)
        self.ops.append(dict(eng=eng, fn=fn, deps=deps, dma=dma, sig=False, rr=rr, ww=ww))
        return idx

    def dma(self, out, in_, eng="sp"):
        return self.add(eng, lambda e: e.dma_start(out=out, in_=in_), [in_], [out], dma=True)

    def emit(self, final_wait_ops=()):
        nc = self.nc
        ops = self.ops

        def needs_wait(x, y):
            X, Y = ops[x], ops[y]
            if Y["dma"] or X["dma"]:
                return True
            if X["eng"] == Y["eng"]:
                if X["eng"] == "pe":
                    return False
                for w in Y["ww"]:
                    for r in X["rr"]:
                        if w[0] == r[0] and _overlap(w, r):
                            return True
                return False
            return True

        for i, X in enumerate(ops):
            X["wdeps"] = [y for y in X["deps"] if needs_wait(i, y)]
            for y in X["wdeps"]:
                ops[y]["sig"] = True
        for i in final_wait_ops:
            ops[i]["sig"] = True
        cnt = {e: 0 for e in self.ENGS}
        dma_k = {e: 0 for e in self.ENGS}
        dma_semcnt = {}
        for X in ops:
            if X["dma"]:
                q = X["eng"]
                k = dma_k[q]
                dma_k[q] += 1
                s = (q, k % self.n_dma_sems)
                dma_semcnt[s] = dma_semcnt.get(s, 0) + 1
                X["dsem"] = s
                X["dval"] = 16 * dma_semcnt[s]
            elif X["sig"]:
                cnt[X["eng"]] += 1
                X["cnt"] = cnt[X["eng"]]
        with contextlib.ExitStack() as st:
            sems = {e: st.enter_context(nc.semaphore("s_" + e)) for e in ("pe", "act", "dve", "pool")}
            dsems = {}
            for q in self.ENGS:
                for j in range(min(self.n_dma_sems, dma_k[q])):
                    dsems[(q, j)] = st.enter_context(nc.semaphore("d_%s_%d" % (q, j)))
            block = st.enter_context(nc.Block())
            per_eng = {e: [i for i, X in enumerate(ops) if X["eng"] == e] for e in self.ENGS}

            def run_stream(ename, e):
                known = {}

                def wait(key, semh, val):
                    if known.get(key, 0) >= val:
                        return
                    e.wait_ge(semh, val)
                    known[key] = val

                def wait_op(Y):
                    if Y["dma"]:
                        wait(Y["dsem"], dsems[Y["dsem"]], Y["dval"])
                    else:
                        wait(Y["eng"], sems[Y["eng"]], Y["cnt"])

                for i in per_eng[ename]:
                    X = ops[i]
                    for y in sorted(X["wdeps"]):
                        wait_op(ops[y])
                    if X["dma"]:
                        if X["dval"] > 16:
                            wait(X["dsem"], dsems[X["dsem"]], X["dval"] - 16)
                        X["fn"](e).then_inc(dsems[X["dsem"]], 16)
                    else:
                        ins = X["fn"](e)
                        if X["sig"]:
                            ins.then_inc(sems[ename], 1)
                if ename == "sp":
                    for i in final_wait_ops:
                        wait_op(ops[i])

            @block.tensor
            def _(e):
                run_stream("pe", e)

            @block.scalar
            def _(e):
                run_stream("act", e)

            @block.vector
            def _(e):
                run_stream("dve", e)

            @block.gpsimd
            def _(e):
                run_stream("pool", e)

            @block.sync
            def _(e):
                run_stream("sp", e)
        self.stats = dict(n_ops=len(ops), cnt=cnt, dma=dma_k)


class KB:
    def __init__(self, mode):
        self.mode = mode
        self.nc = bass.Bass("TRN2", target_bir_lowering=False)
        self.S = Sched(self.nc)
        self.st = contextlib.ExitStack()
        self.final = []
        self._rot = {}

    def din(self, name, shape, dt=F32):
        return self.nc.dram_tensor(name, list(shape), dt, kind="ExternalInput").ap()

    def dout(self, name, shape, dt=F32):
        return self.nc.dram_tensor(name, list(shape), dt, kind="ExternalOutput").ap()

    def dint(self, name, shape, dt=F32):
        return self.nc.dram_tensor(name, list(shape), dt, kind="Internal").ap()

    def sb(self, name, shape, dt):
        return self.st.enter_context(self.nc.sbuf_tensor(name, list(shape), dt))

    def rot(self, key, lst):
        i = self._rot.get(key, 0)
        self._rot[key] = i + 1
        return lst[i % len(lst)]

    def mm(self, out, lhsT, rhs, start=True, stop=True, skip=False):
        kw = dict(skip_group_check=True) if skip else {}
        return self.S.add("pe", lambda e: e.matmul(out, lhsT=lhsT, rhs=rhs, start=start, stop=stop, **kw),
                          [lhsT, rhs], [out])

    def actv(self, out, in_, func, scale=1.0, bias=None):
        reads = [in_]
        kw = {}
        if isinstance(scale, float) or isinstance(scale, int):
            kw["scale"] = float(scale)
        else:
            kw["scale"] = scale
            reads.append(scale)
        if bias is not None:
            kw["bias"] = bias
            if not isinstance(bias, float):
                reads.append(bias)
        return self.S.add("act", lambda e: e.activation(out=out, in_=in_, func=func, **kw), reads, [out])

    def tt(self, eng, out, in0, in1, op):
        return self.S.add(eng, lambda e: e.tensor_tensor(out=out, in0=in0, in1=in1, op=op), [in0, in1], [out])

    def stt(self, eng, out, in0, scalar, in1, op0, op1):
        reads = [in0, in1]
        if not isinstance(scalar, float):
            reads.append(scalar)
        return self.S.add(eng, lambda e: e.scalar_tensor_tensor(out=out, in0=in0, scalar=scalar, in1=in1, op0=op0, op1=op1),
                          reads, [out])

    def ts(self, eng, out, in0, s1, op0, s2=None, op1=None):
        reads = [in0]
        if not isinstance(s1, float):
            reads.append(s1)
        if s2 is not None and not isinstance(s2, float):
            reads.append(s2)
        if op1 is None:
            return self.S.add(eng, lambda e: e.tensor_scalar(out=out, in0=in0, scalar1=s1, scalar2=None, op0=op0), reads, [out])
        return self.S.add(eng, lambda e: e.tensor_scalar(out=out, in0=in0, scalar1=s1, scalar2=s2, op0=op0, op1=op1), reads, [out])

    def copy(self, eng, out, in_):
        if eng == "act":
            return self.S.add("act", lambda e: e.activation(out=out, in_=in_, func=AF.Copy), [in_], [out])
        return self.S.add(eng, lambda e: e.tensor_copy(out=out, in_=in_), [in_], [out])

    def recip(self, out, in_):
        return self.S.add("dve", lambda e: e.reciprocal(out=out, in_=in_), [in_], [out])

    def memset(self, eng, out, val):
        return self.S.add(eng, lambda e: e.memset(out, val), [], [out])

    def dma(self, out, in_, eng="sp"):
        return self.S.dma(out, in_, eng)


def v3(ap2, a):
    return ap2.rearrange("p (a b) -> p a b", a=a)


MULT, ADD = ALU.mult, ALU.add
NA = 36480


def build(mode, stop=0):
    K = KB(mode)
    nc, S = K.nc, K.S
    A_ = mode in ("A", "F")
    B_ = mode in ("B", "F")

    vecs_d = K.din("vecs", [128, 48])
    cmat_d = K.din("cmat", [4, 128, 128], BF16)
    w1_d = K.din("mlp_w1", [2, 1024, 4096])
    w2_d = K.din("mlp_w2", [2, 4096, 1024])
    if A_:
        xT_d = K.din("xT", [1024, T])
        xh_d = K.din("xhT", [1024, 2 * HAL])
        ctx_d = K.din("ctxT", [1024, CT])
        cond_d = K.din("condT", [128, 16])
        adaw_d = K.din("ada_w", [2, 1024, 6144])
        adab_d = K.din("adab", [128, 96])
        ewin_d = K.din("e_w_in", [1024, 2304])
        ewout_d = K.din("e_w_out", [1024, 1024])
        owin_d = K.din("o_w_in", [1024, 672])
        ropeA_d = K.din("ropeA", [2, 128, 2 * HAL + T])
        ropeK_d = K.din("ropeK", [2, 32, T])
        amask_d = K.din("amask", [128, 4, 512], BF16)
        bbias_d = K.din("bbias", [5, 128, 6, 1024], BF16)
        sink_d = K.din("sink", [128, 8])
    if B_:
        wuq_d = K.din("o_w_uq", [384, 1536])
        wukv_d = K.din("o_w_ukv", [256, 2048])
        owout_d = K.din("o_w_out", [1024, 1024])
        ropeQ_d = K.din("ropeQ", [2, 32, T], BF16)
        out_o = K.dout("outT", [1024, T])
    if mode == "A":
        x1_o = K.dout("x1T", [1024, T])
        cqn_o = K.dout("cqn", [384, T], BF16)
        xchg_o = K.dout("xchg", [288, T + CT], BF16)
        mod1_o = K.dout("mod1", [128, 96])
    if mode == "B":
        x1_d = K.din("x1T", [1024, T])
        cqn_d = K.din("cqn", [384, T], BF16)
        kvall_d = K.din("kvall", [288, NKEY], BF16)
        mod1_d = K.din("mod1", [128, 96])
    if mode == "F":
        xchg_o = K.dint("xchg_i", [288, T + CT], BF16)
        kvg_i = K.dint("kvg_i", [4 * 288, T + CT], BF16)

    if stop:
        dbg_o = K.dout("dbg", [128, NA], BF16)
        dbg36_o = K.dout("dbg36", [128, 8 * NQ], BF16)
        dbgx_o = K.dout("dbgx", [128, 8 * NQ])

    def finish():
        if stop:
            K.final.append(K.dma(dbg_o, AR[:, :]))
            K.final.append(K.dma(dbg36_o, A36[:, :]))
            K.final.append(K.dma(dbgx_o, xT[:, :, :].rearrange("p a b -> p (a b)")))
        S.emit(final_wait_ops=K.final)
        return K

    xT = K.sb("xTs", [128, 8, NQ], F32)
    A36 = K.sb("A36", [128, 8 * NQ], BF16)
    A36v = v3(A36[:, :], 8)
    mod = K.sb("mod", [128, 2, 48, 2], F32)
    Amat = K.sb("Amat", [128, 2, 2, 2, 8], F32)
    vecs = K.sb("vecs_s", [128, 48], F32)
    gsc = K.sb("gsc", [128, 4], F32)
    ones128 = K.sb("ones128", [128, 128], BF16)
    blk = K.sb("blk", [128, 128], BF16)
    blk96 = K.sb("blk96", [128, 128], BF16)
    cmat = K.sb("cmat_s", [128, 4, 128], BF16)
    ident = cmat[:, 0, :]
    sqb = [K.sb("sqb%d" % i, [128, 512], BF16) for i in range(2)]
    ftmp = [K.sb("ftmp%d" % i, [128, 512], F32) for i in range(3)]
    rsb = [K.sb("rsb%d" % i, [128, 512], F32) for i in range(2)]
    AR = K.sb("AR", [128, NA], BF16)
    ARF = K.sb("ARF", [128, 3072], F32)
    PD = [K.st.enter_context(nc.psum_tensor("PD%d" % i, [128, 2, 512], F32)) for i in range(4)]
    PS = [PD[i // 2][:, i % 2, :] for i in range(8)]

    def pipeline(n, issue, consume, look=1):
        for i in range(min(look, n)):
            issue(i)
        for i in range(n):
            if i + look < n:
                issue(i + look)
            consume(i)

    def arv(off, dims, p0=0, p1=128, t=AR):
        n = 1
        for d in dims:
            n *= d
        ap = t[p0:p1, off:off + n]
        if len(dims) == 2:
            ap = ap.rearrange("p (a b) -> p a b", a=dims[0])
        elif len(dims) == 3:
            ap = ap.rearrange("p (a b c) -> p a b c", a=dims[0], b=dims[1])
        return ap

    K.dma(vecs[:], vecs_d)
    K.dma(cmat[:], cmat_d.rearrange("c p q -> p c q"))
    epsv = K.sb("epsv", [128, 1], F32)
    K.memset("dve", epsv[:], EPS)
    K.memset("dve", ones128[:], 1.0)
    K.memset("dve", blk[:], 0.0)
    K.memset("dve", blk[0:64, 0:64], 1.0)
    K.memset("dve", blk[64:128, 64:128], 1.0)
    K.memset("dve", blk96[:], 0.0)
    K.memset("dve", blk96[0:64, 0:64], 1.0)
    K.memset("dve", blk96[64:96, 64:96], 1.0)
    K.ts("dve", gsc[:, 0:1], vecs[:, 32:33], 0.125, MULT)
    K.ts("dve", gsc[:, 1:2], vecs[:, 34:35], 0.125, MULT)
    K.ts("dve", gsc[:, 2:3], vecs[:, 41:42], float(96 ** -0.5), MULT)

    def rstd_to(ss_ap, npart, n, scale):
        rs = K.rot("rsb", rsb)
        K.actv(rs[0:npart, :n], ss_ap, AF.Ln, scale=scale, bias=epsv[0:npart, 0:1])
        K.actv(rs[0:npart, :n], rs[0:npart, :n], AF.Exp, scale=-0.5)
        return rs

    def mod_ap(l, kind, m, j):
        return mod[:, l, kind * 8 + m, j:j + 1]

    def norm_block(src, nt, l, which, j, dst):
        ss = K.rot("ssb", [PS[6], PS[7]])
        for k in range(8):
            sq = K.rot("sqb", sqb)
            K.tt("pool", sq[:, :nt], src[:, k, :], src[:, k, :], MULT)
            K.mm(ss[:, :nt], ones128[:], sq[:, :nt], start=(k == 0), stop=(k == 7))
        rs = rstd_to(ss[:, :nt], 128, nt, 1.0 / 1024.0)
        shift_kind = 0 if which == 0 else 3
        for k in range(8):
            t = K.rot("ftmp", ftmp)
            K.stt("dve", t[:, :nt], src[:, k, :], Amat[:, l, which, j, k:k + 1], rs[:, :nt], MULT, MULT)
            K.actv(dst[:, k, :], t[:, :nt], AF.Identity, scale=1.0, bias=mod_ap(l, shift_kind, k, j))

    def mlp(l, nblocks, h2T, hook=None):
        W1v = w1_d[l].rearrange("(k p) f -> p k f", p=128)
        W2v = w2_d[l].rearrange("(k p) f -> p k f", p=128)
        w1b = [arv(0, [8, 512]), arv(4096, [8, 512])]
        w2b = [arv(8192, [4, 1024]), arv(12288, [4, 1024])]
        ub = [arv(16384, [4, 512]), arv(18432, [4, 512])]
        rb = [arv(20480 + i * 512, [512]) for i in range(2)]
        def load_e8(e8):
            K.dma(w1b[e8 % 2], W1v[:, :, e8 * 512:(e8 + 1) * 512], eng="pool")
            K.dma(w2b[e8 % 2], W2v[:, e8 * 4:(e8 + 1) * 4, :], eng="pool")
        load_e8(0)
        for e8 in range(8):
            w1 = w1b[e8 % 2]
            w2 = w2b[e8 % 2]
            if e8 + 1 < 8:
                load_e8(e8 + 1)
            for bi_, (t0, nt, j) in enumerate(nblocks):
                if hook is not None and bi_ in (0, 2):
                    hook()
                u = K.rot("ub", ub)
                for fc in range(4):
                    acc = K.rot("mlpacc", [PS[0], PS[1], PS[2]])
                    for k in range(8):
                        K.mm(acc[:, :nt], w1[:, k, fc * 128:(fc + 1) * 128], h2T[:, k, t0:t0 + nt], start=(k == 0), stop=(k == 7))
                    r = K.rot("rb", rb)
                    K.actv(r[:, :nt], acc[:, :nt], AF.Relu)
                    K.tt("pool", u[:, fc, :nt], r[:, :nt], r[:, :nt], MULT)
                for m in range(8):
                    acc = K.rot("mlpacc2", [PS[3], PS[4], PS[5]])
                    for fc in range(4):
                        K.mm(acc[:, :nt], w2[:, fc, m * 128:(m + 1) * 128], u[:, fc, :nt], start=(fc == 0), stop=(fc == 3))
                    K.stt("dve", xT[:, m, t0:t0 + nt], acc[:, :nt], mod_ap(l, 5, m, j), xT[:, m, t0:t0 + nt], MULT, ADD)

    OWN_BLOCKS = [(b * 512, 512, 0) for b in range(4)]
    CTX_BLOCK = (T, CT, 1)

    if A_:
        xTv = xT_d.rearrange("(k p) t -> p k t", p=128)
        for k in range(8):
            K.dma(xT[:, k, 0:T], xTv[:, k, :])
        K.dma(xT[:, :, T:NQ], ctx_d.rearrange("(k p) t -> p k t", p=128))
        cond = K.sb("cond", [128, 16], F32)
        silu = K.sb("silu", [128, 16], BF16)
        adab = K.sb("adab_s", [128, 96], F32)
        sinkx = K.sb("sinkx", [128, 8], F32)
        K.dma(cond[:], cond_d)
        K.dma(adab[:], adab_d)
        K.dma(sinkx[:], sink_d)
        K.actv(silu[:], cond[:], AF.Silu)
        K.actv(sinkx[:], sinkx[:], AF.Exp)
        siluv = v3(silu[:, :], 8)

        def ada_dma(l, pc, wb):
            Wv = adaw_d[l].rearrange("(k p) f -> p k f", p=128)
            K.dma(wb, Wv[:, :, pc * 512:(pc + 1) * 512], eng="pool")

        def ada_piece(l, pc, wb):
            acc = K.rot("adacc", [PS[6], PS[7]])
            for m in range(4):
                for k in range(8):
                    K.mm(acc[:, 2 * m:2 * m + 2], wb[:, k, m * 128:(m + 1) * 128], siluv[:, k, :], start=(k == 0), stop=(k == 7), skip=True)
            f0 = pc * 4
            K.tt("dve", mod[:, l, f0:f0 + 4, :], v3(acc[:, 0:8], 4), adab[:, l * 48 + f0:l * 48 + f0 + 4].unsqueeze(2).broadcast_to([128, 4, 2]), ADD)
            if pc in (3, 9):
                which = 0 if pc == 3 else 1
                sck = 1 if which == 0 else 4
                nv = vecs[:, (l * 2 + which) * 8:(l * 2 + which) * 8 + 8]
                for j in range(2):
                    K.stt("dve", Amat[:, l, which, j, :], mod[:, l, sck * 8:sck * 8 + 8, j], 1.0, nv, ADD, MULT)

        adw0 = [arv(0, [8, 512]), arv(4096, [8, 512])]
        ada_dma(0, 0, adw0[0])
        for pc in range(4):
            if pc + 1 < 4:
                ada_dma(0, pc + 1, adw0[(pc + 1) % 2])
            ada_piece(0, pc, adw0[pc % 2])
        adwA = [v3(A36[:, 4 * NQ + i * 4096:4 * NQ + (i + 1) * 4096], 8) for i in range(2)]
        ada_later = dict(l0=list(range(4, 12)), l1=list(range(12)))

        def ada_l0_step():
            if not ada_later["l0"]:
                return
            pc = ada_later["l0"].pop(0)
            if pc == 4:
                ada_dma(0, 4, adwA[0])
            if ada_later["l0"]:
                ada_dma(0, pc + 1, adwA[(pc + 1) % 2])
            ada_piece(0, pc, adwA[pc % 2])

        adwM = [arv(22016 + i * 4096, [8, 512]) for i in range(2)]

        def ada_l1_step():
            if not ada_later["l1"]:
                return
            pc = ada_later["l1"].pop(0)
            if pc == 0:
                ada_dma(1, 0, adwM[0])
            if ada_later["l1"]:
                ada_dma(1, pc + 1, adwM[(pc + 1) % 2])
            ada_piece(1, pc, adwM[pc % 2])
            if not ada_later["l1"] and mode == "A":
                mo = K.sb("mo", [128, 96], F32)
                K.copy("dve", v3(mo[:, :], 48), mod[:, 1, :, :])
                K.final.append(K.dma(mod1_o, mo[:]))

        if stop == 1:
            return finish()

        QT = arv(0, [4, NQ])
        KT = arv(9216, [2, E])
        VV = arv(14848, [22, 4, 128])
        wp = arv(26112, [8, 768])
        hblk = arv(32256, [8, 512])
        PT2 = [arv(26112 + i * 1024, [1024]) for i in range(2)]
        biasb = [arv(28160 + i * 3072, [6, 512]) for i in range(2)]
        amask = arv(34304, [4, 512])
        ropetab = arv(0, [2, 512], t=ARF)
        xhs = arv(1024, [8, 256], t=ARF)
        ewv = ewin_d.rearrange("(k p) c -> p k c", p=128)
        xhv = xh_d.rearrange("(k p) t -> p k t", p=128)
        ropeAv = ropeA_d.rearrange("c p t -> p c t")

        def post_chunk(acc, nt, dst, gain, rope, tabc0):
            sq = K.rot("sqb", sqb)
            raw = K.rot("ftmp", ftmp)
            K.actv(sq[:, :nt], acc[:, :nt], AF.Square)
            K.copy("dve", raw[:, :nt], acc[:, :nt])
            ss = PS[3]
            K.mm(ss[:, :nt], blk[:], sq[:, :nt])
            rs = rstd_to(ss[:, :nt], 128, nt, 1.0 / 64.0)
            if not rope:
                K.stt("dve", dst, raw[:, :nt], gain, rs[:, :nt], MULT, MULT)
                return
            qn = K.rot("sqb", sqb)
            K.stt("dve", qn[:, :nt], raw[:, :nt], gain, rs[:, :nt], MULT, MULT)
            sw = PS[4]
            K.mm(sw[:, :nt], cmat[:, 1, :], qn[:, :nt])
            t1 = K.rot("ftmp", ftmp)
            t2 = K.rot("ftmp", ftmp)
            K.tt("pool", t1[:, :nt], qn[:, :nt], ropetab[:, 0, :nt], MULT)
            K.tt("dve", t2[:, :nt], sw[:, :nt], ropetab[:, 1, :nt], MULT)
            K.tt("pool", dst, t1[:, :nt], t2[:, :nt], ADD)

        def l0_pass(pid, do_attn=True):
            isA = pid == 0
            half = pid - 1
            nQc = 4 if isA else 2
            nKc = 1 if isA else 2
            nV = 2 if isA else 4
            if isA:
                for s in range(2):
                    for jj in range(4):
                        K.dma(wp[:, :, jj * 128 + s * 64: jj * 128 + s * 64 + 64], ewv[:, :, s * 256 + jj * 64: s * 256 + jj * 64 + 64], eng="pool")
                K.dma(wp[:, :, 512:640], ewv[:, :, 512:640], eng="pool")
                K.dma(wp[:, :, 640:768], ewv[:, :, 640:768], eng="pool")
                qcol0, kcol0, vcol0 = 0, 512, 640
                gq, gk = gsc[:, 0:1], vecs[:, 33:34]
            else:
                K.dma(wp[:, :, 0:256], ewv[:, :, 768 + 256 * half: 768 + 256 * half + 256], eng="pool")
                K.dma(wp[:, :, 256:512], ewv[:, :, 1280 + 256 * half: 1280 + 256 * half + 256], eng="pool")
                K.dma(wp[:, :, 512:768], ewv[:, :, 1792 + 256 * half: 1792 + 256 * half + 256], eng="pool")
                qcol0, kcol0, vcol0 = 0, 256, 512
                gq, gk = gsc[:, 1:2], vecs[:, 35:36]
            K.memset("pool", VV[:, :, 0:nV, 64:128], 1.0)
            blocks = []
            blocks.append(("hb", 256, 0, None, 0, 0))
            for b in range(4):
                blocks.append((b, 512, HAL + b * 512, b * 512, 0, HAL + b * 512))
            blocks.append(("ha", 256, HAL + T, None, 0, HAL + T))
            blocks.append(("ctx", 256, 2 * HAL + T, T, 1, None))
            for (bid, nt, e0, q0, j, rc0) in blocks:
                if bid == "hb":
                    K.dma(xhs, xhv[:, :, 0:256])
                    src = xhs
                elif bid == "ha":
                    K.dma(xhs, xhv[:, :, 256:512])
                    src = xhs
                elif bid == "ctx":
                    src = xT[:, :, T:NQ]
                else:
                    src = xT[:, :, bid * 512:(bid + 1) * 512]
                rope = isA and (rc0 is not None)
                if rope:
                    K.dma(ropetab[:, :, :nt], ropeAv[:, :, rc0:rc0 + nt])
                norm_block(src, nt, 0, 0, j, hblk[:, :, :nt])
                if q0 is not None:
                    for qc in range(nQc):
                        acc = K.rot("pacc", [PS[0], PS[1], PS[2]])
                        for k in range(8):
                            K.mm(acc[:, :nt], wp[:, k, qcol0 + qc * 128: qcol0 + (qc + 1) * 128], hblk[:, k, :nt], start=(k == 0), stop=(k == 7))
                        post_chunk(acc, nt, QT[:, qc, q0:q0 + nt], gq, rope, rc0)
                for kc in range(nKc):
                    acc = K.rot("pacc", [PS[0], PS[1], PS[2]])
                    for k in range(8):
                        K.mm(acc[:, :nt], wp[:, k, kcol0 + kc * 128: kcol0 + (kc + 1) * 128], hblk[:, k, :nt], start=(k == 0), stop=(k == 7))
                    post_chunk(acc, nt, KT[:, kc, e0:e0 + nt], gk, rope, rc0)
                for tt_ in range(nt // 128):
                    acc = PS[5]
                    for k in range(8):
                        K.mm(acc[:, 0:nV * 64], hblk[:, k, tt_ * 128:(tt_ + 1) * 128], wp[:, k, vcol0:vcol0 + nV * 64], start=(k == 0), stop=(k == 7))
                    ec = e0 // 128 + tt_
                    K.copy("act", VV[:, ec, 0:nV, 0:64], v3(acc[:, 0:nV * 64], nV))
                if isA:
                    ada_l0_step()

            while isA and ada_later["l0"]:
                ada_l0_step()
            if not do_attn:
                return
            if isA:
                K.dma(amask, amask_d)

            def finalize(O, heads_hc, sink_cols):
                rec = K.rot("rsb", rsb)
                if sink_cols is not None:
                    for hh in range(4):
                        K.ts("dve", rec[64:128, hh * 128:(hh + 1) * 128], O[64:128, hh * 128:(hh + 1) * 128], sinkx[64:128, sink_cols[hh]:sink_cols[hh] + 1], ADD)
                    K.actv(rec[64:128, :], rec[64:128, :], AF.Ln)
                else:
                    K.actv(rec[64:128, :], O[64:128, :], AF.Ln)
                K.actv(rec[64:128, :], rec[64:128, :], AF.Exp, scale=-1.0)
                return rec

            def attn_tile_A(q0, chunks):
                sts = [chunks[i:i + 2] for i in range(0, len(chunks), 2)]
                for g in range(2):
                    O = K.rot("Ob", [PS[4], PS[5]])
                    cur = {}

                    def issue(i, g=g, cur=cur):
                        S2 = K.rot("S2", [PD[0], PD[1]])
                        cur[i] = S2
                        for ii, (ec, mv) in enumerate(sts[i]):
                            if mv is not None:
                                K.mm(S2[:, ii, :], ident, amask[:, mv, :], start=True, stop=False, skip=True)
                            K.mm(S2[:, ii, :], KT[64 * g:64 * g + 64, 0, ec * 128:(ec + 1) * 128], QT[64 * g:64 * g + 64, 0:4, q0:q0 + 128],
                                 start=(mv is None), stop=True, skip=True)

                    def consume(i, g=g, cur=cur, O=O):
                        S2 = cur[i]
                        n = len(sts[i])
                        pt = K.rot("PT2", PT2)
                        K.actv(v3(pt, 2)[:, 0:n, :], S2[:, 0:n, :], AF.Exp)
                        for ii, (ec, mv) in enumerate(sts[i]):
                            first = (i == 0 and ii == 0)
                            last = (i == len(sts) - 1 and ii == n - 1)
                            K.mm(O, VV[:, ec, g, :], pt[:, ii * 512:(ii + 1) * 512], start=first, stop=last)

                    pipeline(len(sts), issue, consume)
                    rec = finalize(O, None, [4 * g + hh for hh in range(4)])
                    for hh in range(4):
                        hc = 4 * g + hh
                        dst = A36v[64 * (hc % 2):64 * (hc % 2) + 64, hc // 2, q0:q0 + 128]
                        K.tt("dve", dst, O[0:64, hh * 128:(hh + 1) * 128], rec[64:128, hh * 128:(hh + 1) * 128], MULT)

            def attn_tile_B(q0, chunks, bias):
                O = K.rot("Ob", [PS[4], PS[5]])
                K.memset("dve", O, 0.0)
                cur = {}

                def issue(i):
                    (ec, bi) = chunks[i]
                    S2 = K.rot("S2", [PD[0], PD[1]])
                    cur[i] = S2
                    for s_ in range(2):
                        if bi is not None:
                            K.mm(S2[:, s_, 0:256], ident, bias[:, bi, s_ * 256:(s_ + 1) * 256], start=True, stop=False, skip=True)
                        for cc in range(2):
                            K.mm(S2[:, s_, cc * 128:(cc + 1) * 128], KT[64 * s_:64 * s_ + 64, cc, ec * 128:(ec + 1) * 128],
                                 QT[64 * s_:64 * s_ + 64, cc, q0:q0 + 128], start=(bi is None), stop=True, skip=True)

                def consume(i):
                    (ec, bi) = chunks[i]
                    S2 = cur[i]
                    pt = K.rot("PT2", PT2)
                    K.actv(v3(pt[:, 0:512], 2), S2[:, :, 0:256], AF.Exp)
                    for hh in range(4):
                        pos = (hh % 2) * 2 + hh // 2
                        K.mm(O[:, hh * 128:(hh + 1) * 128], VV[:, ec, hh, :], pt[:, pos * 128:(pos + 1) * 128], start=False, stop=False, skip=True)

                pipeline(len(chunks), issue, consume)
                rec = finalize(O, None, None)
                for hh in range(4):
                    hc = 8 + 4 * half + hh
                    dst = A36v[64 * (hc % 2):64 * (hc % 2) + 64, hc // 2, q0:q0 + 128]
                    K.tt("dve", dst, O[0:64, hh * 128:(hh + 1) * 128], rec[64:128, hh * 128:(hh + 1) * 128], MULT)

            CTXC = [(20, None), (21, None)]
            for jt in range(16):
                q0 = jt * 128
                if isA:
                    chunks = [(2 + jt - 1, 2 if jt == 0 else 0), (2 + jt, None), (2 + jt + 1, 3 if jt == 15 else 1)] + CTXC
                    attn_tile_A(q0, chunks)
                else:
                    if jt == 0:
                        ms, var = list(range(-2, 4)), 0
                    elif jt == 15:
                        ms, var = list(range(12, 18)), 4
                    else:
                        ms = list(range(jt - 2, jt + 3))
                        var = 1 if jt == 1 else (3 if jt == 14 else 2)
                    bias = K.rot("biasb", biasb)
                    K.dma(bias, bbias_d[var][:, :, 512 * half:512 * half + 512])
                    chunks = [(m + 2, ci) for ci, m in enumerate(ms)] + CTXC
                    attn_tile_B(q0, chunks, bias)
            for ct in range(2):
                q0 = T + ct * 128
                if isA:
                    attn_tile_A(q0, CTXC)
                else:
                    attn_tile_B(q0, CTXC, None)

        if stop == 2:
            l0_pass(0, False)
            return finish()
        if stop == 3:
            l0_pass(0)
            return finish()
        if stop == 4:
            l0_pass(1, False)
            return finish()
        if stop == 5:
            l0_pass(1)
            return finish()
        for pid in range(3):
            l0_pass(pid)
        if stop == 6:
            return finish()

        wout = arv(0, [8, 1024])
        K.dma(wout, ewout_d.rearrange("(k p) f -> p k f", p=128), eng="pool")
        for (t0, nt, j) in OWN_BLOCKS + [CTX_BLOCK]:
            for m in range(8):
                acc = K.rot("oacc", [PS[5], PS[6], PS[7]])
                for k in range(8):
                    K.mm(acc[:, :nt], wout[:, k, m * 128:(m + 1) * 128], A36v[:, k, t0:t0 + nt], start=(k == 0), stop=(k == 7))
                K.stt("dve", xT[:, m, t0:t0 + nt], acc[:, :nt], mod_ap(0, 2, m, j), xT[:, m, t0:t0 + nt], MULT, ADD)
        if stop == 7:
            return finish()
        for (t0, nt, j) in OWN_BLOCKS + [CTX_BLOCK]:
            norm_block(xT[:, :, t0:t0 + nt], nt, 0, 1, j, A36v[:, :, t0:t0 + nt])
        mlp(0, OWN_BLOCKS + [CTX_BLOCK], A36v, hook=ada_l1_step)
        while ada_later["l1"]:
            ada_l1_step()

        if stop == 8:
            return finish()
        win1 = arv(0, [8, 672])
        hb1 = arv(5376, [8, 512])
        CQN = arv(9472, [3, T])
        CKVN = arv(15616, [2, NQ])
        KR = arv(20224, [NQ], p0=0, p1=32)
        K.dma(win1, owin_d.rearrange("(k p) c -> p k c", p=128), eng="pool")
        rawv = arv(1024, [3, 512], t=ARF)
        rk = arv(0, [2, 512], p0=0, p1=32, t=ARF)
        ropeKv = ropeK_d.rearrange("c p t -> p c t")
        for (t0, nt, j) in OWN_BLOCKS + [CTX_BLOCK]:
            norm_block(xT[:, :, t0:t0 + nt], nt, 1, 0, j, hb1[:, :, :nt])
            groups = [(384, 2, 256.0, 39, CKVN)]
            if j == 0:
                groups = [(0, 3, 384.0, 36, CQN)] + groups
            for (c0, ncn, dn, gcol, dstT) in groups:
                ss = PS[3]
                for c in range(ncn):
                    acc = K.rot("pacc", [PS[0], PS[1], PS[2]])
                    for k in range(8):
                        K.mm(acc[:, :nt], win1[:, k, c0 + c * 128:c0 + (c + 1) * 128], hb1[:, k, :nt], start=(k == 0), stop=(k == 7))
                    sq = K.rot("sqb", sqb)
                    K.actv(sq[:, :nt], acc[:, :nt], AF.Square)
                    K.copy("dve", rawv[:, c, :nt], acc[:, :nt])
                    K.mm(ss[:, :nt], ones128[:], sq[:, :nt], start=(c == 0), stop=(c == ncn - 1))
                rs = rstd_to(ss[:, :nt], 128, nt, 1.0 / dn)
                for c in range(ncn):
                    K.stt("dve", dstT[:, c, t0:t0 + nt], rawv[:, c, :nt], vecs[:, gcol + c:gcol + c + 1], rs[:, :nt], MULT, MULT)
            acc = K.rot("pacc", [PS[0], PS[1], PS[2]])
            for k in range(8):
                K.mm(acc[0:32, :nt], win1[:, k, 640:672], hb1[:, k, :nt], start=(k == 0), stop=(k == 7))
            sq = K.rot("sqb", sqb)
            raw = K.rot("ftmp", ftmp)
            K.actv(sq[0:32, :nt], acc[0:32, :nt], AF.Square)
            K.copy("dve", raw[0:32, :nt], acc[0:32, :nt])
            ss = PS[3]
            K.mm(ss[0:32, :nt], ones128[0:32, 0:32], sq[0:32, :nt])
            rs = rstd_to(ss[0:32, :nt], 32, nt, 1.0 / 32.0)
            if j == 1:
                K.stt("dve", KR[:, t0:t0 + nt], raw[0:32, :nt], vecs[0:32, 43:44], rs[0:32, :nt], MULT, MULT)
            else:
                K.dma(rk[:, :, :nt], ropeKv[:, :, t0:t0 + nt])
                qn = K.rot("sqb", sqb)
                K.stt("dve", qn[0:32, :nt], raw[0:32, :nt], vecs[0:32, 43:44], rs[0:32, :nt], MULT, MULT)
                sw = PS[4]
                K.mm(sw[0:32, :nt], cmat[0:32, 2, 0:32], qn[0:32, :nt])
                t1 = K.rot("ftmp", ftmp)
                t2 = K.rot("ftmp", ftmp)
                K.tt("pool", t1[0:32, :nt], qn[0:32, :nt], rk[:, 0, :nt], MULT)
                K.tt("dve", t2[0:32, :nt], sw[0:32, :nt], rk[:, 1, :nt], MULT)
                K.tt("pool", KR[:, t0:t0 + nt], t1[0:32, :nt], t2[0:32, :nt], ADD)
        d1 = K.dma(xchg_o[0:256, :].rearrange("(c p) t -> p c t", p=128), CKVN)
        d2 = K.dma(xchg_o[256:288, :], KR)
        if mode == "A":
            K.final += [d1, d2]
            K.final.append(K.dma(cqn_o.rearrange("(c p) t -> p c t", p=128), CQN))
            x1v = x1_o.rearrange("(k p) t -> p k t", p=128)
            for k in range(8):
                K.final.append(K.dma(x1v[:, k, :], xT[:, k, 0:T]))

    if B_:
        CQN = arv(9472, [3, T])
        CKVALL = v3(A36[:, 0:2 * NKEY], 2)
        VVh = arv(0, [66, 128])
        KTh = arv(15616, [NKEY], p0=0, p1=96)
        QTh = arv(24064, [T], p0=0, p1=96)
        ropeQ = arv(26112, [2, T], p0=64, p1=96)
        PT2 = [arv(30208 + i * 1024, [1024]) for i in range(2)]
        wuqh = [arv(32256 + i * 288, [3, 96]) for i in range(2)]
        wukvh = [arv(32832 + i * 256, [2, 128]) for i in range(2)]
        woh = [arv(33344 + i * 1024, [1024], p0=0, p1=64) for i in range(2)]
        OTh = [arv(35392 + i * 512, [512], p0=0, p1=64) for i in range(2)]
        if mode == "B":
            x1v = x1_d.rearrange("(k p) t -> p k t", p=128)
            for k in range(8):
                K.dma(xT[:, k, 0:T], x1v[:, k, :])
            K.dma(CQN, cqn_d.rearrange("(c p) t -> p c t", p=128))
            mo = K.sb("mo", [128, 96], F32)
            K.dma(mo[:], mod1_d)
            K.copy("dve", mod[:, 1, :, :], v3(mo[:, :], 48))
            for which in range(2):
                sck = 1 if which == 0 else 4
                nv = vecs[:, (2 + which) * 8:(2 + which) * 8 + 8]
                K.stt("dve", Amat[:, 1, which, 0, :], mod[:, 1, sck * 8:sck * 8 + 8, 0], 1.0, nv, ADD, MULT)
            for c in range(2):
                for kq in range(4):
                    K.dma(CKVALL[:, c, kq * 2112:(kq + 1) * 2112], kvall_d[c * 128:(c + 1) * 128, kq * 2112:(kq + 1) * 2112])
            K.dma(KTh[64:96, :], kvall_d[256:288, :])
        else:
            gat = K.S.add("pool", lambda e: e.collective_compute("AllGather", ALU.bypass, replica_groups=[[0, 1, 2, 3], [4, 5, 6, 7]],
                                                                 ins=[xchg_o[:, :]], outs=[kvg_i[:, :]]),
                          [xchg_o[:, :]], [kvg_i[:, :]], dma=True)
            for rr in range(4):
                for c in range(2):
                    K.dma(CKVALL[:, c, rr * T:(rr + 1) * T], kvg_i[rr * 288 + c * 128: rr * 288 + (c + 1) * 128, 0:T])
                K.dma(KTh[64:96, rr * T:(rr + 1) * T], kvg_i[rr * 288 + 256: rr * 288 + 288, 0:T])
            for c in range(2):
                K.dma(CKVALL[:, c, 4 * T:NKEY], xchg_o[c * 128:(c + 1) * 128, T:T + CT])
            K.dma(KTh[64:96, 4 * T:NKEY], xchg_o[256:288, T:T + CT])
        K.dma(ropeQ, ropeQ_d.rearrange("c p t -> p c t"))
        K.memset("pool", VVh[:, :, 64:128], 1.0)
        wuqv = wuq_d.rearrange("(c p) f -> p c f", p=128)
        wukvv = wukv_d.rearrange("(c p) f -> p c f", p=128)
        MISC = [PS[6], PS[7]]
        KBLK = [(kb * 512, 512) for kb in range(16)] + [(8192, 256)]
        def load_head(h):
            K.dma(wuqh[h % 2], wuqv[:, :, h * 96:(h + 1) * 96], eng="pool")
            K.dma(wukvh[h % 2], wukvv[:, :, h * 128:(h + 1) * 128], eng="pool")
            K.dma(woh[h % 2], owout_d[h * 64:(h + 1) * 64, :], eng="pool")
        def tasks_K(h, kbi):
            wkv = wukvh[h % 2]
            (k0, nk) = KBLK[kbi]
            st_ = {}

            def k1():
                acc = K.rot("misc", MISC)
                for c in range(2):
                    K.mm(acc[0:64, :nk], wkv[:, c, 0:64], CKVALL[:, c, k0:k0 + nk], start=(c == 0), stop=(c == 1))
                raw = ftmp[st_["slot"]]
                K.copy("dve", raw[0:64, :nk], acc[0:64, :nk])
                sq = sqb[st_["slot"]]
                K.tt("pool", sq[0:64, :nk], raw[0:64, :nk], raw[0:64, :nk], MULT)
                st_["raw"], st_["sq"] = raw, sq

            def k2():
                ss = K.rot("misc", MISC)
                K.mm(ss[0:64, :nk], ones128[0:64, 0:64], st_["sq"][0:64, :nk])
                rs = rstd_to(ss[0:64, :nk], 64, nk, 1.0 / 64.0)
                K.stt("dve", KTh[0:64, k0:k0 + nk], st_["raw"][0:64, :nk], vecs[0:64, 42:43], rs[0:64, :nk], MULT, MULT)

            return ([k1, k2], [2], st_)

        def tasks_V(h, c4):
            wkv = wukvh[h % 2]
            n4 = min(4, 66 - c4)

            def v1():
                acc = K.rot("misc", MISC)
                for i in range(n4):
                    for c in range(2):
                        K.mm(acc[:, i * 64:(i + 1) * 64], CKVALL[:, c, (c4 + i) * 128:(c4 + i + 1) * 128], wkv[:, c, 64:128],
                             start=(c == 0), stop=(c == 1), skip=True)
                K.copy("dve", VVh[:, c4:c4 + n4, 0:64], v3(acc[:, 0:n4 * 64], n4))

            return ([v1], [], None)

        def tasks_Q(h, qb):
            wq = wuqh[h % 2]
            st_ = {}
            qd = QTh[0:96, qb * 512:(qb + 1) * 512]
            qr = QTh[64:96, qb * 512:(qb + 1) * 512]

            def q1():
                acc = K.rot("misc", MISC)
                for c in range(3):
                    K.mm(acc[0:96, :], wq[:, c, :], CQN[:, c, qb * 512:(qb + 1) * 512], start=(c == 0), stop=(c == 2))
                raw = ftmp[st_["slot"]]
                K.copy("dve", raw[0:96, :], acc[0:96, :])
                sq = sqb[st_["slot"]]
                K.tt("pool", sq[0:96, :], raw[0:96, :], raw[0:96, :], MULT)
                st_["raw"], st_["sq"] = raw, sq

            def q2():
                ss = K.rot("misc", MISC)
                K.mm(ss[0:96, :], blk96[0:96, 0:96], st_["sq"][0:96, :])
                rs = rstd_to(ss[0:96, :], 96, 512, vecs[0:96, 44:45])
                K.stt("dve", qd, st_["raw"][0:96, :], gsc[0:96, 2:3], rs[0:96, :], MULT, MULT)

            def q4():
                sw = K.rot("misc", MISC)
                K.mm(sw[0:32, :], cmat[64:96, 3, 0:32], qr)
                t1 = ftmp[2]
                t2 = K.rot("rsb", rsb)
                K.tt("pool", t1[64:96, :], qr, ropeQ[:, 0, qb * 512:(qb + 1) * 512], MULT)
                K.tt("dve", t2[64:96, :], sw[0:32, :], ropeQ[:, 1, qb * 512:(qb + 1) * 512], MULT)
                K.tt("pool", qr, t1[64:96, :], t2[64:96, :], ADD)

            return ([q1, q2, q4], [2, 2], st_)

        def tasks_fin(O, qb, h, wo):
            st_ = {}

            def f1():
                rec = K.rot("rsb", rsb)
                K.recip(rec[64:128, :], O[64:128, :])
                ot = K.rot("OTh", OTh)
                K.tt("dve", ot, O[0:64, :], rec[64:128, :], MULT)
                st_["ot"] = ot

            def fm(m):
                def f():
                    Y = K.rot("misc", MISC)
                    K.mm(Y, wo[:, m * 128:(m + 1) * 128], st_["ot"])
                    K.stt("dve", xT[:, m, qb * 512:(qb + 1) * 512], Y, mod_ap(1, 2, m, 0), xT[:, m, qb * 512:(qb + 1) * 512], MULT, ADD)
                return f

            return ([f1] + [fm(m) for m in range(8)], [2] + [1] * 7, None)

        tq = []
        tstat = dict(enq=0, done=0, step=0)

        slots_free = [0, 1]

        def enq(task):
            stages, gaps, ctx = task
            tstat["enq"] += 1
            tq.append([stages, gaps, 0, tstat["step"], tstat["enq"], ctx])

        def run_tasks(n):
            ran = 0
            for t_ in list(tq):
                if ran >= n:
                    break
                if t_[3] > tstat["step"]:
                    continue
                ctx = t_[5]
                if ctx is not None and t_[2] == 0:
                    if not slots_free:
                        continue
                    ctx["slot"] = slots_free.pop(0)
                t_[0][t_[2]]()
                ran += 1
                if ctx is not None and t_[2] == 1:
                    slots_free.append(ctx["slot"])
                if t_[2] == len(t_[0]) - 1:
                    tq.remove(t_)
                else:
                    t_[3] = tstat["step"] + t_[1][t_[2]]
                    t_[2] += 1
            tstat["step"] += 1

        def drain_until(mk):
            while any(t_[4] <= mk for t_ in tq):
                run_tasks(4)

        def run_now(task):
            if task[2] is not None:
                task[2]["slot"] = 0
            for f in task[0]:
                f()

        def load_head_qkv(h):
            K.dma(wuqh[h % 2], wuqv[:, :, h * 96:(h + 1) * 96], eng="pool")
            K.dma(wukvh[h % 2], wukvv[:, :, h * 128:(h + 1) * 128], eng="pool")

        def load_head_wo(h):
            K.dma(woh[h % 2], owout_d[h * 64:(h + 1) * 64, :], eng="pool")

        load_head_qkv(0)
        load_head_wo(0)
        for kbi in range(17):
            run_now(tasks_K(0, kbi))
            run_now(tasks_V(0, 4 * kbi))
        for qb in range(4):
            run_now(tasks_Q(0, qb))
        fin3_mark = 0
        for h in range(16):
            wo = woh[h % 2]
            nxt = h + 1 < 16
            for qb in range(4):
                if nxt and qb == 0:
                    load_head_qkv(h + 1)
                if nxt and qb == 1:
                    drain_until(fin3_mark)
                    load_head_wo(h + 1)
                O = K.rot("Ob", [PS[4], PS[5]])
                cur = {}

                def issue(i, qb=qb, cur=cur):
                    S2 = K.rot("S2", [PD[0], PD[1]])
                    cur[i] = S2
                    for ii in range(2):
                        c = 2 * i + ii
                        K.mm(S2[:, ii, :], KTh[0:96, c * 128:(c + 1) * 128], QTh[0:96, qb * 512:(qb + 1) * 512])

                def consume(i, qb=qb, cur=cur, O=O, h=h, nxt=nxt):
                    S2 = cur[i]
                    pt = K.rot("PT2", PT2)
                    K.actv(v3(pt, 2), S2[:, :, :], AF.Exp)
                    for ii in range(2):
                        c = 2 * i + ii
                        K.mm(O, VVh[:, c, :], pt[:, ii * 512:(ii + 1) * 512], start=(c == 0), stop=(c == 65))
                    if nxt and qb == 3 and (i % 2 == 1 or i == 32):
                        j = i // 2
                        enq(tasks_K(h + 1, j))
                        enq(tasks_V(h + 1, 4 * j))
                    run_tasks(3)

                pipeline(33, issue, consume)
                enq(tasks_fin(O, qb, h, wo))
                if nxt:
                    enq(tasks_Q(h + 1, qb))
                if qb == 3:
                    fin3_mark = tstat["enq"]
        while tq:
            run_tasks(4)
        for (t0, nt, j) in OWN_BLOCKS:
            norm_block(xT[:, :, t0:t0 + nt], nt, 1, 1, 0, A36v[:, :, t0:t0 + nt])
        mlp(1, OWN_BLOCKS, A36v)
        ov = out_o.rearrange("(k p) t -> p k t", p=128)
        for k in range(8):
            K.final.append(K.dma(ov[:, k, :], xT[:, k, 0:T]))

    return finish()


_BF = ml_dtypes.bfloat16


def _fm(v):
    return np.ascontiguousarray(np.asarray(v, np.float32).reshape(-1, 128).T)


def _rope_tab(pos, hw):
    inv = (np.float32(10000.0) ** (-np.arange(hw, dtype=np.float32) / np.float32(hw))).astype(np.float32)
    ang = pos.astype(np.float32)[None, :] * inv[:, None]
    return np.cos(ang).astype(np.float32), np.sin(ang).astype(np.float32)


def _perm_signed(blocks, n):
    P = np.zeros((n, n), np.float32)
    for (b, hw) in blocks:
        for i in range(hw):
            P[b + hw + i, b + i] = -1.0
            P[b + i, b + hw + i] = 1.0
    return P


def _host_common(inp):
    f32 = np.float32
    vecs = np.zeros((128, 48), f32)
    vecs[:, 0:8] = _fm(inp["norm_mix"][0])
    vecs[:, 8:16] = _fm(inp["norm_mlp"][0])
    vecs[:, 16:24] = _fm(inp["norm_mix"][1])
    vecs[:, 24:32] = _fm(inp["norm_mlp"][1])
    rep64 = lambda v: np.tile(np.asarray(v, f32).reshape(64), 2)
    vecs[:, 32] = rep64(inp["a_q_norm"][0])
    vecs[:, 33] = rep64(inp["a_k_norm"][0])
    vecs[:, 34] = rep64(inp["b_q_norm"][0])
    vecs[:, 35] = rep64(inp["b_k_norm"][0])
    vecs[:, 36:39] = _fm(inp["o_qa_norm"][0])
    vecs[:, 39:41] = _fm(inp["o_kva_norm"][0])
    vecs[0:64, 41] = inp["o_qn_nope"][0]
    vecs[64:96, 41] = inp["o_qn_rope"][0]
    vecs[:, 42] = rep64(inp["o_kn_nope"][0])
    vecs[:, 43] = np.tile(np.asarray(inp["o_kn_rope"][0], f32), 4)
    vecs[0:64, 44] = 1.0 / 64.0
    vecs[64:128, 44] = 1.0 / 32.0
    cmat = np.zeros((4, 128, 128), f32)
    cmat[0] = np.eye(128, dtype=f32)
    cmat[1] = _perm_signed([(0, 16), (32, 16), (64, 16), (96, 16)], 128)
    p32 = _perm_signed([(0, 8), (16, 8)], 32)
    cmat[2, 0:32, 0:32] = p32
    cmat[3, 64:96, 0:32] = p32
    return dict(vecs=vecs, cmat=cmat.astype(_BF),
                mlp_w1=np.ascontiguousarray(inp["mlp_w1"], f32), mlp_w2=np.ascontiguousarray(inp["mlp_w2"], f32))


def _rope32_tables(tok):
    row, col = tok // 64, tok % 64
    cr, sr = _rope_tab(row, 8)
    cc, sc = _rope_tab(col, 8)
    cos = np.concatenate([cr, cr, cc, cc], 0)
    sin = np.concatenate([sr, sr, sc, sc], 0)
    return np.stack([cos, sin], 0).astype(np.float32)


def _host_A(inp, core):
    f32 = np.float32
    b, r = core // 4, core % 4
    x = inp["x"][b]
    d = {}
    d["xT"] = np.ascontiguousarray(x[r * T:(r + 1) * T].T, f32)
    xh = np.zeros((1024, 2 * HAL), f32)
    if r > 0:
        xh[:, 0:HAL] = x[r * T - HAL:r * T].T
    if r < 3:
        xh[:, HAL:] = x[(r + 1) * T:(r + 1) * T + HAL].T
    d["xhT"] = xh
    d["ctxT"] = np.ascontiguousarray(inp["ctx"][b].T, f32)
    cond = np.zeros((128, 8, 2), f32)
    cond[:, :, 0] = _fm(inp["c"][b])
    cond[:, :, 1] = _fm(inp["c_ctx"])
    d["condT"] = cond.reshape(128, 16)
    d["ada_w"] = np.ascontiguousarray(inp["ada_w"], f32)
    d["adab"] = np.concatenate([_fm(inp["ada_b"][0]), _fm(inp["ada_b"][1])], 1)
    d["e_w_in"] = np.ascontiguousarray(inp["e_w_in"][0], f32)
    d["e_w_out"] = np.ascontiguousarray(inp["e_w_out"][0], f32)
    d["o_w_in"] = np.ascontiguousarray(inp["o_w_in"][0], f32)
    tok = np.arange(r * T - HAL, (r + 1) * T + HAL)
    tokc = np.clip(tok, 0, 8191)
    cr, sr = _rope_tab(tokc // 64, 16)
    cc, sc = _rope_tab(tokc % 64, 16)
    cos64 = np.concatenate([cr, cr, cc, cc], 0)
    sin64 = np.concatenate([sr, sr, sc, sc], 0)
    d["ropeA"] = np.stack([np.tile(cos64, (2, 1)), np.tile(sin64, (2, 1))], 0).astype(f32)
    d["ropeK"] = _rope32_tables(np.arange(r * T, (r + 1) * T))
    kk = np.arange(128)[:, None]
    qq = np.arange(128)[None, :]
    prev = np.where(kk >= qq, 0.0, NEG).astype(f32)
    nxt = np.where(kk <= qq, 0.0, NEG).astype(f32)
    allneg = np.full((128, 128), NEG, f32)
    var = [prev, nxt, allneg if r == 0 else prev, allneg if r == 3 else nxt]
    d["amask"] = np.stack([np.tile(v, (1, 4)) for v in var], 1).astype(_BF)
    rpb = np.asarray(inp["b_rpb"][0], f32)
    bb = np.full((5, 128, 6, 8, 128), NEG, f32)
    k_i = np.arange(128)
    q_i = np.arange(128)
    for vi, jt in enumerate([0, 1, 5, 14, 15]):
        if jt == 0:
            ms = list(range(-2, 4))
        elif jt == 15:
            ms = list(range(12, 18))
        else:
            ms = list(range(jt - 2, jt + 3))
        rr = r if vi != 2 else 1
        gq = 32 * rr + 2 * jt + q_i // 64
        cq = q_i % 64
        start = np.clip(gq - 4, 0, 120)
        c0 = np.clip(cq - 8, 0, 48)
        for ci, m in enumerate(ms):
            gk = 32 * rr + 2 * m + k_i // 64
            ck = k_i % 64
            valid = ((gk[:, None] >= 0) & (gk[:, None] < 128) & (gk[:, None] >= start[None, :]) & (gk[:, None] < start[None, :] + 8)
                     & (ck[:, None] >= c0[None, :]) & (ck[:, None] < c0[None, :] + 16))
            dri = np.clip(gk[:, None] - gq[None, :] + 7, 0, 14)
            dci = np.clip(ck[:, None] - cq[None, :], -15, 15) + 15
            g = rpb[:, dri, dci]
            bb[vi, :, ci, :, :] = np.where(valid[None], g, NEG).transpose(1, 0, 2)
    bb = bb[:, :, :, [0, 2, 1, 3, 4, 6, 5, 7], :]
    d["bbias"] = np.ascontiguousarray(bb).reshape(5, 128, 6, 1024).astype(_BF)
    d["sink"] = np.tile(np.asarray(inp["a_sink"][0], f32)[None, :], (128, 1))
    return d


def _host_B(inp, core):
    f32 = np.float32
    r = core % 4
    d = {}
    d["o_w_uq"] = np.ascontiguousarray(inp["o_w_uq"][0], f32)
    d["o_w_ukv"] = np.ascontiguousarray(inp["o_w_ukv"][0], f32)
    d["o_w_out"] = np.ascontiguousarray(inp["o_w_out"][0], f32)
    d["ropeQ"] = _rope32_tables(np.arange(r * T, (r + 1) * T)).astype(_BF)
    return d


_NC_CACHE = {}


def _get_nc(mode):
    if mode not in _NC_CACHE:
        _NC_CACHE[mode] = build(mode).nc
    return _NC_CACHE[mode]


FUSED = False


def kernel(**inputs):
    inp = {k: np.asarray(v) for k, v in inputs.items()}
    common = _host_common(inp)
    out = np.empty((2, 8192, 1024), np.float32)
    if FUSED:
        maps = []
        for c in range(NCORES):
            m = dict(common)
            m.update(_host_A(inp, c))
            m.update(_host_B(inp, c))
            maps.append(m)
        res = run_bass_kernel_spmd(_get_nc("F"), maps, core_ids=list(range(NCORES)))
        for c in range(NCORES):
            out[c // 4, (c % 4) * T:(c % 4 + 1) * T, :] = np.asarray(res.results[c]["outT"]).T
        return out
    mapsA = []
    for c in range(NCORES):
        m = dict(common)
        m.update(_host_A(inp, c))
        mapsA.append(m)
    resA = run_bass_kernel_spmd(_get_nc("A"), mapsA, core_ids=list(range(NCORES)))
    ra = resA.results
    mapsB = []
    for c in range(NCORES):
        b = c // 4
        m = dict(common)
        m.update(_host_B(inp, c))
        m["x1T"] = np.asarray(ra[c]["x1T"])
        m["cqn"] = np.asarray(ra[c]["cqn"])
        m["mod1"] = np.asarray(ra[c]["mod1"])
        kv = np.concatenate([np.asarray(ra[4 * b + rr]["xchg"])[:, 0:T] for rr in range(4)] + [np.asarray(ra[c]["xchg"])[:, T:T + CT]], axis=1)
        m["kvall"] = np.ascontiguousarray(kv)
        mapsB.append(m)
    resB = run_bass_kernel_spmd(_get_nc("B"), mapsB, core_ids=list(range(NCORES)))
    for c in range(NCORES):
        out[c // 4, (c % 4) * T:(c % 4 + 1) * T, :] = np.asarray(resB.results[c]["outT"]).T
    return out
```

```python
import contextlib
import numpy as np
import ml_dtypes
import concourse.bass as bass
import concourse.mybir as mybir
from concourse.bass_utils import run_bass_kernel_spmd

F32 = mybir.dt.float32
BF16 = mybir.dt.bfloat16
AF = mybir.ActivationFunctionType
ALU = mybir.AluOpType

NCORES = 8
T = 2048
HAL = 256
CT = 256
E = HAL + T + HAL + CT
NQ = T + CT
NKEY = 8192 + CT
EPS = 1e-6
NEG = -30000.0


def _region(ap):
    name = ap.name
    space = str(ap.space)
    dims = ap.ap
    off = int(ap.offset)
    if space == "DRAM":
        lo = off
        hi = off + sum(int(s) * (int(c) - 1) for s, c in dims if int(s) > 0) + 1
        return (name, "DRAM", 0, 1, lo, hi)
    if space == "PSUM":
        fszp = 1
        for d in ap.tensor.shape[1:]:
            fszp *= int(d)
        g0 = off % fszp
        g1 = g0 + sum(int(s) * (int(c) - 1) for s, c in dims[1:] if int(s) > 0) + 1
        return (name, "PSUM", 0, 128, g0 // 512, (g1 - 1) // 512 + 1)
    pstep, pcnt = int(dims[0][0]), int(dims[0][1])
    fsz = 1
    for d in ap.tensor.shape[1:]:
        fsz *= int(d)
    p0 = off // fsz
    f0 = off % fsz
    p1 = p0 + 1 if pstep == 0 else p0 + (pstep // fsz) * (pcnt - 1) + 1
    f1 = f0 + sum(int(s) * (int(c) - 1) for s, c in dims[1:] if int(s) > 0) + 1
    return (name, "SB", p0, p1, f0, f1)


def _overlap(a, b):
    return a[2] < b[3] and b[2] < a[3] and a[4] < b[5] and b[4] < a[5]


def _covers(a, b):
    return a[2] <= b[2] and a[3] >= b[3] and a[4] <= b[4] and a[5] >= b[5]


class Sched:
    ENGS = ("pe", "act", "dve", "pool", "sp")

    def __init__(self, nc, n_dma_sems=12):
        self.nc = nc
        self.ops = []
        self.track = {}
        self.n_dma_sems = n_dma_sems

    def add(self, eng, fn, reads=(), writes=(), dma=False):
        idx = len(self.ops)
        rr = list(dict.fromkeys(_region(a) for a in reads))
        ww = list(dict.fromkeys(_region(a) for a in writes))
        deps = set()
        for r in rr:
            lst = self.track.setdefault(r[0], [])
            psum = r[1] == "PSUM"
            for (box, oi, kind) in lst:
                if (kind == "w" or psum) and _overlap(box, r):
                    deps.add(oi)
        for w in ww:
            lst = self.track.setdefault(w[0], [])
            for (box, oi, kind) in lst:
                if _overlap(box, w):
                    deps.add(oi)
        for r in rr:
            lst = self.track[r[0]]
            if r[1] == "PSUM":
                lst[:] = [t for t in lst if not _covers(r, t[0])]
                lst.append((r, idx, "w"))
            else:
                lst[:] = [t for t in lst if not (t[2] == "r" and t[1] < idx and self.ops[t[1]]["eng"] == eng
                                                 and not self.ops[t[1]]["dma"] and not dma and _covers(r, t[0]))]
                lst.append((r, idx, "r"))
        for w in ww:
            lst = self.track[w[0]]
            lst[:] = [t for t in lst if not _covers(w, t[0])]
            lst.append((w, idx, "w"))
        deps.discard(idx)
        self.ops.append(dict(eng=eng, fn=fn, deps=deps, dma=dma, sig=False, rr=rr, ww=ww))
        return idx

    def dma(self, out, in_, eng="sp"):
        return self.add(eng, lambda e: e.dma_start(out=out, in_=in_), [in_], [out], dma=True)

    def emit(self, final_wait_ops=()):
        nc = self.nc
        ops = self.ops

        def needs_wait(x, y):
            X, Y = ops[x], ops[y]
            if Y["dma"] or X["dma"]:
                return True
            if X["eng"] == Y["eng"]:
                if X["eng"] == "pe":
                    return False
                for w in Y["ww"]:
                    for r in X["rr"]:
                        if w[0] == r[0] and _overlap(w, r):
                            return True
                return False
            return True

        for i, X in enumerate(ops):
            X["wdeps"] = [y for y in X["deps"] if needs_wait(i, y)]
            for y in X["wdeps"]:
                ops[y]["sig"] = True
        for i in final_wait_ops:
            ops[i]["sig"] = True
        cnt = {e: 0 for e in self.ENGS}
        dma_k = {e: 0 for e in self.ENGS}
        dma_semcnt = {}
        for X in ops:
            if X["dma"]:
                q = X["eng"]
                k = dma_k[q]
                dma_k[q] += 1
                s = (q, k % self.n_dma_sems)
                dma_semcnt[s] = dma_semcnt.get(s, 0) + 1
                X["dsem"] = s
                X["dval"] = 16 * dma_semcnt[s]
            elif X["sig"]:
                cnt[X["eng"]] += 1
                X["cnt"] = cnt[X["eng"]]
        with contextlib.ExitStack() as st:
            sems = {e: st.enter_context(nc.semaphore("s_" + e)) for e in ("pe", "act", "dve", "pool")}
            dsems = {}
            for q in self.ENGS:
                for j in range(min(self.n_dma_sems, dma_k[q])):
                    dsems[(q, j)] = st.enter_context(nc.semaphore("d_%s_%d" % (q, j)))
            block = st.enter_context(nc.Block())
            per_eng = {e: [i for i, X in enumerate(ops) if X["eng"] == e] for e in self.ENGS}

            def run_stream(ename, e):
                known = {}

                def wait(key, semh, val):
                    if known.get(key, 0) >= val:
                        return
                    e.wait_ge(semh, val)
                    known[key] = val

                def wait_op(Y):
                    if Y["dma"]:
                        wait(Y["dsem"], dsems[Y["dsem"]], Y["dval"])
                    else:
                        wait(Y["eng"], sems[Y["eng"]], Y["cnt"])

                for i in per_eng[ename]:
                    X = ops[i]
                    for y in sorted(X["wdeps"]):
                        wait_op(ops[y])
                    if X["dma"]:
                        if X["dval"] > 16:
                            wait(X["dsem"], dsems[X["dsem"]], X["dval"] - 16)
                        X["fn"](e).then_inc(dsems[X["dsem"]], 16)
                    else:
                        ins = X["fn"](e)
                        if X["sig"]:
                            ins.then_inc(sems[ename], 1)
                if ename == "sp":
                    for i in final_wait_ops:
                        wait_op(ops[i])

            @block.tensor
            def _(e):
                run_stream("pe", e)

            @block.scalar
            def _(e):
                run_stream("act", e)

            @block.vector
            def _(e):
                run_stream("dve", e)

            @block.gpsimd
            def _(e):
                run_stream("pool", e)

            @block.sync
            def _(e):
                run_stream("sp", e)
        self.stats = dict(n_ops=len(ops), cnt=cnt, dma=dma_k)


class KB:
    def __init__(self, mode):
        self.mode = mode
        self.nc = bass.Bass("TRN2", target_bir_lowering=False)
        self.S = Sched(self.nc)
        self.st = contextlib.ExitStack()
        self.final = []
        self._rot = {}

    def din(self, name, shape, dt=F32):
        return self.nc.dram_tensor(name, list(shape), dt, kind="ExternalInput").ap()

    def dout(self, name, shape, dt=F32):
        return self.nc.dram_tensor(name, list(shape), dt, kind="ExternalOutput").ap()

    def dint(self, name, shape, dt=F32):
        return self.nc.dram_tensor(name, list(shape), dt, kind="Internal").ap()

    def sb(self, name, shape, dt):
        return self.st.enter_context(self.nc.sbuf_tensor(name, list(shape), dt))

    def rot(self, key, lst):
        i = self._rot.get(key, 0)
        self._rot[key] = i + 1
        return lst[i % len(lst)]

    def mm(self, out, lhsT, rhs, start=True, stop=True, skip=False):
        kw = dict(skip_group_check=True) if skip else {}
        return self.S.add("pe", lambda e: e.matmul(out, lhsT=lhsT, rhs=rhs, start=start, stop=stop, **kw),
                          [lhsT, rhs], [out])

    def actv(self, out, in_, func, scale=1.0, bias=None):
        reads = [in_]
        kw = {}
        if isinstance(scale, float) or isinstance(scale, int):
            kw["scale"] = float(scale)
        else:
            kw["scale"] = scale
            reads.append(scale)
        if bias is not None:
            kw["bias"] = bias
            if not isinstance(bias, float):
                reads.append(bias)
        return self.S.add("act", lambda e: e.activation(out=out, in_=in_, func=func, **kw), reads, [out])

    def tt(self, eng, out, in0, in1, op):
        return self.S.add(eng, lambda e: e.tensor_tensor(out=out, in0=in0, in1=in1, op=op), [in0, in1], [out])

    def stt(self, eng, out, in0, scalar, in1, op0, op1):
        reads = [in0, in1]
        if not isinstance(scalar, float):
            reads.append(scalar)
        return self.S.add(eng, lambda e: e.scalar_tensor_tensor(out=out, in0=in0, scalar=scalar, in1=in1, op0=op0, op1=op1),
                          reads, [out])

    def ts(self, eng, out, in0, s1, op0, s2=None, op1=None):
        reads = [in0]
        if not isinstance(s1, float):
            reads.append(s1)
        if s2 is not None and not isinstance(s2, float):
            reads.append(s2)
        if op1 is None:
            return self.S.add(eng, lambda e: e.tensor_scalar(out=out, in0=in0, scalar1=s1, scalar2=None, op0=op0), reads, [out])
        return self.S.add(eng, lambda e: e.tensor_scalar(out=out, in0=in0, scalar1=s1, scalar2=s2, op0=op0, op1=op1), reads, [out])

    def copy(self, eng, out, in_):
        if eng == "act":
            return self.S.add("act", lambda e: e.activation(out=out, in_=in_, func=AF.Copy), [in_], [out])
        return self.S.add(eng, lambda e: e.tensor_copy(out=out, in_=in_), [in_], [out])

    def recip(self, out, in_):
        return self.S.add("dve", lambda e: e.reciprocal(out=out, in_=in_), [in_], [out])

    def memset(self, eng, out, val):
        return self.S.add(eng, lambda e: e.memset(out, val), [], [out])

    def dma(self, out, in_, eng="sp"):
        return self.S.dma(out, in_, eng)


def v3(ap2, a):
    return ap2.rearrange("p (a b) -> p a b", a=a)


MULT, ADD = ALU.mult, ALU.add
NA = 36480


def build(mode, stop=0):
    K = KB(mode)
    nc, S = K.nc, K.S
    A_ = mode in ("A", "F")
    B_ = mode in ("B", "F")

    vecs_d = K.din("vecs", [128, 48])
    cmat_d = K.din("cmat", [4, 128, 128], BF16)
    w1_d = K.din("mlp_w1", [2, 1024, 4096])
    w2_d = K.din("mlp_w2", [2, 4096, 1024])
    if A_:
        xT_d = K.din("xT", [1024, T])
        xh_d = K.din("xhT", [1024, 2 * HAL])
        ctx_d = K.din("ctxT", [1024, CT])
        cond_d = K.din("condT", [128, 16])
        adaw_d = K.din("ada_w", [2, 1024, 6144])
        adab_d = K.din("adab", [128, 96])
        ewin_d = K.din("e_w_in", [1024, 2304])
        ewout_d = K.din("e_w_out", [1024, 1024])
        owin_d = K.din("o_w_in", [1024, 672])
        ropeA_d = K.din("ropeA", [2, 128, 2 * HAL + T])
        ropeK_d = K.din("ropeK", [2, 32, T])
        amask_d = K.din("amask", [128, 4, 512], BF16)
        bbias_d = K.din("bbias", [5, 128, 6, 1024], BF16)
        sink_d = K.din("sink", [128, 8])
    if B_:
        wuq_d = K.din("o_w_uq", [384, 1536])
        wukv_d = K.din("o_w_ukv", [256, 2048])
        owout_d = K.din("o_w_out", [1024, 1024])
        ropeQ_d = K.din("ropeQ", [2, 32, T], BF16)
        out_o = K.dout("outT", [1024, T])
    if mode == "A":
        x1_o = K.dout("x1T", [1024, T])
        cqn_o = K.dout("cqn", [384, T], BF16)
        xchg_o = K.dout("xchg", [288, T + CT], BF16)
        mod1_o = K.dout("mod1", [128, 96])
    if mode == "B":
        x1_d = K.din("x1T", [1024, T])
        cqn_d = K.din("cqn", [384, T], BF16)
        kvall_d = K.din("kvall", [288, NKEY], BF16)
        mod1_d = K.din("mod1", [128, 96])
    if mode == "F":
        xchg_o = K.dint("xchg_i", [288, T + CT], BF16)
        kvg_i = K.dint("kvg_i", [4 * 288, T + CT], BF16)

    if stop:
        dbg_o = K.dout("dbg", [128, NA], BF16)
        dbg36_o = K.dout("dbg36", [128, 8 * NQ], BF16)
        dbgx_o = K.dout("dbgx", [128, 8 * NQ])

    def finish():
        if stop:
            K.final.append(K.dma(dbg_o, AR[:, :]))
            K.final.append(K.dma(dbg36_o, A36[:, :]))
            K.final.append(K.dma(dbgx_o, xT[:, :, :].rearrange("p a b -> p (a b)")))
        S.emit(final_wait_ops=K.final)
        return K

    xT = K.sb("xTs", [128, 8, NQ], F32)
    A36 = K.sb("A36", [128, 8 * NQ], BF16)
    A36v = v3(A36[:, :], 8)
    mod = K.sb("mod", [128, 2, 48, 2], F32)
    Amat = K.sb("Amat", [128, 2, 2, 2, 8], F32)
    vecs = K.sb("vecs_s", [128, 48], F32)
    gsc = K.sb("gsc", [128, 4], F32)
    ones128 = K.sb("ones128", [128, 128], BF16)
    blk = K.sb("blk", [128, 128], BF16)
    blk96 = K.sb("blk96", [128, 128], BF16)
    cmat = K.sb("cmat_s", [128, 4, 128], BF16)
    ident = cmat[:, 0, :]
    sqb = [K.sb("sqb%d" % i, [128, 512], BF16) for i in range(2)]
    ftmp = [K.sb("ftmp%d" % i, [128, 512], F32) for i in range(3)]
    rsb = [K.sb("rsb%d" % i, [128, 512], F32) for i in range(2)]
    AR = K.sb("AR", [128, NA], BF16)
    ARF = K.sb("ARF", [128, 3072], F32)
    PD = [K.st.enter_context(nc.psum_tensor("PD%d" % i, [128, 2, 512], F32)) for i in range(4)]
    PS = [PD[i // 2][:, i % 2, :] for i in range(8)]

    def pipeline(n, issue, consume, look=1):
        for i in range(min(look, n)):
            issue(i)
        for i in range(n):
            if i + look < n:
                issue(i + look)
            consume(i)

    def arv(off, dims, p0=0, p1=128, t=AR):
        n = 1
        for d in dims:
            n *= d
        ap = t[p0:p1, off:off + n]
        if len(dims) == 2:
            ap = ap.rearrange("p (a b) -> p a b", a=dims[0])
        elif len(dims) == 3:
            ap = ap.rearrange("p (a b c) -> p a b c", a=dims[0], b=dims[1])
        return ap

    K.dma(vecs[:], vecs_d)
    K.dma(cmat[:], cmat_d.rearrange("c p q -> p c q"))
    epsv = K.sb("epsv", [128, 1], F32)
    K.memset("dve", epsv[:], EPS)
    K.memset("dve", ones128[:], 1.0)
    K.memset("dve", blk[:], 0.0)
    K.memset("dve", blk[0:64, 0:64], 1.0)
    K.memset("dve", blk[64:128, 64:128], 1.0)
    K.memset("dve", blk96[:], 0.0)
    K.memset("dve", blk96[0:64, 0:64], 1.0)
    K.memset("dve", blk96[64:96, 64:96], 1.0)
    K.ts("dve", gsc[:, 0:1], vecs[:, 32:33], 0.125, MULT)
    K.ts("dve", gsc[:, 1:2], vecs[:, 34:35], 0.125, MULT)
    K.ts("dve", gsc[:, 2:3], vecs[:, 41:42], float(96 ** -0.5), MULT)

    def rstd_to(ss_ap, npart, n, scale):
        rs = K.rot("rsb", rsb)
        K.actv(rs[0:npart, :n], ss_ap, AF.Ln, scale=scale, bias=epsv[0:npart, 0:1])
        K.actv(rs[0:npart, :n], rs[0:npart, :n], AF.Exp, scale=-0.5)
        return rs

    def mod_ap(l, kind, m, j):
        return mod[:, l, kind * 8 + m, j:j + 1]

    def norm_block(src, nt, l, which, j, dst):
        ss = K.rot("ssb", [PS[6], PS[7]])
        for k in range(8):
            sq = K.rot("sqb", sqb)
            K.tt("pool", sq[:, :nt], src[:, k, :], src[:, k, :], MULT)
            K.mm(ss[:, :nt], ones128[:], sq[:, :nt], start=(k == 0), stop=(k == 7))
        rs = rstd_to(ss[:, :nt], 128, nt, 1.0 / 1024.0)
        shift_kind = 0 if which == 0 else 3
        for k in range(8):
            t = K.rot("ftmp", ftmp)
            K.stt("dve", t[:, :nt], src[:, k, :], Amat[:, l, which, j, k:k + 1], rs[:, :nt], MULT, MULT)
            K.actv(dst[:, k, :], t[:, :nt], AF.Identity, scale=1.0, bias=mod_ap(l, shift_kind, k, j))

    def mlp(l, nblocks, h2T, hook=None):
        W1v = w1_d[l].rearrange("(k p) f -> p k f", p=128)
        W2v = w2_d[l].rearrange("(k p) f -> p k f", p=128)
        w1b = [arv(0, [8, 512]), arv(4096, [8, 512])]
        w2b = [arv(8192, [4, 1024]), arv(12288, [4, 1024])]
        ub = [arv(16384, [4, 512]), arv(18432, [4, 512])]
        rb = [arv(20480 + i * 512, [512]) for i in range(2)]
        def load_e8(e8):
            K.dma(w1b[e8 % 2], W1v[:, :, e8 * 512:(e8 + 1) * 512], eng="pool")
            K.dma(w2b[e8 % 2], W2v[:, e8 * 4:(e8 + 1) * 4, :], eng="pool")
        load_e8(0)
        for e8 in range(8):
            w1 = w1b[e8 % 2]
            w2 = w2b[e8 % 2]
            if e8 + 1 < 8:
                load_e8(e8 + 1)
            for bi_, (t0, nt, j) in enumerate(nblocks):
                if hook is not None and bi_ in (0, 2):
                    hook()
                u = K.rot("ub", ub)
                for fc in range(4):
                    acc = K.rot("mlpacc", [PS[0], PS[1], PS[2]])
                    for k in range(8):
                        K.mm(acc[:, :nt], w1[:, k, fc * 128:(fc + 1) * 128], h2T[:, k, t0:t0 + nt], start=(k == 0), stop=(k == 7))
                    r = K.rot("rb", rb)
                    K.actv(r[:, :nt], acc[:, :nt], AF.Relu)
                    K.tt("pool", u[:, fc, :nt], r[:, :nt], r[:, :nt], MULT)
                for m in range(8):
                    acc = K.rot("mlpacc2", [PS[3], PS[4], PS[5]])
                    for fc in range(4):
                        K.mm(acc[:, :nt], w2[:, fc, m * 128:(m + 1) * 128], u[:, fc, :nt], start=(fc == 0), stop=(fc == 3))
                    K.stt("dve", xT[:, m, t0:t0 + nt], acc[:, :nt], mod_ap(l, 5, m, j), xT[:, m, t0:t0 + nt], MULT, ADD)

    OWN_BLOCKS = [(b * 512, 512, 0) for b in range(4)]
    CTX_BLOCK = (T, CT, 1)

    if A_:
        xTv = xT_d.rearrange("(k p) t -> p k t", p=128)
        for k in range(8):
            K.dma(xT[:, k, 0:T], xTv[:, k, :])
        K.dma(xT[:, :, T:NQ], ctx_d.rearrange("(k p) t -> p k t", p=128))
        cond = K.sb("cond", [128, 16], F32)
        silu = K.sb("silu", [128, 16], BF16)
        adab = K.sb("adab_s", [128, 96], F32)
        sinkx = K.sb("sinkx", [128, 8], F32)
        K.dma(cond[:], cond_d)
        K.dma(adab[:], adab_d)
        K.dma(sinkx[:], sink_d)
        K.actv(silu[:], cond[:], AF.Silu)
        K.actv(sinkx[:], sinkx[:], AF.Exp)
        siluv = v3(silu[:, :], 8)

        def ada_dma(l, pc, wb):
            Wv = adaw_d[l].rearrange("(k p) f -> p k f", p=128)
            K.dma(wb, Wv[:, :, pc * 512:(pc + 1) * 512], eng="pool")

        def ada_piece(l, pc, wb):
            acc = K.rot("adacc", [PS[6], PS[7]])
            for m in range(4):
                for k in range(8):
                    K.mm(acc[:, 2 * m:2 * m + 2], wb[:, k, m * 128:(m + 1) * 128], siluv[:, k, :], start=(k == 0), stop=(k == 7), skip=True)
            f0 = pc * 4
            K.tt("dve", mod[:, l, f0:f0 + 4, :], v3(acc[:, 0:8], 4), adab[:, l * 48 + f0:l * 48 + f0 + 4].unsqueeze(2).broadcast_to([128, 4, 2]), ADD)
            if pc in (3, 9):
                which = 0 if pc == 3 else 1
                sck = 1 if which == 0 else 4
                nv = vecs[:, (l * 2 + which) * 8:(l * 2 + which) * 8 + 8]
                for j in range(2):
                    K.stt("dve", Amat[:, l, which, j, :], mod[:, l, sck * 8:sck * 8 + 8, j], 1.0, nv, ADD, MULT)

        adw0 = [arv(0, [8, 512]), arv(4096, [8, 512])]
        ada_dma(0, 0, adw0[0])
        for pc in range(4):
            if pc + 1 < 4:
                ada_dma(0, pc + 1, adw0[(pc + 1) % 2])
            ada_piece(0, pc, adw0[pc % 2])
        adwA = [v3(A36[:, 4 * NQ + i * 4096:4 * NQ + (i + 1) * 4096], 8) for i in range(2)]
        ada_later = dict(l0=list(range(4, 12)), l1=list(range(12)))

        def ada_l0_step():
            if not ada_later["l0"]:
                return
            pc = ada_later["l0"].pop(0)
            if pc == 4:
                ada_dma(0, 4, adwA[0])
            if ada_later["l0"]:
                ada_dma(0, pc + 1, adwA[(pc + 1) % 2])
            ada_piece(0, pc, adwA[pc % 2])

        adwM = [arv(22016 + i * 4096, [8, 512]) for i in range(2)]

        def ada_l1_step():
            if not ada_later["l1"]:
                return
            pc = ada_later["l1"].pop(0)
            if pc == 0:
                ada_dma(1, 0, adwM[0])
            if ada_later["l1"]:
                ada_dma(1, pc + 1, adwM[(pc + 1) % 2])
            ada_piece(1, pc, adwM[pc % 2])
            if not ada_later["l1"] and mode == "A":
                mo = K.sb("mo", [128, 96], F32)
                K.copy("dve", v3(mo[:, :], 48), mod[:, 1, :, :])
                K.final.append(K.dma(mod1_o, mo[:]))

        if stop == 1:
            return finish()

        QT = arv(0, [4, NQ])
        KT = arv(9216, [2, E])
        VV = arv(14848, [22, 4, 128])
        wp = arv(26112, [8, 768])
        hblk = arv(32256, [8, 512])
        PT2 = [arv(26112 + i * 1024, [1024]) for i in range(2)]
        biasb = [arv(28160 + i * 3072, [6, 512]) for i in range(2)]
        amask = arv(34304, [4, 512])
        ropetab = arv(0, [2, 512], t=ARF)
        xhs = arv(1024, [8, 256], t=ARF)
        ewv = ewin_d.rearrange("(k p) c -> p k c", p=128)
        xhv = xh_d.rearrange("(k p) t -> p k t", p=128)
        ropeAv = ropeA_d.rearrange("c p t -> p c t")

        def post_chunk(acc, nt, dst, gain, rope, tabc0):
            sq = K.rot("sqb", sqb)
            raw = K.rot("ftmp", ftmp)
            K.actv(sq[:, :nt], acc[:, :nt], AF.Square)
            K.copy("dve", raw[:, :nt], acc[:, :nt])
            ss = PS[3]
            K.mm(ss[:, :nt], blk[:], sq[:, :nt])
            rs = rstd_to(ss[:, :nt], 128, nt, 1.0 / 64.0)
            if not rope:
                K.stt("dve", dst, raw[:, :nt], gain, rs[:, :nt], MULT, MULT)
                return
            qn = K.rot("sqb", sqb)
            K.stt("dve", qn[:, :nt], raw[:, :nt], gain, rs[:, :nt], MULT, MULT)
            sw = PS[4]
            K.mm(sw[:, :nt], cmat[:, 1, :], qn[:, :nt])
            t1 = K.rot("ftmp", ftmp)
            t2 = K.rot("ftmp", ftmp)
            K.tt("pool", t1[:, :nt], qn[:, :nt], ropetab[:, 0, :nt], MULT)
            K.tt("dve", t2[:, :nt], sw[:, :nt], ropetab[:, 1, :nt], MULT)
            K.tt("pool", dst, t1[:, :nt], t2[:, :nt], ADD)

        def l0_pass(pid, do_attn=True):
            isA = pid == 0
            half = pid - 1
            nQc = 4 if isA else 2
            nKc = 1 if isA else 2
            nV = 2 if isA else 4
            if isA:
                for s in range(2):
                    for jj in range(4):
                        K.dma(wp[:, :, jj * 128 + s * 64: jj * 128 + s * 64 + 64], ewv[:, :, s * 256 + jj * 64: s * 256 + jj * 64 + 64], eng="pool")
                K.dma(wp[:, :, 512:640], ewv[:, :, 512:640], eng="pool")
                K.dma(wp[:, :, 640:768], ewv[:, :, 640:768], eng="pool")
                qcol0, kcol0, vcol0 = 0, 512, 640
                gq, gk = gsc[:, 0:1], vecs[:, 33:34]
            else:
                K.dma(wp[:, :, 0:256], ewv[:, :, 768 + 256 * half: 768 + 256 * half + 256], eng="pool")
                K.dma(wp[:, :, 256:512], ewv[:, :, 1280 + 256 * half: 1280 + 256 * half + 256], eng="pool")
                K.dma(wp[:, :, 512:768], ewv[:, :, 1792 + 256 * half: 1792 + 256 * half + 256], eng="pool")
                qcol0, kcol0, vcol0 = 0, 256, 512
                gq, gk = gsc[:, 1:2], vecs[:, 35:36]
            K.memset("pool", VV[:, :, 0:nV, 64:128], 1.0)
            blocks = []
            blocks.append(("hb", 256, 0, None, 0, 0))
            for b in range(4):
                blocks.append((b, 512, HAL + b * 512, b * 512, 0, HAL + b * 512))
            blocks.append(("ha", 256, HAL + T, None, 0, HAL + T))
            blocks.append(("ctx", 256, 2 * HAL + T, T, 1, None))
            for (bid, nt, e0, q0, j, rc0) in blocks:
                if bid == "hb":
                    K.dma(xhs, xhv[:, :, 0:256])
                    src = xhs
                elif bid == "ha":
                    K.dma(xhs, xhv[:, :, 256:512])
                    src = xhs
                elif bid == "ctx":
                    src = xT[:, :, T:NQ]
                else:
                    src = xT[:, :, bid * 512:(bid + 1) * 512]
                rope = isA and (rc0 is not None)
                if rope:
                    K.dma(ropetab[:, :, :nt], ropeAv[:, :, rc0:rc0 + nt])
                norm_block(src, nt, 0, 0, j, hblk[:, :, :nt])
                if q0 is not None:
                    for qc in range(nQc):
                        acc = K.rot("pacc", [PS[0], PS[1], PS[2]])
                        for k in range(8):
                            K.mm(acc[:, :nt], wp[:, k, qcol0 + qc * 128: qcol0 + (qc + 1) * 128], hblk[:, k, :nt], start=(k == 0), stop=(k == 7))
                        post_chunk(acc, nt, QT[:, qc, q0:q0 + nt], gq, rope, rc0)
                for kc in range(nKc):
                    acc = K.rot("pacc", [PS[0], PS[1], PS[2]])
                    for k in range(8):
                        K.mm(acc[:, :nt], wp[:, k, kcol0 + kc * 128: kcol0 + (kc + 1) * 128], hblk[:, k, :nt], start=(k == 0), stop=(k == 7))
                    post_chunk(acc, nt, KT[:, kc, e0:e0 + nt], gk, rope, rc0)
                for tt_ in range(nt // 128):
                    acc = PS[5]
                    for k in range(8):
                        K.mm(acc[:, 0:nV * 64], hblk[:, k, tt_ * 128:(tt_ + 1) * 128], wp[:, k, vcol0:vcol0 + nV * 64], start=(k == 0), stop=(k == 7))
                    ec = e0 // 128 + tt_
                    K.copy("act", VV[:, ec, 0:nV, 0:64], v3(acc[:, 0:nV * 64], nV))
                if isA:
                    ada_l0_step()

            while isA and ada_later["l0"]:
                ada_l0_step()
            if not do_attn:
                return
            if isA:
                K.dma(amask, amask_d)

            def finalize(O, heads_hc, sink_cols):
                rec = K.rot("rsb", rsb)
                if sink_cols is not None:
                    for hh in range(4):
                        K.ts("dve", rec[64:128, hh * 128:(hh + 1) * 128], O[64:128, hh * 128:(hh + 1) * 128], sinkx[64:128, sink_cols[hh]:sink_cols[hh] + 1], ADD)
                    K.actv(rec[64:128, :], rec[64:128, :], AF.Ln)
                else:
                    K.actv(rec[64:128, :], O[64:128, :], AF.Ln)
                K.actv(rec[64:128, :], rec[64:128, :], AF.Exp, scale=-1.0)
                return rec

            def attn_tile_A(q0, chunks):
                sts = [chunks[i:i + 2] for i in range(0, len(chunks), 2)]
                for g in range(2):
                    O = K.rot("Ob", [PS[4], PS[5]])
                    cur = {}

                    def issue(i, g=g, cur=cur):
                        S2 = K.rot("S2", [PD[0], PD[1]])
                        cur[i] = S2
                        for ii, (ec, mv) in enumerate(sts[i]):
                            if mv is not None:
                                K.mm(S2[:, ii, :], ident, amask[:, mv, :], start=True, stop=False, skip=True)
                            K.mm(S2[:, ii, :], KT[64 * g:64 * g + 64, 0, ec * 128:(ec + 1) * 128], QT[64 * g:64 * g + 64, 0:4, q0:q0 + 128],
                                 start=(mv is None), stop=True, skip=True)

                    def consume(i, g=g, cur=cur, O=O):
                        S2 = cur[i]
                        n = len(sts[i])
                        pt = K.rot("PT2", PT2)
                        K.actv(v3(pt, 2)[:, 0:n, :], S2[:, 0:n, :], AF.Exp)
                        for ii, (ec, mv) in enumerate(sts[i]):
                            first = (i == 0 and ii == 0)
                            last = (i == len(sts) - 1 and ii == n - 1)
                            K.mm(O, VV[:, ec, g, :], pt[:, ii * 512:(ii + 1) * 512], start=first, stop=last)

                    pipeline(len(sts), issue, consume)
                    rec = finalize(O, None, [4 * g + hh for hh in range(4)])
                    for hh in range(4):
                        hc = 4 * g + hh
                        dst = A36v[64 * (hc % 2):64 * (hc % 2) + 64, hc // 2, q0:q0 + 128]
                        K.tt("dve", dst, O[0:64, hh * 128:(hh + 1) * 128], rec[64:128, hh * 128:(hh + 1) * 128], MULT)

            def attn_tile_B(q0, chunks, bias):
                O = K.rot("Ob", [PS[4], PS[5]])
                K.memset("dve", O, 0.0)
                cur = {}

                def issue(i):
                    (ec, bi) = chunks[i]
                    S2 = K.rot("S2", [PD[0], PD[1]])
                    cur[i] = S2
                    for s_ in range(2):
                        if bi is not None:
                            K.mm(S2[:, s_, 0:256], ident, bias[:, bi, s_ * 256:(s_ + 1) * 256], start=True, stop=False, skip=True)
                        for cc in range(2):
                            K.mm(S2[:, s_, cc * 128:(cc + 1) * 128], KT[64 * s_:64 * s_ + 64, cc, ec * 128:(ec + 1) * 128],
                                 QT[64 * s_:64 * s_ + 64, cc, q0:q0 + 128], start=(bi is None), stop=True, skip=True)

                def consume(i):
                    (ec, bi) = chunks[i]
                    S2 = cur[i]
                    pt = K.rot("PT2", PT2)
                    K.actv(v3(pt[:, 0:512], 2), S2[:, :, 0:256], AF.Exp)
                    for hh in range(4):
                        pos = (hh % 2) * 2 + hh // 2
                        K.mm(O[:, hh * 128:(hh + 1) * 128], VV[:, ec, hh, :], pt[:, pos * 128:(pos + 1) * 128], start=False, stop=False, skip=True)

                pipeline(len(chunks), issue, consume)
                rec = finalize(O, None, None)
                for hh in range(4):
                    hc = 8 + 4 * half + hh
                    dst = A36v[64 * (hc % 2):64 * (hc % 2) + 64, hc // 2, q0:q0 + 128]
                    K.tt("dve", dst, O[0:64, hh * 128:(hh + 1) * 128], rec[64:128, hh * 128:(hh + 1) * 128], MULT)

            CTXC = [(20, None), (21, None)]
            for jt in range(16):
                q0 = jt * 128
                if isA:
                    chunks = [(2 + jt - 1, 2 if jt == 0 else 0), (2 + jt, None), (2 + jt + 1, 3 if jt == 15 else 1)] + CTXC
                    attn_tile_A(q0, chunks)
                else:
                    if jt == 0:
                        ms, var = list(range(-2, 4)), 0
                    elif jt == 15:
                        ms, var = list(range(12, 18)), 4
                    else:
                        ms = list(range(jt - 2, jt + 3))
                        var = 1 if jt == 1 else (3 if jt == 14 else 2)
                    bias = K.rot("biasb", biasb)
                    K.dma(bias, bbias_d[var][:, :, 512 * half:512 * half + 512])
                    chunks = [(m + 2, ci) for ci, m in enumerate(ms)] + CTXC
                    attn_tile_B(q0, chunks, bias)
            for ct in range(2):
                q0 = T + ct * 128
                if isA:
                    attn_tile_A(q0, CTXC)
                else:
                    attn_tile_B(q0, CTXC, None)

        if stop == 2:
            l0_pass(0, False)
            return finish()
        if stop == 3:
            l0_pass(0)
            return finish()
        if stop == 4:
            l0_pass(1, False)
            return finish()
        if stop == 5:
            l0_pass(1)
            return finish()
        for pid in range(3):
            l0_pass(pid)
        if stop == 6:
            return finish()

        wout = arv(0, [8, 1024])
        K.dma(wout, ewout_d.rearrange("(k p) f -> p k f", p=128), eng="pool")
        for (t0, nt, j) in OWN_BLOCKS + [CTX_BLOCK]:
            for m in range(8):
                acc = K.rot("oacc", [PS[5], PS[6], PS[7]])
                for k in range(8):
                    K.mm(acc[:, :nt], wout[:, k, m * 128:(m + 1) * 128], A36v[:, k, t0:t0 + nt], start=(k == 0), stop=(k == 7))
                K.stt("dve", xT[:, m, t0:t0 + nt], acc[:, :nt], mod_ap(0, 2, m, j), xT[:, m, t0:t0 + nt], MULT, ADD)
        if stop == 7:
            return finish()
        for (t0, nt, j) in OWN_BLOCKS + [CTX_BLOCK]:
            norm_block(xT[:, :, t0:t0 + nt], nt, 0, 1, j, A36v[:, :, t0:t0 + nt])
        mlp(0, OWN_BLOCKS + [CTX_BLOCK], A36v, hook=ada_l1_step)
        while ada_later["l1"]:
            ada_l1_step()

        if stop == 8:
            return finish()
        win1 = arv(0, [8, 672])
        hb1 = arv(5376, [8, 512])
        CQN = arv(9472, [3, T])
        CKVN = arv(15616, [2, NQ])
        KR = arv(20224, [NQ], p0=0, p1=32)
        K.dma(win1, owin_d.rearrange("(k p) c -> p k c", p=128), eng="pool")
        rawv = arv(1024, [3, 512], t=ARF)
        rk = arv(0, [2, 512], p0=0, p1=32, t=ARF)
        ropeKv = ropeK_d.rearrange("c p t -> p c t")
        for (t0, nt, j) in OWN_BLOCKS + [CTX_BLOCK]:
            norm_block(xT[:, :, t0:t0 + nt], nt, 1, 0, j, hb1[:, :, :nt])
            groups = [(384, 2, 256.0, 39, CKVN)]
            if j == 0:
                groups = [(0, 3, 384.0, 36, CQN)] + groups
            for (c0, ncn, dn, gcol, dstT) in groups:
                ss = PS[3]
                for c in range(ncn):
                    acc = K.rot("pacc", [PS[0], PS[1], PS[2]])
                    for k in range(8):
                        K.mm(acc[:, :nt], win1[:, k, c0 + c * 128:c0 + (c + 1) * 128], hb1[:, k, :nt], start=(k == 0), stop=(k == 7))
                    sq = K.rot("sqb", sqb)
                    K.actv(sq[:, :nt], acc[:, :nt], AF.Square)
                    K.copy("dve", rawv[:, c, :nt], acc[:, :nt])
                    K.mm(ss[:, :nt], ones128[:], sq[:, :nt], start=(c == 0), stop=(c == ncn - 1))
                rs = rstd_to(ss[:, :nt], 128, nt, 1.0 / dn)
                for c in range(ncn):
                    K.stt("dve", dstT[:, c, t0:t0 + nt], rawv[:, c, :nt], vecs[:, gcol + c:gcol + c + 1], rs[:, :nt], MULT, MULT)
            acc = K.rot("pacc", [PS[0], PS[1], PS[2]])
            for k in range(8):
                K.mm(acc[0:32, :nt], win1[:, k, 640:672], hb1[:, k, :nt], start=(k == 0), stop=(k == 7))
            sq = K.rot("sqb", sqb)
            raw = K.rot("ftmp", ftmp)
            K.actv(sq[0:32, :nt], acc[0:32, :nt], AF.Square)
            K.copy("dve", raw[0:32, :nt], acc[0:32, :nt])
            ss = PS[3]
            K.mm(ss[0:32, :nt], ones128[0:32, 0:32], sq[0:32, :nt])
            rs = rstd_to(ss[0:32, :nt], 32, nt, 1.0 / 32.0)
            if j == 1:
                K.stt("dve", KR[:, t0:t0 + nt], raw[0:32, :nt], vecs[0:32, 43:44], rs[0:32, :nt], MULT, MULT)
            else:
                K.dma(rk[:, :, :nt], ropeKv[:, :, t0:t0 + nt])
                qn = K.rot("sqb", sqb)
                K.stt("dve", qn[0:32, :nt], raw[0:32, :nt], vecs[0:32, 43:44], rs[0:32, :nt], MULT, MULT)
                sw = PS[4]
                K.mm(sw[0:32, :nt], cmat[0:32, 2, 0:32], qn[0:32, :nt])
                t1 = K.rot("ftmp", ftmp)
                t2 = K.rot("ftmp", ftmp)
                K.tt("pool", t1[0:32, :nt], qn[0:32, :nt], rk[:, 0, :nt], MULT)
                K.tt("dve", t2[0:32, :nt], sw[0:32, :nt], rk[:, 1, :nt], MULT)
                K.tt("pool", KR[:, t0:t0 + nt], t1[0:32, :nt], t2[0:32, :nt], ADD)
        d1 = K.dma(xchg_o[0:256, :].rearrange("(c p) t -> p c t", p=128), CKVN)
        d2 = K.dma(xchg_o[256:288, :], KR)
        if mode == "A":
            K.final += [d1, d2]
            K.final.append(K.dma(cqn_o.rearrange("(c p) t -> p c t", p=128), CQN))
            x1v = x1_o.rearrange("(k p) t -> p k t", p=128)
            for k in range(8):
                K.final.append(K.dma(x1v[:, k, :], xT[:, k, 0:T]))

    if B_:
        CQN = arv(9472, [3, T])
        CKVALL = v3(A36[:, 0:2 * NKEY], 2)
        VVh = arv(0, [66, 128])
        KTh = arv(15616, [NKEY], p0=0, p1=96)
        QTh = arv(24064, [T], p0=0, p1=96)
        ropeQ = arv(26112, [2, T], p0=64, p1=96)
        PT2 = [arv(30208 + i * 1024, [1024]) for i in range(2)]
        wuqh = [arv(32256 + i * 288, [3, 96]) for i in range(2)]
        wukvh = [arv(32832 + i * 256, [2, 128]) for i in range(2)]
        woh = [arv(33344 + i * 1024, [1024], p0=0, p1=64) for i in range(2)]
        OTh = [arv(35392 + i * 512, [512], p0=0, p1=64) for i in range(2)]
        if mode == "B":
            x1v = x1_d.rearrange("(k p) t -> p k t", p=128)
            for k in range(8):
                K.dma(xT[:, k, 0:T], x1v[:, k, :])
            K.dma(CQN, cqn_d.rearrange("(c p) t -> p c t", p=128))
            mo = K.sb("mo", [128, 96], F32)
            K.dma(mo[:], mod1_d)
            K.copy("dve", mod[:, 1, :, :], v3(mo[:, :], 48))
            for which in range(2):
                sck = 1 if which == 0 else 4
                nv = vecs[:, (2 + which) * 8:(2 + which) * 8 + 8]
                K.stt("dve", Amat[:, 1, which, 0, :], mod[:, 1, sck * 8:sck * 8 + 8, 0], 1.0, nv, ADD, MULT)
            for c in range(2):
                for kq in range(4):
                    K.dma(CKVALL[:, c, kq * 2112:(kq + 1) * 2112], kvall_d[c * 128:(c + 1) * 128, kq * 2112:(kq + 1) * 2112])
            K.dma(KTh[64:96, :], kvall_d[256:288, :])
        else:
            gat = K.S.add("pool", lambda e: e.collective_compute("AllGather", ALU.bypass, replica_groups=[[0, 1, 2, 3], [4, 5, 6, 7]],
                                                                 ins=[xchg_o[:, :]], outs=[kvg_i[:, :]]),
                          [xchg_o[:, :]], [kvg_i[:, :]], dma=True)
            for rr in range(4):
                for c in range(2):
                    K.dma(CKVALL[:, c, rr * T:(rr + 1) * T], kvg_i[rr * 288 + c * 128: rr * 288 + (c + 1) * 128, 0:T])
                K.dma(KTh[64:96, rr * T:(rr + 1) * T], kvg_i[rr * 288 + 256: rr * 288 + 288, 0:T])
            for c in range(2):
                K.dma(CKVALL[:, c, 4 * T:NKEY], xchg_o[c * 128:(c + 1) * 128, T:T + CT])
            K.dma(KTh[64:96, 4 * T:NKEY], xchg_o[256:288, T:T + CT])
        K.dma(ropeQ, ropeQ_d.rearrange("c p t -> p c t"))
        K.memset("pool", VVh[:, :, 64:128], 1.0)
        wuqv = wuq_d.rearrange("(c p) f -> p c f", p=128)
        wukvv = wukv_d.rearrange("(c p) f -> p c f", p=128)
        MISC = [PS[6], PS[7]]
        KBLK = [(kb * 512, 512) for kb in range(16)] + [(8192, 256)]
        def load_head(h):
            K.dma(wuqh[h % 2], wuqv[:, :, h * 96:(h + 1) * 96], eng="pool")
            K.dma(wukvh[h % 2], wukvv[:, :, h * 128:(h + 1) * 128], eng="pool")
            K.dma(woh[h % 2], owout_d[h * 64:(h + 1) * 64, :], eng="pool")
        def tasks_K(h, kbi):
            wkv = wukvh[h % 2]
            (k0, nk) = KBLK[kbi]
            st_ = {}

            def k1():
                acc = K.rot("misc", MISC)
                for c in range(2):
                    K.mm(acc[0:64, :nk], wkv[:, c, 0:64], CKVALL[:, c, k0:k0 + nk], start=(c == 0), stop=(c == 1))
                yield
                raw = ftmp[st_["slot"]]
                K.copy("dve", raw[0:64, :nk], acc[0:64, :nk])
                sq = sqb[st_["slot"]]
                K.tt("pool", sq[0:64, :nk], raw[0:64, :nk], raw[0:64, :nk], MULT)
                st_["raw"], st_["sq"] = raw, sq

            def k2():
                ss = K.rot("misc", MISC)
                K.mm(ss[0:64, :nk], ones128[0:64, 0:64], st_["sq"][0:64, :nk])
                yield
                rs = rstd_to(ss[0:64, :nk], 64, nk, 1.0 / 64.0)
                K.stt("dve", KTh[0:64, k0:k0 + nk], st_["raw"][0:64, :nk], vecs[0:64, 42:43], rs[0:64, :nk], MULT, MULT)

            return ([k1, k2], [2], st_)

        def tasks_V(h, c4):
            wkv = wukvh[h % 2]
            n4 = min(4, 66 - c4)

            def v1():
                acc = K.rot("misc", MISC)
                for i in range(n4):
                    for c in range(2):
                        K.mm(acc[:, i * 64:(i + 1) * 64], CKVALL[:, c, (c4 + i) * 128:(c4 + i + 1) * 128], wkv[:, c, 64:128],
                             start=(c == 0), stop=(c == 1), skip=True)
                yield
                K.copy("dve", VVh[:, c4:c4 + n4, 0:64], v3(acc[:, 0:n4 * 64], n4))

            return ([v1], [], None)

        def tasks_Q(h, qb):
            wq = wuqh[h % 2]
            st_ = {}
            qd = QTh[0:96, qb * 512:(qb + 1) * 512]
            qr = QTh[64:96, qb * 512:(qb + 1) * 512]

            def q1():
                acc = K.rot("misc", MISC)
                for c in range(3):
                    K.mm(acc[0:96, :], wq[:, c, :], CQN[:, c, qb * 512:(qb + 1) * 512], start=(c == 0), stop=(c == 2))
                yield
                raw = ftmp[st_["slot"]]
                K.copy("dve", raw[0:96, :], acc[0:96, :])
                sq = sqb[st_["slot"]]
                K.tt("pool", sq[0:96, :], raw[0:96, :], raw[0:96, :], MULT)
                st_["raw"], st_["sq"] = raw, sq

            def q2():
                ss = K.rot("misc", MISC)
                K.mm(ss[0:96, :], blk96[0:96, 0:96], st_["sq"][0:96, :])
                yield
                rs = rstd_to(ss[0:96, :], 96, 512, vecs[0:96, 44:45])
                K.stt("dve", qd, st_["raw"][0:96, :], gsc[0:96, 2:3], rs[0:96, :], MULT, MULT)

            def q4():
                sw = K.rot("misc", MISC)
                K.mm(sw[0:32, :], cmat[64:96, 3, 0:32], qr)
                yield
                t1 = ftmp[2]
                t2 = K.rot("rsb", rsb)
                K.tt("pool", t1[64:96, :], qr, ropeQ[:, 0, qb * 512:(qb + 1) * 512], MULT)
                K.tt("dve", t2[64:96, :], sw[0:32, :], ropeQ[:, 1, qb * 512:(qb + 1) * 512], MULT)
                K.tt("pool", qr, t1[64:96, :], t2[64:96, :], ADD)

            return ([q1, q2, q4], [2, 2], st_)

        def tasks_fin(O, qb, h, wo):
            st_ = {}

            def f1():
                rec = K.rot("rsb", rsb)
                K.recip(rec[64:128, :], O[64:128, :])
                ot = K.rot("OTh", OTh)
                K.tt("dve", ot, O[0:64, :], rec[64:128, :], MULT)
                st_["ot"] = ot

            def fm(m):
                def f():
                    Y = K.rot("misc", MISC)
                    K.mm(Y, wo[:, m * 128:(m + 1) * 128], st_["ot"])
                    yield
                    K.stt("dve", xT[:, m, qb * 512:(qb + 1) * 512], Y, mod_ap(1, 2, m, 0), xT[:, m, qb * 512:(qb + 1) * 512], MULT, ADD)
                return f

            return ([f1] + [fm(m) for m in range(8)], [2] + [1] * 7, None)

        tq = []
        tstat = dict(enq=0, done=0, step=0)

        slots_free = [0, 1]

        def enq(task):
            stages, gaps, ctx = task
            tstat["enq"] += 1
            tq.append([stages, gaps, 0, tstat["step"], tstat["enq"], ctx])

        def tasks_begin(n):
            started = []
            for t_ in list(tq):
                if len(started) >= n:
                    break
                if t_[3] > tstat["step"]:
                    continue
                ctx = t_[5]
                if ctx is not None and t_[2] == 0:
                    if not slots_free:
                        continue
                    ctx["slot"] = slots_free.pop(0)
                g = t_[0][t_[2]]()
                if g is not None:
                    next(g, None)
                started.append((t_, g))
            return started

        def tasks_end(started):
            for (t_, g) in started:
                if g is not None:
                    for _ in g:
                        pass
                ctx = t_[5]
                if ctx is not None and t_[2] == 1:
                    slots_free.append(ctx["slot"])
                if t_[2] == len(t_[0]) - 1:
                    tq.remove(t_)
                else:
                    t_[3] = tstat["step"] + t_[1][t_[2]]
                    t_[2] += 1
            tstat["step"] += 1

        def run_tasks(n):
            tasks_end(tasks_begin(n))

        def drain_until(mk):
            while any(t_[4] <= mk for t_ in tq):
                run_tasks(4)

        def run_now(task):
            if task[2] is not None:
                task[2]["slot"] = 0
            for f in task[0]:
                g = f()
                if g is not None:
                    for _ in g:
                        pass

        def load_head_qkv(h):
            K.dma(wuqh[h % 2], wuqv[:, :, h * 96:(h + 1) * 96], eng="pool")
            K.dma(wukvh[h % 2], wukvv[:, :, h * 128:(h + 1) * 128], eng="pool")

        def load_head_wo(h):
            K.dma(woh[h % 2], owout_d[h * 64:(h + 1) * 64, :], eng="pool")

        load_head_qkv(0)
        load_head_wo(0)
        for kbi in range(17):
            run_now(tasks_K(0, kbi))
            run_now(tasks_V(0, 4 * kbi))
        for qb in range(4):
            run_now(tasks_Q(0, qb))
        fin3_mark = 0
        for h in range(16):
            wo = woh[h % 2]
            nxt = h + 1 < 16
            for qb in range(4):
                if nxt and qb == 0:
                    load_head_qkv(h + 1)
                if nxt and qb == 1:
                    drain_until(fin3_mark)
                    load_head_wo(h + 1)
                O = K.rot("Ob", [PS[4], PS[5]])
                cur = {}

                def issue(i, qb=qb, cur=cur):
                    S2 = K.rot("S2", [PD[0], PD[1]])
                    cur[i] = S2
                    for ii in range(2):
                        c = 2 * i + ii
                        K.mm(S2[:, ii, :], KTh[0:96, c * 128:(c + 1) * 128], QTh[0:96, qb * 512:(qb + 1) * 512])

                def consume(i, qb=qb, cur=cur, O=O, h=h, nxt=nxt):
                    started = tasks_begin(2)
                    S2 = cur[i]
                    pt = K.rot("PT2", PT2)
                    K.actv(v3(pt, 2), S2[:, :, :], AF.Exp)
                    for ii in range(2):
                        c = 2 * i + ii
                        K.mm(O, VVh[:, c, :], pt[:, ii * 512:(ii + 1) * 512], start=(c == 0), stop=(c == 65))
                    if nxt and qb == 3 and (i % 2 == 1 or i == 32):
                        j = i // 2
                        enq(tasks_K(h + 1, j))
                        enq(tasks_V(h + 1, 4 * j))
                    tasks_end(started)

                pipeline(33, issue, consume)
                enq(tasks_fin(O, qb, h, wo))
                if nxt:
                    enq(tasks_Q(h + 1, qb))
                if qb == 3:
                    fin3_mark = tstat["enq"]
        while tq:
            run_tasks(4)
        for (t0, nt, j) in OWN_BLOCKS:
            norm_block(xT[:, :, t0:t0 + nt], nt, 1, 1, 0, A36v[:, :, t0:t0 + nt])
        mlp(1, OWN_BLOCKS, A36v)
        ov = out_o.rearrange("(k p) t -> p k t", p=128)
        for k in range(8):
            K.final.append(K.dma(ov[:, k, :], xT[:, k, 0:T]))

    return finish()


_BF = ml_dtypes.bfloat16


def _fm(v):
    return np.ascontiguousarray(np.asarray(v, np.float32).reshape(-1, 128).T)


def _rope_tab(pos, hw):
    inv = (np.float32(10000.0) ** (-np.arange(hw, dtype=np.float32) / np.float32(hw))).astype(np.float32)
    ang = pos.astype(np.float32)[None, :] * inv[:, None]
    return np.cos(ang).astype(np.float32), np.sin(ang).astype(np.float32)


def _perm_signed(blocks, n):
    P = np.zeros((n, n), np.float32)
    for (b, hw) in blocks:
        for i in range(hw):
            P[b + hw + i, b + i] = -1.0
            P[b + i, b + hw + i] = 1.0
    return P


def _host_common(inp):
    f32 = np.float32
    vecs = np.zeros((128, 48), f32)
    vecs[:, 0:8] = _fm(inp["norm_mix"][0])
    vecs[:, 8:16] = _fm(inp["norm_mlp"][0])
    vecs[:, 16:24] = _fm(inp["norm_mix"][1])
    vecs[:, 24:32] = _fm(inp["norm_mlp"][1])
    rep64 = lambda v: np.tile(np.asarray(v, f32).reshape(64), 2)
    vecs[:, 32] = rep64(inp["a_q_norm"][0])
    vecs[:, 33] = rep64(inp["a_k_norm"][0])
    vecs[:, 34] = rep64(inp["b_q_norm"][0])
    vecs[:, 35] = rep64(inp["b_k_norm"][0])
    vecs[:, 36:39] = _fm(inp["o_qa_norm"][0])
    vecs[:, 39:41] = _fm(inp["o_kva_norm"][0])
    vecs[0:64, 41] = inp["o_qn_nope"][0]
    vecs[64:96, 41] = inp["o_qn_rope"][0]
    vecs[:, 42] = rep64(inp["o_kn_nope"][0])
    vecs[:, 43] = np.tile(np.asarray(inp["o_kn_rope"][0], f32), 4)
    vecs[0:64, 44] = 1.0 / 64.0
    vecs[64:128, 44] = 1.0 / 32.0
    cmat = np.zeros((4, 128, 128), f32)
    cmat[0] = np.eye(128, dtype=f32)
    cmat[1] = _perm_signed([(0, 16), (32, 16), (64, 16), (96, 16)], 128)
    p32 = _perm_signed([(0, 8), (16, 8)], 32)
    cmat[2, 0:32, 0:32] = p32
    cmat[3, 64:96, 0:32] = p32
    return dict(vecs=vecs, cmat=cmat.astype(_BF),
                mlp_w1=np.ascontiguousarray(inp["mlp_w1"], f32), mlp_w2=np.ascontiguousarray(inp["mlp_w2"], f32))


def _rope32_tables(tok):
    row, col = tok // 64, tok % 64
    cr, sr = _rope_tab(row, 8)
    cc, sc = _rope_tab(col, 8)
    cos = np.concatenate([cr, cr, cc, cc], 0)
    sin = np.concatenate([sr, sr, sc, sc], 0)
    return np.stack([cos, sin], 0).astype(np.float32)


def _host_A(inp, core):
    f32 = np.float32
    b, r = core // 4, core % 4
    x = inp["x"][b]
    d = {}
    d["xT"] = np.ascontiguousarray(x[r * T:(r + 1) * T].T, f32)
    xh = np.zeros((1024, 2 * HAL), f32)
    if r > 0:
        xh[:, 0:HAL] = x[r * T - HAL:r * T].T
    if r < 3:
        xh[:, HAL:] = x[(r + 1) * T:(r + 1) * T + HAL].T
    d["xhT"] = xh
    d["ctxT"] = np.ascontiguousarray(inp["ctx"][b].T, f32)
    cond = np.zeros((128, 8, 2), f32)
    cond[:, :, 0] = _fm(inp["c"][b])
    cond[:, :, 1] = _fm(inp["c_ctx"])
    d["condT"] = cond.reshape(128, 16)
    d["ada_w"] = np.ascontiguousarray(inp["ada_w"], f32)
    d["adab"] = np.concatenate([_fm(inp["ada_b"][0]), _fm(inp["ada_b"][1])], 1)
    d["e_w_in"] = np.ascontiguousarray(inp["e_w_in"][0], f32)
    d["e_w_out"] = np.ascontiguousarray(inp["e_w_out"][0], f32)
    d["o_w_in"] = np.ascontiguousarray(inp["o_w_in"][0], f32)
    tok = np.arange(r * T - HAL, (r + 1) * T + HAL)
    tokc = np.clip(tok, 0, 8191)
    cr, sr = _rope_tab(tokc // 64, 16)
    cc, sc = _rope_tab(tokc % 64, 16)
    cos64 = np.concatenate([cr, cr, cc, cc], 0)
    sin64 = np.concatenate([sr, sr, sc, sc], 0)
    d["ropeA"] = np.stack([np.tile(cos64, (2, 1)), np.tile(sin64, (2, 1))], 0).astype(f32)
    d["ropeK"] = _rope32_tables(np.arange(r * T, (r + 1) * T))
    kk = np.arange(128)[:, None]
    qq = np.arange(128)[None, :]
    prev = np.where(kk >= qq, 0.0, NEG).astype(f32)
    nxt = np.where(kk <= qq, 0.0, NEG).astype(f32)
    allneg = np.full((128, 128), NEG, f32)
    var = [prev, nxt, allneg if r == 0 else prev, allneg if r == 3 else nxt]
    d["amask"] = np.stack([np.tile(v, (1, 4)) for v in var], 1).astype(_BF)
    rpb = np.asarray(inp["b_rpb"][0], f32)
    bb = np.full((5, 128, 6, 8, 128), NEG, f32)
    k_i = np.arange(128)
    q_i = np.arange(128)
    for vi, jt in enumerate([0, 1, 5, 14, 15]):
        if jt == 0:
            ms = list(range(-2, 4))
        elif jt == 15:
            ms = list(range(12, 18))
        else:
            ms = list(range(jt - 2, jt + 3))
        rr = r if vi != 2 else 1
        gq = 32 * rr + 2 * jt + q_i // 64
        cq = q_i % 64
        start = np.clip(gq - 4, 0, 120)
        c0 = np.clip(cq - 8, 0, 48)
        for ci, m in enumerate(ms):
            gk = 32 * rr + 2 * m + k_i // 64
            ck = k_i % 64
            valid = ((gk[:, None] >= 0) & (gk[:, None] < 128) & (gk[:, None] >= start[None, :]) & (gk[:, None] < start[None, :] + 8)
                     & (ck[:, None] >= c0[None, :]) & (ck[:, None] < c0[None, :] + 16))
            dri = np.clip(gk[:, None] - gq[None, :] + 7, 0, 14)
            dci = np.clip(ck[:, None] - cq[None, :], -15, 15) + 15
            g = rpb[:, dri, dci]
            bb[vi, :, ci, :, :] = np.where(valid[None], g, NEG).transpose(1, 0, 2)
    bb = bb[:, :, :, [0, 2, 1, 3, 4, 6, 5, 7], :]
    d["bbias"] = np.ascontiguousarray(bb).reshape(5, 128, 6, 1024).astype(_BF)
    d["sink"] = np.tile(np.asarray(inp["a_sink"][0], f32)[None, :], (128, 1))
    return d


def _host_B(inp, core):
    f32 = np.float32
    r = core % 4
    d = {}
    d["o_w_uq"] = np.ascontiguousarray(inp["o_w_uq"][0], f32)
    d["o_w_ukv"] = np.ascontiguousarray(inp["o_w_ukv"][0], f32)
    d["o_w_out"] = np.ascontiguousarray(inp["o_w_out"][0], f32)
    d["ropeQ"] = _rope32_tables(np.arange(r * T, (r + 1) * T)).astype(_BF)
    return d


_NC_CACHE = {}


def _get_nc(mode):
    if mode not in _NC_CACHE:
        _NC_CACHE[mode] = build(mode).nc
    return _NC_CACHE[mode]


FUSED = False


def kernel(**inputs):
    inp = {k: np.asarray(v) for k, v in inputs.items()}
    common = _host_common(inp)
    out = np.empty((2, 8192, 1024), np.float32)
    if FUSED:
        maps = []
        for c in range(NCORES):
            m = dict(common)
            m.update(_host_A(inp, c))
            m.update(_host_B(inp, c))
            maps.append(m)
        res = run_bass_kernel_spmd(_get_nc("F"), maps, core_ids=list(range(NCORES)))
        for c in range(NCORES):
            out[c // 4, (c % 4) * T:(c % 4 + 1) * T, :] = np.asarray(res.results[c]["outT"]).T
        return out
    mapsA = []
    for c in range(NCORES):
        m = dict(common)
        m.update(_host_A(inp, c))
        mapsA.append(m)
    resA = run_bass_kernel_spmd(_get_nc("A"), mapsA, core_ids=list(range(NCORES)))
    ra = resA.results
    mapsB = []
    for c in range(NCORES):
        b = c // 4
        m = dict(common)
        m.update(_host_B(inp, c))
        m["x1T"] = np.asarray(ra[c]["x1T"])
        m["cqn"] = np.asarray(ra[c]["cqn"])
        m["mod1"] = np.asarray(ra[c]["mod1"])
        kv = np.concatenate([np.asarray(ra[4 * b + rr]["xchg"])[:, 0:T] for rr in range(4)] + [np.asarray(ra[c]["xchg"])[:, T:T + CT]], axis=1)
        m["kvall"] = np.ascontiguousarray(kv)
        mapsB.append(m)
    resB = run_bass_kernel_spmd(_get_nc("B"), mapsB, core_ids=list(range(NCORES)))
    for c in range(NCORES):
        out[c // 4, (c % 4) * T:(c % 4 + 1) * T, :] = np.asarray(resB.results[c]["outT"]).T
    return out
```
